# Optimizing a Trainium2 kernel written in Bass

```python
import math
import jax, jax.numpy as jnp
from jax import lax
import numpy as np

D_MODEL = 1024
BATCH = 32
SEQ = 256
DEPTH = 4
DEC_BATCH = 2
DEC_SEQ = 1024
PAST_LEN = 256

GRID_W = 64
N_MIXERS = 3
N_A_LAYERS = len(range(0, DEPTH, N_MIXERS))
N_B_LAYERS = len(range(1, DEPTH, N_MIXERS))
N_C_LAYERS = len(range(2, DEPTH, N_MIXERS))
N_MOD = 9
EPS = 1e-6
D_FF = 2816
D_RNN = D_MODEL
N_LRU_BLOCKS = 16
LRU_BLOCK = D_RNN // N_LRU_BLOCKS
CONV_W = 4
CONV_LEFT = 2
LRU_C = 8.0
N_DIFF_HEADS = 8
DIFF_HD = D_MODEL // N_DIFF_HEADS // 2
ROPE_THETA = 10000.0
Q_BLOCK = 128
POOL_WINDOWS = (2, 4, 8, 16)
N_POOL_GROUPS = len(POOL_WINDOWS)
POOL_GROUP = D_MODEL // N_POOL_GROUPS

kernel_name = "hybrid_diffusion_macaron_rglru_diffattn_pool_step"

F32 = jnp.float32


def rmsnorm(x, g):
    xf = x.astype(F32)
    y = xf * lax.rsqrt(jnp.mean(xf * xf, axis=-1, keepdims=True) + EPS)
    return (y * g.astype(F32)).astype(x.dtype)


def adaln(cond, w_mod, b_mod):
    m = jnp.einsum('nd,de->ne', jax.nn.silu(cond), w_mod) + b_mod
    return m.reshape(cond.shape[0], N_MOD, D_MODEL)


def pre(x, g, mod, k):
    return rmsnorm(x, g) * (1 + mod[:, 3 * k + 1][:, None]) + mod[:, 3 * k][:, None]


def gate_of(mod, k):
    return mod[:, 3 * k + 2][:, None]


def swiglu(h, w_in, w_out):
    a, b = jnp.split(h @ w_in, 2, axis=-1)
    return (jax.nn.silu(a) * b) @ w_out


def centred_conv(x, w, b):
    T = x.shape[1]
    xp = jnp.pad(x, ((0, 0), (CONV_LEFT, CONV_W - 1 - CONV_LEFT), (0, 0)))
    out = xp[:, 0:T] * w[0]
    for k in range(1, CONV_W):
        out = out + xp[:, k:k + T] * w[k]
    return out + b


def block_diag(x, w, b):
    xb = x.reshape(*x.shape[:-1], N_LRU_BLOCKS, LRU_BLOCK)
    y = jnp.einsum('btnk,nkj->btnj', xb, w.astype(F32)).reshape(x.shape)
    return y + b.astype(F32)


def linear_scan(a, b, h0):
    def comb(l, r):
        return l[0] * r[0], r[0] * l[1] + r[1]
    A, Bc = lax.associative_scan(comb, (a, b), axis=1)
    return A * h0[:, None] + Bc


def rglru_mixer(h, h0, w_in, conv_w, conv_b, gw_a, gb_a, gw_x, gb_x, lam, w_out):
    gate, xr = jnp.split(h @ w_in, 2, axis=-1)
    xr = centred_conv(xr, conv_w, conv_b).astype(F32)
    h0 = h0.astype(F32)
    ys, finals = [], []
    for d in range(2):
        r = jax.nn.sigmoid(block_diag(xr, gw_a[d], gb_a[d]))
        i = jax.nn.sigmoid(block_diag(xr, gw_x[d], gb_x[d]))
        log_a = -LRU_C * r * jax.nn.softplus(-lam[d].astype(F32))
        a = jnp.exp(log_a)
        b = jnp.sqrt(-jnp.expm1(2.0 * log_a)) * (i * xr)
        if d == 1:
            a, b = jnp.flip(a, 1), jnp.flip(b, 1)
        hs = linear_scan(a, b, h0[:, d])
        finals.append(hs[:, -1])
        if d == 1:
            hs = jnp.flip(hs, 1)
        ys.append(hs)
    y = (ys[0] + ys[1]).astype(h.dtype) * jax.nn.gelu(gate)
    return y @ w_out, jnp.stack(finals, axis=1)


def axial_rope(x):
    T = x.shape[1]
    rows = T // GRID_W
    row = jnp.repeat(jnp.arange(rows), GRID_W)
    col = jnp.tile(jnp.arange(GRID_W), rows)
    pos = jnp.stack([row, col], axis=-1).astype(F32)
    nf = DIFF_HD // 4
    inv = ROPE_THETA ** (-jnp.arange(nf, dtype=F32) / nf)
    ang = pos[:, :, None] * inv
    cos = jnp.cos(ang)[None, :, None, None]
    sin = jnp.sin(ang)[None, :, None, None]
    xf = x.astype(F32).reshape(*x.shape[:-1], 2, 2, nf)
    x1, x2 = xf[..., 0, :], xf[..., 1, :]
    out = jnp.stack([x1 * cos - x2 * sin, x2 * cos + x1 * sin], axis=-2)
    return out.reshape(x.shape).astype(x.dtype)


def diff_attend(q, k, v, lam):
    B, Tq = q.shape[:2]
    nblk = Tq // Q_BLOCK
    qb = jnp.moveaxis(q.reshape(B, nblk, Q_BLOCK, *q.shape[2:]), 1, 0)
    kf, vf = k.astype(F32), v.astype(F32)
    scale = DIFF_HD ** -0.5

    def one(qblk):
        s = jnp.einsum('bqhmd,bkhmd->bhmqk', qblk.astype(F32), kf) * scale
        p = jax.nn.softmax(s, axis=-1)
        w = p[:, :, 0] - lam * p[:, :, 1]
        return jnp.einsum('bhqk,bkhd->bqhd', w, vf)

    o = lax.map(one, qb)
    return jnp.moveaxis(o, 0, 1).reshape(B, Tq, *o.shape[3:])


def diff_qkv(h, w_qkv):
    B, T, _ = h.shape
    q, k, v = jnp.split(h @ w_qkv, 3, axis=-1)
    q = q.reshape(B, T, N_DIFF_HEADS, 2, DIFF_HD)
    k = k.reshape(B, T, N_DIFF_HEADS, 2, DIFF_HD)
    v = v.reshape(B, T, N_DIFF_HEADS, 2 * DIFF_HD)
    return q, k, v


def diff_out(o, lam_init, subln_g, w_o, dtype):
    B, T = o.shape[:2]
    o = rmsnorm(o, subln_g) * (1.0 - lam_init)
    return o.reshape(B, T, D_MODEL).astype(dtype) @ w_o


def multiscale_pool(h, w_pool, scale):
    B, T, _ = h.shape
    hf = h.astype(F32)
    cs = jnp.pad(jnp.cumsum(hf, axis=1), ((0, 0), (1, 0), (0, 0)))
    t = jnp.arange(T)
    outs = []
    for g, win in enumerate(POOL_WINDOWS):
        sl = slice(g * POOL_GROUP, (g + 1) * POOL_GROUP)
        lo = jnp.clip(t - win // 2, 0, T)
        hi = jnp.clip(t + win // 2, 0, T)
        csg = cs[:, :, sl]
        cnt = (hi - lo).astype(F32)[None, :, None]
        mean = (jnp.take(csg, hi, axis=1) - jnp.take(csg, lo, axis=1)) / cnt
        outs.append(jnp.einsum('btc,cd->btd', mean - hf[:, :, sl], w_pool[g].astype(F32)))
    return (jnp.concatenate(outs, axis=-1) * scale.astype(F32)).astype(h.dtype)


def setup_inputs(seed: int = 0) -> dict:
    key = jax.random.key(seed)
    ks = iter(jax.random.split(key, 40))
    nrm = lambda shape, s=1.0: jax.random.normal(next(ks), shape, F32) * s
    D = D_MODEL
    u = jax.random.uniform(next(ks), (N_A_LAYERS, 2, D_RNN), F32, 0.9, 0.999)
    a_base = u ** (1.0 / LRU_C)
    a_lambda = jnp.log(a_base) - jnp.log1p(-a_base)
    return {
        "x_prompt": nrm((BATCH, SEQ, D)),
        "x_sample": nrm((DEC_BATCH, DEC_SEQ, D)),
        "state_rglru": nrm((DEC_BATCH, N_A_LAYERS, 2, D_RNN), 0.5),
        "cache_k_diff": nrm((DEC_BATCH, N_B_LAYERS, PAST_LEN, N_DIFF_HEADS, 2 * DIFF_HD)),
        "cache_v_diff": nrm((DEC_BATCH, N_B_LAYERS, PAST_LEN, N_DIFF_HEADS, 2 * DIFF_HD)),
        "c": nrm((DEC_BATCH, D)),
        "c_ctx": nrm((D,)),
        "norm_g": 1.0 + nrm((DEPTH, 3, D), 0.02),
        "w_mod": nrm((DEPTH, D, N_MOD * D), 0.5 * D ** -0.5),
        "b_mod": nrm((DEPTH, N_MOD * D), 0.02),
        "w_ffn_in": nrm((DEPTH, 2, D, 2 * D_FF), D ** -0.5),
        "w_ffn_out": nrm((DEPTH, 2, D_FF, D), D_FF ** -0.5),
        "a_w_in": nrm((N_A_LAYERS, D, 2 * D_RNN), D ** -0.5),
        "a_conv_w": nrm((N_A_LAYERS, CONV_W, D_RNN), CONV_W ** -0.5),
        "a_conv_b": nrm((N_A_LAYERS, D_RNN), 0.02),
        "a_gate_w_a": nrm((N_A_LAYERS, 2, N_LRU_BLOCKS, LRU_BLOCK, LRU_BLOCK), LRU_BLOCK ** -0.5),
        "a_gate_b_a": nrm((N_A_LAYERS, 2, D_RNN), 0.02),
        "a_gate_w_x": nrm((N_A_LAYERS, 2, N_LRU_BLOCKS, LRU_BLOCK, LRU_BLOCK), LRU_BLOCK ** -0.5),
        "a_gate_b_x": nrm((N_A_LAYERS, 2, D_RNN), 0.02),
        "a_lambda": a_lambda,
        "a_w_out": nrm((N_A_LAYERS, D_RNN, D), D_RNN ** -0.5),
        "b_w_qkv": nrm((N_B_LAYERS, D, 3 * D), D ** -0.5),
        "b_lam_q": nrm((N_B_LAYERS, 2, DIFF_HD), 0.1),
        "b_lam_k": nrm((N_B_LAYERS, 2, DIFF_HD), 0.1),
        "b_subln_g": 1.0 + nrm((N_B_LAYERS, 2 * DIFF_HD), 0.02),
        "b_w_o": nrm((N_B_LAYERS, D, D), D ** -0.5),
        "c_w_pool": nrm((N_C_LAYERS, N_POOL_GROUPS, POOL_GROUP, POOL_GROUP), POOL_GROUP ** -0.5),
        "c_scale": 1.0 + nrm((N_C_LAYERS, D), 0.1),
        "final_norm_g": 1.0 + nrm((D,), 0.02),
    }


def reference(x_prompt, x_sample, state_rglru, cache_k_diff, cache_v_diff, c, c_ctx,
              norm_g, w_mod, b_mod, w_ffn_in, w_ffn_out,
              a_w_in, a_conv_w, a_conv_b, a_gate_w_a, a_gate_b_a, a_gate_w_x, a_gate_b_x, a_lambda, a_w_out,
              b_w_qkv, b_lam_q, b_lam_k, b_subln_g, b_w_o,
              c_w_pool, c_scale, final_norm_g):
    xp, xs = x_prompt, x_sample
    new_states, new_k, new_v = [], [], []
    for l in range(DEPTH):
        mod_p = adaln(c_ctx[None], w_mod[l], b_mod[l])
        mod_s = adaln(c, w_mod[l], b_mod[l])
        xp = xp + 0.5 * gate_of(mod_p, 0) * swiglu(pre(xp, norm_g[l, 0], mod_p, 0), w_ffn_in[l, 0], w_ffn_out[l, 0])
        xs = xs + 0.5 * gate_of(mod_s, 0) * swiglu(pre(xs, norm_g[l, 0], mod_s, 0), w_ffn_in[l, 0], w_ffn_out[l, 0])
        hp = pre(xp, norm_g[l, 1], mod_p, 1)
        hs = pre(xs, norm_g[l, 1], mod_s, 1)
        kind, j = l % N_MIXERS, l // N_MIXERS
        if kind == 0:
            prm = (a_w_in[j], a_conv_w[j], a_conv_b[j], a_gate_w_a[j], a_gate_b_a[j],
                   a_gate_w_x[j], a_gate_b_x[j], a_lambda[j], a_w_out[j])
            h0 = jnp.zeros((hp.shape[0], 2, D_RNN), F32)
            mp, st = rglru_mixer(hp, h0, *prm)
            ms, _ = rglru_mixer(hs, state_rglru[:, j], *prm)
            new_states.append(st)
        elif kind == 1:
            lam_init = 0.8 - 0.6 * math.exp(-0.3 * l)
            lq, lk = b_lam_q[j].astype(F32), b_lam_k[j].astype(F32)
            lam = jnp.exp(jnp.sum(lq[0] * lk[0])) - jnp.exp(jnp.sum(lq[1] * lk[1])) + lam_init
            qp, kp, vp = diff_qkv(hp, b_w_qkv[j])
            mp = diff_out(diff_attend(qp, kp, vp, lam), lam_init, b_subln_g[j], b_w_o[j], hp.dtype)
            new_k.append(kp.reshape(kp.shape[0], kp.shape[1], N_DIFF_HEADS, 2 * DIFF_HD))
            new_v.append(vp)
            qs, ks_, vs = diff_qkv(hs, b_w_qkv[j])
            qs, ks_ = axial_rope(qs), axial_rope(ks_)
            kc = cache_k_diff[:, j].reshape(DEC_BATCH, -1, N_DIFF_HEADS, 2, DIFF_HD).astype(ks_.dtype)
            k_all = jnp.concatenate([kc, ks_], axis=1)
            v_all = jnp.concatenate([cache_v_diff[:, j].astype(vs.dtype), vs], axis=1)
            ms = diff_out(diff_attend(qs, k_all, v_all, lam), lam_init, b_subln_g[j], b_w_o[j], hs.dtype)
        else:
            mp = multiscale_pool(hp, c_w_pool[j], c_scale[j])
            ms = multiscale_pool(hs, c_w_pool[j], c_scale[j])
        xp = xp + gate_of(mod_p, 1) * mp
        xs = xs + gate_of(mod_s, 1) * ms
        xp = xp + 0.5 * gate_of(mod_p, 2) * swiglu(pre(xp, norm_g[l, 2], mod_p, 2), w_ffn_in[l, 1], w_ffn_out[l, 1])
        xs = xs + 0.5 * gate_of(mod_s, 2) * swiglu(pre(xs, norm_g[l, 2], mod_s, 2), w_ffn_in[l, 1], w_ffn_out[l, 1])
    y_prompt = rmsnorm(xp, final_norm_g)
    y_sample = rmsnorm(xs, final_norm_g)
    new_state_rglru = jnp.stack(new_states, axis=1).astype(x_prompt.dtype)
    new_cache_k_diff = jnp.stack(new_k, axis=1).astype(x_prompt.dtype)
    new_cache_v_diff = jnp.stack(new_v, axis=1).astype(x_prompt.dtype)
    return (y_prompt, y_sample, new_state_rglru, new_cache_k_diff, new_cache_v_diff)
```

```python
import math
from contextlib import ExitStack

import numpy as np
import concourse.bass as bass
import concourse.mybir as mybir
from concourse.bass_utils import run_bass_kernel_spmd

F32 = mybir.dt.float32
BF16 = mybir.dt.bfloat16
AF = mybir.ActivationFunctionType
ALU = mybir.AluOpType

NCORES = 8
P = 128
D = 1024
DC = 8
TOK = 1280
NSEG = 5
SEG = 256
DFF = 2816
FC = 22
DEPTH = 4
TILES = [(0, 512), (512, 512), (1024, 256)]
EPS = 1e-6
LAM_INIT = 0.8 - 0.6 * math.exp(-0.3 * 1)
NEG = -30000.0
SLOT_E = 4096
NSLOT = 5
LOOKAHEAD = 2
XRW = 259
HPW = 272
POOL_W = (2, 4, 8, 16)

DEBUG = False
SAME_ENG_ALL = True
ATT_LEVEL = 5
NDBG = 12
STAGES = 12


class _Eng:
    def __init__(self, name, obj, sem, step):
        self.name, self.obj, self.sem, self.step = name, obj, sem, step
        self.count = 0
        self.seen = {}
        self.pend_r = []
        self.pend_w = []
        self.nosame = False


class Buf:
    __slots__ = ("name", "lw", "rd", "excl")

    def __init__(self, name="", excl=False):
        self.name = name
        self.lw = None
        self.rd = {}
        self.excl = excl


def emit(eng, fn, reads=(), writes=(), signal=True, via=None):
    q = via if via is not None else eng
    need = {}

    def add(e, v, raw):
        if e is q and (q.nosame or (not raw and not SAME_ENG_ALL)):
            return
        if v > need.get(e, 0):
            need[e] = v

    for b in reads:
        if b.lw is not None:
            add(b.lw[0], b.lw[1], True)
        if b.excl:
            for e, v in b.rd.items():
                if e is not q:
                    add(e, v, False)
    for b in writes:
        if b.lw is not None:
            add(b.lw[0], b.lw[1], False)
        for e, v in b.rd.items():
            add(e, v, False)
    if via is not None and eng.count > 0:
        add(eng, eng.count, True)
    for e, v in need.items():
        if v > q.seen.get(e, 0):
            q.obj.wait_ge(e.sem, v)
            q.seen[e] = v
    ins = fn()
    eng.pend_r.extend(reads)
    eng.pend_w.extend(writes)
    if signal:
        eng.count += eng.step
        ins.then_inc(eng.sem, eng.step)
        c = eng.count
        for b in eng.pend_w:
            b.lw = (eng, c)
            b.rd = {}
        for b in eng.pend_r:
            if b.lw is not None and b.lw[0] is eng and b.lw[1] == c:
                continue
            if c > b.rd.get(eng, 0):
                b.rd[eng] = c
        eng.pend_r = []
        eng.pend_w = []
    return ins


def _fm(v):
    v = np.asarray(v, np.float32)
    lead = v.shape[:-1]
    r = v.reshape(*lead, DC, P)
    r = np.moveaxis(r, -1, 0)
    return np.ascontiguousarray(r)


def _rope_partner():
    p = np.arange(P)
    within = p % 64
    half = (within % 32) // 16
    partner = np.where(half == 0, p + 16, p - 16)
    sign = np.where(half == 0, -1.0, 1.0).astype(np.float32)
    axis = within // 32
    f = within % 16
    return partner, sign, axis, f


def _sched():
    out = []
    modq = [(0, n) for n in range(6, 18)] + [(l, n) for l in range(1, DEPTH) for n in range(18)]
    for n in range(6):
        out.append(("mod", 4096, (0, n)))

    def ffn(l, i):
        for jp in range(FC // 2):
            out.append(("fin", 4096, (l, i, jp)))
            if modq:
                out.append(("mod", 4096, modq.pop(0)))
        for dc in range(DC):
            out.append(("fout", FC * P, (l, i, dc)))
            if modq:
                out.append(("mod", 4096, modq.pop(0)))

    for l in range(DEPTH):
        ffn(l, 0)
        kind, j = l % 3, l // 3
        if kind == 0:
            for cp in range(4):
                out.append(("ain", 4096, (j, cp)))
                out.append(("agate", 1024, (j, cp)))
            for half in range(2):
                out.append(("aout", 4096, ("a", j, half)))
        elif kind == 1:
            for half in range(2):
                out.append(("wv", 4096, (j, half)))
            for hd in range(8):
                out.append(("qk", 4096, (j, hd)))
            for half in range(2):
                out.append(("aout", 4096, ("b", j, half)))
        else:
            out.append(("pool", 2048, (j,)))
        ffn(l, 1)
    assert not modq
    return out


def _piece_array(inp, tag, key):
    if tag == "mod":
        l, n = key
        w = inp["w_mod"][l]
        return w[:, n * 512:(n + 1) * 512].reshape(DC, P, 512).transpose(1, 0, 2).reshape(P, -1)
    if tag == "fin":
        l, i, jp = key
        wi = inp["w_ffn_in"][l, i].reshape(DC, P, 2, FC, P)
        return wi[:, :, :, 2 * jp:2 * jp + 2, :].transpose(1, 3, 2, 0, 4).reshape(P, -1)
    if tag == "fout":
        l, i, dc = key
        wo = inp["w_ffn_out"][l, i].reshape(FC, P, DC, P)
        return wo[:, :, dc, :].transpose(1, 0, 2).reshape(P, -1)
    if tag == "ain":
        j, cp = key
        w_in = inp["a_w_in"][j].reshape(DC, P, 2, DC, P)
        return w_in[:, :, :, 2 * cp:2 * cp + 2, :].transpose(1, 3, 2, 0, 4).reshape(P, -1)
    if tag == "agate":
        j, cp = key
        gws = [inp["a_gate_w_a"][j, 0], inp["a_gate_w_x"][j, 0], inp["a_gate_w_a"][j, 1], inp["a_gate_w_x"][j, 1]]
        g = np.zeros((P, 2, 4, P), np.float32)
        for cc in range(2):
            c = 2 * cp + cc
            for qi in range(4):
                for hb in range(2):
                    g[hb * 64:(hb + 1) * 64, cc, qi, hb * 64:(hb + 1) * 64] = gws[qi][2 * c + hb]
        return g.reshape(P, -1)
    if tag == "aout":
        which, j, half = key
        w = inp["a_w_out"][j] if which == "a" else inp["b_w_o"][j]
        wo = w.reshape(DC, P, DC, P)
        return wo[:, :, 4 * half:4 * half + 4, :].transpose(1, 2, 0, 3).reshape(P, -1)
    if tag == "wv":
        j, half = key
        wv = inp["b_w_qkv"][j][:, 2048:3072]
        return wv[:, half * 512:(half + 1) * 512].reshape(DC, P, 512).transpose(1, 0, 2).reshape(P, -1)
    if tag == "qk":
        j, hd = key
        w = inp["b_w_qkv"][j]
        partner, _, _, _ = _rope_partner()
        q = w[:, 0:1024].reshape(DC, P, 8, P)[:, :, hd, :]
        k = w[:, 1024:2048].reshape(DC, P, 8, P)[:, :, hd, :]
        blk = np.stack([q, q[:, :, partner], k, k[:, :, partner]], 0)
        return blk.transpose(2, 0, 1, 3).reshape(P, -1)
    if tag == "pool":
        (j,) = key
        wp = inp["c_w_pool"][j].reshape(4, 2, P, 2, P)
        return wp.transpose(2, 0, 1, 3, 4).reshape(P, -1)
    raise KeyError(tag)


def _weight_plan(inp, n_used):
    return [(t, _piece_array(inp, t, key)) for t, e, key in _sched()[:n_used]]


def _piece_tags():
    return [(t, e) for t, e, _ in _sched()]


def _core_tokens(inp, c):
    xp, xs = inp["x_prompt"], inp["x_sample"]
    if c < 6:
        x = xp[5 * c:5 * c + 5].reshape(TOK, D)
        prompts = [5 * c + s for s in range(5)]
        sample = None
    else:
        b = c - 6
        x = np.concatenate([xs[b], xp[30 + b]], 0)
        prompts = [None] * 4 + [30 + b]
        sample = b
    return x, prompts, sample


def _host_prep(inp, n_used):
    inp = {k: np.asarray(v) for k, v in inp.items()}
    plan = _weight_plan(inp, n_used)
    tags = _piece_tags()[:n_used]
    assert len(plan) == len(tags)
    for (t0, a), (t1, e) in zip(plan, tags):
        assert t0 == t1 and a.shape == (P, e), (t0, t1, a.shape, e)
    wstream = np.ascontiguousarray(np.concatenate([a for _, a in plan], axis=1), dtype=np.float32)

    partner, sign, axis, f = _rope_partner()
    inv = (10000.0 ** (-np.arange(16, dtype=np.float32) / 16)).astype(np.float32)
    t = np.arange(1024)
    pos = np.stack([t // 64, t % 64], 0).astype(np.float32)
    ang = pos[axis][:, :] * inv[f][:, None]
    ang = ang.astype(np.float32)
    cos_s = np.cos(ang).astype(np.float32)
    sin_s = (np.sin(ang).astype(np.float32) * sign[:, None]).astype(np.float32)

    sm = {
        "norm_g": _fm(inp["norm_g"]).reshape(P, -1),
        "b_mod": _fm(inp["b_mod"].reshape(DEPTH, 9, D)).reshape(P, -1),
        "conv_w": _fm(inp["a_conv_w"]).reshape(P, -1),
        "conv_b": _fm(inp["a_conv_b"]).reshape(P, -1),
        "gb_a": _fm(inp["a_gate_b_a"]).reshape(P, -1),
        "gb_x": _fm(inp["a_gate_b_x"]).reshape(P, -1),
        "lam_a": _fm(inp["a_lambda"]).reshape(P, -1),
        "c_scale": _fm(inp["c_scale"]).reshape(P, -1),
        "fin_g": _fm(inp["final_norm_g"]).reshape(P, -1),
    }
    sub_g = np.asarray(inp["b_subln_g"][0], np.float32).reshape(P, 1)
    lqk = np.concatenate([np.asarray(inp["b_lam_q"][0], np.float32).reshape(1, 128),
                          np.asarray(inp["b_lam_k"][0], np.float32).reshape(1, 128)], axis=1)
    lqk = np.ascontiguousarray(np.broadcast_to(lqk, (P, 256)))
    small = np.concatenate([sm["norm_g"], sm["b_mod"], sm["conv_w"], sm["conv_b"], sm["gb_a"],
                            sm["gb_x"], sm["lam_a"], sm["c_scale"], sm["fin_g"], sub_g], axis=1)
    ident = np.eye(P, dtype=np.float32)

    in_maps = []
    for c in range(NCORES):
        x, prompts, sample = _core_tokens(inp, c)
        xT = np.ascontiguousarray(x.T.reshape(DC, P, TOK).transpose(1, 0, 2))
        if sample is None:
            cond = np.stack([inp["c_ctx"], inp["c_ctx"]], 0)
            h0 = np.zeros((P, 2 * 2 * DC), np.float32)
            sf = 0.0
            kc = np.zeros((P, 8, 256), np.float32)
            vc = np.zeros((P, 2, D), np.float32)
            cos_t = np.ones((P, TOK), np.float32)
            sin_t = np.zeros((P, TOK), np.float32)
        else:
            cond = np.stack([inp["c"][sample], inp["c_ctx"]], 0)
            h0 = _fm(inp["state_rglru"][sample]).reshape(P, -1)
            sf = 1.0
            ck = inp["cache_k_diff"][sample, 0]
            kc = np.ascontiguousarray(ck.transpose(2, 1, 0))
            cv = inp["cache_v_diff"][sample, 0].reshape(256, D)
            vc = np.ascontiguousarray(cv.reshape(2, P, D).transpose(1, 0, 2))
            cos_t = np.concatenate([cos_s, np.ones((P, 256), np.float32)], 1)
            sin_t = np.concatenate([sin_s, np.zeros((P, 256), np.float32)], 1)
        condT = np.ascontiguousarray(cond.T.reshape(DC, P, 2).transpose(1, 0, 2)).reshape(P, -1)
        mb = np.full((6, 5), NEG, np.float32)
        for qs in range(5):
            for ks in range(6):
                if sample is None:
                    ok = (ks == qs)
                else:
                    ok = (qs < 4 and (ks < 4 or ks == 5)) or (qs == 4 and ks == 4)
                if ok:
                    mb[ks, qs] = 0.0
        mbias = np.broadcast_to(mb.reshape(1, 30), (P, 30)).astype(np.float32)
        ic = np.zeros((4, TOK), np.float32)
        for g, win in enumerate(POOL_W):
            for s in range(NSEG):
                if sample is not None and s < 4:
                    T, tt = 1024, s * 256 + np.arange(256)
                else:
                    T, tt = 256, np.arange(256)
                lo = np.clip(tt - win // 2, 0, T)
                hi = np.clip(tt + win // 2, 0, T)
                ic[g, s * 256:(s + 1) * 256] = 1.0 / (hi - lo).astype(np.float32)
        icnt = np.ascontiguousarray(np.broadcast_to(ic[None], (P, 4, TOK))).astype(np.float32)
        flags = np.full((P, 1), sf, np.float32)
        percore = np.concatenate([condT, h0, flags, mbias], axis=1).astype(np.float32)
        in_maps.append({
            "xT": xT, "wstream": wstream, "small": small, "ident": ident, "percore": percore, "lqk": lqk,
            "kcache": kc, "vcache": vc, "cos_t": cos_t, "sin_t": sin_t, "icnt": icnt,
        })
    return in_maps, len(tags), wstream.shape[1]


_SM = {}
_o = 0
for _n, _w in [("norm_g", 96), ("b_mod", 288), ("conv_w", 64), ("conv_b", 16), ("gb_a", 32), ("gb_x", 32),
               ("lam_a", 32), ("c_scale", 8), ("fin_g", 8), ("sub_g", 1)]:
    _SM[_n] = _o
    _o += _w
SMALL_W = _o
PC_COND, PC_H0, PC_SF, PC_MB = 0, 16, 48, 49
PERCORE_W = 79


def _build(n_pieces, wtotal):
    tags = _piece_tags()[:n_pieces]
    nc = bass.Bass("TRN2", target_bir_lowering=False)
    dt = lambda name, shape, kind="ExternalInput": nc.dram_tensor(name, list(shape), F32, kind=kind).ap()
    xT_d = dt("xT", [P, DC, TOK])
    ws_d = dt("wstream", [P, wtotal])
    small_d = dt("small", [P, SMALL_W])
    ident_d = dt("ident", [P, P])
    lqk_d = dt("lqk", [P, 256])
    pc_d = dt("percore", [P, PERCORE_W])
    kc_d = dt("kcache", [P, 8, 256])
    vc_d = dt("vcache", [P, 2, D])
    cos_d = dt("cos_t", [P, TOK])
    sin_d = dt("sin_t", [P, TOK])
    icnt_d = dt("icnt", [P, 4, TOK])
    yT_d = dt("yT", [P, DC, TOK], "ExternalOutput")
    kT_d = dt("kT", [P, DC, TOK], "ExternalOutput")
    v_d = dt("vout", [TOK, D], "ExternalOutput")
    st_d = dt("stout", [P, 2 * NSEG * 2 * DC], "ExternalOutput")
    if DEBUG:
        dbg_d = dt("dbg", [NDBG, P, DC, TOK], "ExternalOutput")

    es = ExitStack()
    es.enter_context(nc.allow_low_precision("bf16 matmul operands, fp32 accumulation"))
    _uid = [0]

    def _un(name):
        _uid[0] += 1
        return "t%d_%s" % (_uid[0], name)

    sb = lambda name, shape, dtype=F32: es.enter_context(nc.sbuf_tensor(_un(name), list(shape), dtype))

    def mk_eng(name, obj, step):
        return _Eng(name, obj, es.enter_context(nc.semaphore("s_" + name)), step)

    PE = mk_eng("pe", nc.tensor, 1)
    PE.nosame = True
    ACT = mk_eng("act", nc.scalar, 1)
    DVE = mk_eng("dve", nc.vector, 1)
    POOLQ = mk_eng("pool", nc.gpsimd, 1)
    SP = mk_eng("sp", nc.sync, 1)
    slot_eng = [mk_eng("ws%d" % i, None, 16) for i in range(NSLOT)]
    NCH = 12
    chan = [mk_eng("ch%d" % i, None, 16) for i in range(NCH)]
    chan_i = [0]
    gchan = [mk_eng("gch%d" % i, None, 16) for i in range(3)]

    def dma(out_ap, in_ap, reads=(), writes=(), q=None):
        q = q or SP
        ch = chan[chan_i[0] % NCH]
        chan_i[0] += 1
        return emit(ch, lambda: q.obj.dma_start(out=out_ap, in_=in_ap), reads, writes, via=q), ch

    def act(out, in_, func, reads, writes, bias=None, scale=1.0):
        kw = {}
        if bias is not None:
            kw["bias"] = bias
        return emit(ACT, lambda: nc.scalar.activation(out=out, in_=in_, func=func, scale=scale, **kw), reads, writes)

    def tt(out, a, b, op, reads, writes):
        return emit(DVE, lambda: nc.vector.tensor_tensor(out=out, in0=a, in1=b, op=op), reads, writes)

    def ptt(out, a, b, op, reads, writes):
        return emit(POOLQ, lambda: nc.gpsimd.tensor_tensor(out=out, in0=a, in1=b, op=op), reads, writes)

    def ts(out, a, s1, s2, op0, op1, reads, writes):
        return emit(DVE, lambda: nc.vector.tensor_scalar(out=out, in0=a, scalar1=s1, scalar2=s2, op0=op0, op1=op1),
                    reads, writes)

    def stt(out, a, s, b, op0, op1, reads, writes):
        return emit(DVE, lambda: nc.vector.scalar_tensor_tensor(out=out, in0=a, scalar=s, in1=b, op0=op0, op1=op1),
                    reads, writes)

    def mm(out, lhsT, rhs, start, stop, reads, writes, signal):
        return emit(PE, lambda: nc.tensor.matmul(out, lhsT, rhs, start=start, stop=stop), reads, writes, signal=signal)

    x_t = sb("x", [P, DC, TOK])
    xb = [[Buf("x%d_%d" % (c, i)) for i in range(3)] for c in range(DC)]
    h_t = sb("h", [P, DC, TOK], BF16)
    hb = [[Buf("h%d_%d" % (c, i)) for i in range(3)] for c in range(DC)]
    slots = [sb("wslot%d" % i, [P, SLOT_E], BF16) for i in range(NSLOT)]
    slot_b = [Buf("slot%d" % i) for i in range(NSLOT)]
    small_t = sb("small", [P, SMALL_W]); small_b = Buf("small")
    pc_t = sb("percore", [P, PERCORE_W]); pc_b = Buf("pc")
    ident_t = sb("ident", [P, P]); ident_b = Buf("ident")
    ones_bf = sb("ones_bf", [P, P], BF16)
    eps_t = sb("eps", [P, 1])
    one_t = sb("one", [P, 1])
    const_b = Buf("const")
    scond_t = sb("scond", [P, DC, 2], BF16); scond_b = Buf("scond")
    MOD = sb("mod", [P, DEPTH, 72, 2]); mod_b = [Buf("mod%d" % l) for l in range(DEPTH)]
    GS = sb("gs", [P, 3, DC, 2]); HG = sb("hg", [P, 3, DC, 2]); gs_b = Buf("gs")
    fin_t = sb("fin", [P, 2, NSEG, 2, DC]); fin_b = Buf("fin")
    rs_t = [sb("rs%d" % i, [P, 512]) for i in range(2)]; rs_b = [Buf("rs0"), Buf("rs1")]
    sq_t = [sb("sq0", [P, DC, 512], BF16)] * 2; sq_b = [Buf("sq0")] * 2
    tmp_t = [sb("tmp%d" % i, [P, 512]) for i in range(3)]; tmp_b = [Buf("tmp%d" % i) for i in range(3)]
    modT_t = sb("modT", [2, 512]); modT_b = Buf("modT")
    dmasem_b = Buf("dmasem")

    banks = [es.enter_context(nc.psum_tensor("bank%d" % i, [P, 512], F32)) for i in range(8)]
    bank_b = [Buf("bank%d" % i, excl=True) for i in range(8)]

    sm = lambda name, off=0, w=1: small_t[:, _SM[name] + off:_SM[name] + off + w]

    wst = {"issued": 0, "next": 0, "off": 0}
    offs = []
    o = 0
    for tg, e in tags:
        offs.append(o)
        o += e
    assert o == wtotal

    WS_LIMIT = [10 ** 9]

    def ws_issue_upto(k):
        while wst["issued"] <= min(k, n_pieces - 1) and wst["issued"] < WS_LIMIT[0]:
            i = wst["issued"]
            tg, e = tags[i]
            s = i % NSLOT
            emit(slot_eng[s], lambda: nc.gpsimd.dma_start(out=slots[s][:, 0:e], in_=ws_d[:, offs[i]:offs[i] + e]),
                 (), (slot_b[s],), via=POOLQ)
            wst["issued"] += 1

    def ws_next(tag):
        i = wst["next"]
        assert tags[i][0] == tag, (i, tags[i], tag)
        ws_issue_upto(i + LOOKAHEAD)
        wst["next"] += 1
        s = i % NSLOT
        return slots[s], slot_b[s]

    emit(DVE, lambda: nc.vector.memset(ones_bf[:], 1.0), (), (const_b,))
    emit(DVE, lambda: nc.vector.memset(eps_t[:], EPS), (), (const_b,))
    emit(DVE, lambda: nc.vector.memset(one_t[:], 1.0), (), (const_b,))
    emit(DVE, lambda: nc.vector.memset(fin_t[:], 0.0), (), (fin_b,))
    dma(small_t[:], small_d, (), (small_b,))
    dma(pc_t[:], pc_d, (), (pc_b,))
    dma(ident_t[:], ident_d, (), (ident_b,))
    ws_issue_upto(LOOKAHEAD)
    for c in range(DC):
        dma(x_t[:, c, :], xT_d[:, c, :], (), tuple(xb[c]))
    act(scond_t[:], pc_t[:, PC_COND:PC_COND + 16].rearrange("p (c n) -> p c n", n=2), AF.Silu, (pc_b,), (scond_b,))

    dbg_i = [0]

    def dbg_dump():
        if not DEBUG:
            return
        i = dbg_i[0]
        dbg_i[0] += 1
        if i >= NDBG:
            return
        for c in range(DC):
            dma(dbg_d[i, :, c, :], x_t[:, c, :], tuple(xb[c]), ())

    modq_dev = [(0, n) for n in range(6, 18)] + [(l_, n) for l_ in range(1, DEPTH) for n in range(18)]
    mod_done = set()

    mod_pend = []

    def mod_piece(l, n, defer=False):
        w, wb = ws_next("mod")
        wv = w[:, 0:4096].rearrange("p (k n) -> p k n", n=512)
        mod_flush()
        for kc in range(DC):
            mm(banks[7][0:2, :], scond_t[:, kc, :], wv[:, kc, :], kc == 0, kc == DC - 1,
               (scond_b, wb), (bank_b[7],), kc == DC - 1)
        act(modT_t[:], banks[7][0:2, :], AF.Copy, (bank_b[7],), (modT_b,))

        def part_b():
            for i4 in range(4):
                emit(PE, lambda: nc.tensor.transpose(banks[6][:, 2 * i4:2 * i4 + 2], modT_t[0:2, i4 * P:(i4 + 1) * P],
                                                     ident_t[0:2, 0:2]),
                     (modT_b, ident_b), (bank_b[6],), signal=(i4 == 3))
            fc0 = n * 4
            for cd in range(2):
                tt(MOD[:, l, fc0:fc0 + 4, cd], banks[6][:, 0:8].rearrange("p (f n) -> p f n", n=2)[:, :, cd],
                   small_t[:, _SM["b_mod"] + l * 72 + fc0:_SM["b_mod"] + l * 72 + fc0 + 4], ALU.add,
                   (bank_b[6], small_b), (mod_b[l],))
            mod_done.add((l, n))
        mod_pend.append(part_b)
        if not defer:
            mod_flush()

    def mod_flush():
        while mod_pend:
            mod_pend.pop(0)()

    def mod_step():
        if modq_dev:
            mod_piece(*modq_dev.pop(0), defer=False)
        else:
            mod_flush()

    def prep_mods(l, k):
        for n in range(6 * (k + 1)):
            assert (l, n) in mod_done, (l, k, n)
        for cd in range(2):
            stt(GS[:, k, :, cd], MOD[:, l, (3 * k + 1) * 8:(3 * k + 2) * 8, cd], 1.0,
                small_t[:, _SM["norm_g"] + (l * 3 + k) * 8:_SM["norm_g"] + (l * 3 + k) * 8 + 8],
                ALU.add, ALU.mult, (mod_b[l], small_b), (gs_b,))
            ts(HG[:, k, :, cd], MOD[:, l, (3 * k + 2) * 8:(3 * k + 3) * 8, cd], 0.5 if k != 1 else 1.0, None,
               ALU.mult, ALU.bypass, (mod_b[l],), (gs_b,))

    nrm_i = [0]

    def rms_stats(ti, nfeat_chunks=DC, src=None, src_bufs=None, inv_n=1.0 / D):
        t0, n = TILES[ti]
        i = nrm_i[0] % 2
        nrm_i[0] += 1
        for c in range(nfeat_chunks):
            s_ap = x_t[:, c, t0:t0 + n] if src is None else src[c]
            s_b = xb[c][ti] if src is None else src_bufs[c]
            act(sq_t[i][:, c, 0:n], s_ap, AF.Square, (s_b,), (sq_b[i],))
        for c in range(nfeat_chunks):
            mm(banks[6][:, 0:n], ones_bf[:], sq_t[i][:, c, 0:n], c == 0, c == nfeat_chunks - 1,
               (sq_b[i], const_b), (bank_b[6],), c == nfeat_chunks - 1)
        act(rs_t[i][:, 0:n], banks[6][:, 0:n], AF.Ln, (bank_b[6], const_b), (rs_b[i],), bias=eps_t[:], scale=inv_n)
        act(rs_t[i][:, 0:n], rs_t[i][:, 0:n], AF.Exp, (rs_b[i],), (rs_b[i],), scale=-0.5)
        return rs_t[i], rs_b[i]

    tmp_i = [0]

    def norm_mod(k, out_fn, only_ti=None):
        l = cur["l"]
        for ti, (t0, n) in enumerate(TILES):
            if only_ti is not None and ti != only_ti:
                continue
            cd = 0 if ti < 2 else 1
            rs, rsb = rms_stats(ti)
            for c in range(DC):
                j = tmp_i[0] % 3
                tmp_i[0] += 1
                tt(tmp_t[j][:, 0:n], x_t[:, c, t0:t0 + n], rs[:, 0:n], ALU.mult, (xb[c][ti], rsb), (tmp_b[j],))
                o_ap, o_b = out_fn(c, ti)
                act(o_ap, tmp_t[j][:, 0:n], AF.Identity, (tmp_b[j], gs_b, mod_b[l]), (o_b,),
                    bias=MOD[:, l, (3 * k) * 8 + c:(3 * k) * 8 + c + 1, cd], scale=GS[:, k, c:c + 1, cd])

    def h_out(c, ti):
        t0, n = TILES[ti]
        return h_t[:, c, t0:t0 + n], hb[c][ti]

    def resid_add(c, ti, ps_ap, ps_b, k, extra_reads=(), gate_ap=None):
        t0, n = TILES[ti]
        cd = 0 if ti < 2 else 1
        g = HG[:, k, c:c + 1, cd] if gate_ap is None else gate_ap(c, cd)
        stt(x_t[:, c, t0:t0 + n], ps_ap, g, x_t[:, c, t0:t0 + n], ALU.mult, ALU.add,
            (ps_b, gs_b, xb[c][ti]) + tuple(extra_reads), (xb[c][ti],))

    cur = {"l": 0}

    def ffn(k, ph):
        l = cur["l"]
        u_t = ph.enter_context(nc.sbuf_tensor(_un("u"), [P, FC, TOK], BF16))
        ub = [[Buf() for _ in range(3)] for _ in range(FC)]
        sa_t = [ph.enter_context(nc.sbuf_tensor(_un("sa"), [P, 512], F32)) for i in range(2)]
        sa_b = [Buf(), Buf()]
        it = 0
        for jp in range(FC // 2):
            w, wb = ws_next("fin")
            wv = w[:, 0:4096].rearrange("p (jj hf kc m) -> p jj hf kc m", jj=2, hf=2, kc=DC)
            for jj in range(2):
                j = 2 * jp + jj
                for ti, (t0, n) in enumerate(TILES):
                    if jp == 0 and jj == 0:
                        norm_mod(k, h_out, only_ti=ti)
                    pa, pb_ = it % 2, 2 + it % 2
                    it += 1
                    for hf, bk in ((0, pa), (1, pb_)):
                        for kc in range(DC):
                            mm(banks[bk][:, 0:n], wv[:, jj, hf, kc, :], h_t[:, kc, t0:t0 + n], kc == 0, kc == DC - 1,
                               (wb, hb[kc][ti]), (bank_b[bk],), kc == DC - 1)
                    si = it % 2
                    act(sa_t[si][:, 0:n], banks[pa][:, 0:n], AF.Silu, (bank_b[pa],), (sa_b[si],))
                    tt(u_t[:, j, t0:t0 + n], sa_t[si][:, 0:n], banks[pb_][:, 0:n], ALU.mult,
                       (sa_b[si], bank_b[pb_]), (ub[j][ti],))
            mod_step()
        for dc in range(DC):
            w, wb = ws_next("fout")
            wv = w[:, 0:FC * P].rearrange("p (j m) -> p j m", m=P)
            for ti, (t0, n) in enumerate(TILES):
                bk = 4 + it % 2
                it += 1
                for j in range(FC):
                    mm(banks[bk][:, 0:n], wv[:, j, :], u_t[:, j, t0:t0 + n], j == 0, j == FC - 1,
                       (wb, ub[j][ti]), (bank_b[bk],), j == FC - 1)
                resid_add(dc, ti, banks[bk][:, 0:n], bank_b[bk], k)
            mod_step()
        mod_flush()

    def barrier():
        engs = [PE, ACT, DVE, POOLQ, SP]
        allp = engs + slot_eng + chan + gchan
        for q in engs:
            for e in allp:
                if e is q or e.count == 0:
                    continue
                if e.count > q.seen.get(e, 0):
                    q.obj.wait_ge(e.sem, e.count)
                    q.seen[e] = e.count

    def out_proj(y_t, yb, k=1):
        it = 0
        for half in range(2):
            w, wb = ws_next("aout")
            wv = w[:, 0:4096].rearrange("p (dc kc m) -> p dc kc m", dc=4, kc=DC)
            for d4 in range(4):
                dc = half * 4 + d4
                for ti, (t0, n) in enumerate(TILES):
                    bk = 4 + it % 2
                    it += 1
                    for kc in range(DC):
                        mm(banks[bk][:, 0:n], wv[:, d4, kc, :], y_t[:, kc, t0:t0 + n], kc == 0, kc == DC - 1,
                           (wb, yb[kc]), (bank_b[bk],), kc == DC - 1)
                    resid_add(dc, ti, banks[bk][:, 0:n], bank_b[bk], k)

    def rglru(j, ph):
        l = cur["l"]
        sbp = lambda name, shape, dtype=F32: ph.enter_context(nc.sbuf_tensor(_un(name), list(shape), dtype))
        y_t = sbp("y", [P, DC, TOK], BF16); yb = [Buf() for _ in range(DC)]
        gl2_t = [sbp("gl%d" % i_, [P, TOK], BF16) for i_ in range(2)]; gl2_b = [Buf(), Buf()]
        xrp_t = sbp("xrp", [P, NSEG, XRW]); xrp_b = Buf()
        xc_t = sbp("xc", [P, TOK]); xc_b = Buf()
        xcb2_t = [sbp("xcb%d" % i_, [P, TOK], BF16) for i_ in range(2)]; xcb2_b = [Buf(), Buf()]
        r2_t = [sbp("r%d" % i_, [P, TOK]) for i_ in range(2)]; r2_b = [Buf(), Buf()]
        i2_t = [sbp("i%d" % i_, [P, TOK]) for i_ in range(2)]; i2_b = [Buf(), Buf()]
        a2_t = [sbp("a%d" % i_, [P, TOK]) for i_ in range(2)]; a2_b = [Buf(), Buf()]
        hs_t = [sbp("hs%d" % d, [P, TOK]) for d in range(2)]; hs_b = [Buf(), Buf()]
        c8_t = sbp("c8", [P, 2, DC]); c8_b = Buf()
        ini_t = sbp("ini", [P, 16]); ini_b = Buf()
        sf = pc_t[:, PC_SF:PC_SF + 1]
        norm_mod(1, h_out)
        lam_ap = small_t[:, _SM["lam_a"] + j * 16:_SM["lam_a"] + j * 16 + 16].rearrange("p (d c) -> p d c", d=2)
        yy_t = sbp("yy", [P, 2, DC]); pl_t = sbp("pl", [P, 2, DC]); mk_t = sbp("mk", [P, 2, DC])
        act(yy_t[:], lam_ap, AF.Exp, (small_b,), (c8_b,), scale=-1.0)
        act(c8_t[:], yy_t[:], AF.Ln, (c8_b, const_b), (c8_b,), bias=one_t[:])
        ts(pl_t[:], yy_t[:], -0.25, 1.0 / 3.0, ALU.mult, ALU.add, (c8_b,), (c8_b,))
        tt(pl_t[:], pl_t[:], yy_t[:], ALU.mult, (c8_b,), (c8_b,))
        ts(pl_t[:], pl_t[:], -0.5, None, ALU.add, ALU.bypass, (c8_b,), (c8_b,))
        tt(pl_t[:], pl_t[:], yy_t[:], ALU.mult, (c8_b,), (c8_b,))
        ts(pl_t[:], pl_t[:], 1.0, None, ALU.add, ALU.bypass, (c8_b,), (c8_b,))
        tt(pl_t[:], pl_t[:], yy_t[:], ALU.mult, (c8_b,), (c8_b,))
        ts(mk_t[:], yy_t[:], 0.1, None, ALU.is_lt, ALU.bypass, (c8_b,), (c8_b,))
        tt(pl_t[:], pl_t[:], c8_t[:], ALU.subtract, (c8_b,), (c8_b,))
        tt(pl_t[:], pl_t[:], mk_t[:], ALU.mult, (c8_b,), (c8_b,))
        tt(c8_t[:], c8_t[:], pl_t[:], ALU.add, (c8_b,), (c8_b,))
        ts(c8_t[:], c8_t[:], -8.0, None, ALU.mult, ALU.bypass, (c8_b,), (c8_b,))
        emit(DVE, lambda: nc.vector.memset(xrp_t[:], 0.0), (), (xrp_b,))
        itc = [0]
        wts = {}
        xc3 = xc_t[:, :].rearrange("p (s t) -> p s t", t=SEG)

        def F1(c):
            cp, cc = c // 2, c % 2
            if cc == 0:
                w, wb = ws_next("ain")
                wg, wgb = ws_next("agate")
                wts[cp] = (w[:, 0:4096].rearrange("p (cc g kc m) -> p cc g kc m", cc=2, g=2, kc=DC), wb,
                           wg[:, 0:1024].rearrange("p (cc q m) -> p cc q m", cc=2, q=4), wgb)
            wv, wb, wgv, wgb = wts[cp]
            gl_t, gl_b = gl2_t[c % 2], gl2_b[c % 2]
            xcb_t, xcb_b = xcb2_t[c % 2], xcb2_b[c % 2]
            for ti, (t0, n) in enumerate(TILES):
                for g in range(2):
                    bk = (0 if g == 0 else 2) + itc[0] % 2
                    for kc in range(DC):
                        mm(banks[bk][:, 0:n], wv[:, cc, g, kc, :], h_t[:, kc, t0:t0 + n], kc == 0, kc == DC - 1,
                           (wb, hb[kc][ti]), (bank_b[bk],), kc == DC - 1)
                    if g == 0:
                        act(gl_t[:, t0:t0 + n], banks[bk][:, 0:n], AF.Gelu, (bank_b[bk],), (gl_b,))
                    else:
                        ns = n // SEG
                        emit(DVE, lambda: nc.vector.tensor_copy(
                            out=xrp_t[:, 2 * ti:2 * ti + ns, 2:2 + SEG],
                            in_=banks[bk][:, 0:n].rearrange("p (s t) -> p s t", t=SEG)), (bank_b[bk],), (xrp_b,))
                itc[0] += 1
            ts(xrp_t[:, 1:4, 0:2], xrp_t[:, 0:3, SEG:SEG + 2], sf, None, ALU.mult, ALU.bypass, (xrp_b, pc_b), (xrp_b,))
            ts(xrp_t[:, 0:3, SEG + 2:SEG + 3], xrp_t[:, 1:4, 2:3], sf, None, ALU.mult, ALU.bypass, (xrp_b, pc_b), (xrp_b,))
            cw = lambda kk: small_t[:, _SM["conv_w"] + (j * 4 + kk) * 8 + c:_SM["conv_w"] + (j * 4 + kk) * 8 + c + 1]
            cb = small_t[:, _SM["conv_b"] + j * 8 + c:_SM["conv_b"] + j * 8 + c + 1]
            ts(xc3, xrp_t[:, :, 0:SEG], cw(0), cb, ALU.mult, ALU.add, (xrp_b, small_b), (xc_b,))
            for kk in range(1, 4):
                stt(xc3, xrp_t[:, :, kk:kk + SEG], cw(kk), xc3, ALU.mult, ALU.add, (xrp_b, small_b, xc_b), (xc_b,))
            act(xcb_t[:], xc_t[:], AF.Copy, (xc_b,), (xcb_b,))

        def F2(c, d):
            cp, cc = c // 2, c % 2
            wv, wb, wgv, wgb = wts[cp]
            xcb_t, xcb_b = xcb2_t[c % 2], xcb2_b[c % 2]
            r_t, r_b, i_t, i_b, a_t, a_b = r2_t[d], r2_b[d], i2_t[d], i2_b[d], a2_t[d], a2_b[d]
            gba = small_t[:, _SM["gb_a"] + (j * 2 + d) * 8 + c:_SM["gb_a"] + (j * 2 + d) * 8 + c + 1]
            gbx = small_t[:, _SM["gb_x"] + (j * 2 + d) * 8 + c:_SM["gb_x"] + (j * 2 + d) * 8 + c + 1]
            for ti, (t0, n) in enumerate(TILES):
                br, bi = 4 + itc[0] % 2, 6 + itc[0] % 2
                itc[0] += 1
                mm(banks[br][:, 0:n], wgv[:, cc, 2 * d, :], xcb_t[:, t0:t0 + n], True, True,
                   (wgb, xcb_b), (bank_b[br],), True)
                mm(banks[bi][:, 0:n], wgv[:, cc, 2 * d + 1, :], xcb_t[:, t0:t0 + n], True, True,
                   (wgb, xcb_b), (bank_b[bi],), True)
                act(r_t[:, t0:t0 + n], banks[br][:, 0:n], AF.Sigmoid, (bank_b[br], small_b), (r_b,), bias=gba)
                act(i_t[:, t0:t0 + n], banks[bi][:, 0:n], AF.Sigmoid, (bank_b[bi], small_b), (i_b,), bias=gbx)

        def F2b(c):
            for d in range(2):
                r_t, r_b, a_t, a_b = r2_t[d], r2_b[d], a2_t[d], a2_b[d]
                act(a_t[:], r_t[:], AF.Exp, (r_b, c8_b), (a_b,), scale=c8_t[:, d, c:c + 1])
                act(r_t[:], a_t[:], AF.Square, (a_b,), (r_b,))

        def F3(c):
            xcb_t, xcb_b = xcb2_t[c % 2], xcb2_b[c % 2]
            for d in range(2):
                r_t, r_b = r2_t[d], r2_b[d]
                act(r_t[:], r_t[:], AF.Sqrt, (r_b, const_b), (r_b,), bias=one_t[:], scale=-1.0)
            for d in range(2):
                r_t, r_b, i_t, i_b = r2_t[d], r2_b[d], i2_t[d], i2_b[d]
                tt(i_t[:], i_t[:], xcb_t[:], ALU.mult, (i_b, xcb_b), (i_b,))
                tt(i_t[:], i_t[:], r_t[:], ALU.mult, (i_b, r_b), (i_b,))

        def T(c):
            gl_t, gl_b = gl2_t[c % 2], gl2_b[c % 2]
            for d in range(2):
                i_t, i_b, a_t, a_b = i2_t[d], i2_b[d], a2_t[d], a2_b[d]
                hs = hs_t[d]
                h0 = pc_t[:, PC_H0 + (j * 2 + d) * 8 + c:PC_H0 + (j * 2 + d) * 8 + c + 1]
                order = [0, 1, 2, 3, 4] if d == 0 else [3, 2, 1, 0, 4]
                for oi, s in enumerate(order):
                    lo, hi = s * SEG, (s + 1) * SEG
                    if s == 4:
                        init = 0.0
                        rd = ()
                    elif oi == 0:
                        init = h0
                        rd = (pc_b,)
                    else:
                        prev = order[oi - 1]
                        pcol = prev * SEG + (SEG - 1 if d == 0 else 0)
                        ic_ = (itc[0] + oi) % 16
                        ts(ini_t[:, ic_:ic_ + 1], hs[:, pcol:pcol + 1], sf, None, ALU.mult, ALU.bypass,
                           (hs_b[d], pc_b), (ini_b,))
                        init = ini_t[:, ic_:ic_ + 1]
                        rd = (ini_b,)
                    if d == 0:
                        a_ap, b_ap, o_ap = a_t[:, lo:hi], i_t[:, lo:hi], hs[:, lo:hi]
                    else:
                        a_ap, b_ap, o_ap = a_t[:, lo:hi][:, ::-1], i_t[:, lo:hi][:, ::-1], hs[:, lo:hi][:, ::-1]
                    emit(DVE, lambda: nc.vector.tensor_tensor_scan(o_ap, a_ap, b_ap, init, ALU.mult, ALU.add),
                         (a_b, i_b) + rd, (hs_b[d],))
                itc[0] += 5
                col0 = SEG - 1 if d == 0 else 0
                emit(DVE, lambda: nc.vector.tensor_copy(
                    out=fin_t[:, j, :, d, c],
                    in_=hs[:, :].rearrange("p (s t) -> p s t", t=SEG)[:, :, col0]), (hs_b[d],), (fin_b,))
            tt(hs_t[0][:], hs_t[0][:], hs_t[1][:], ALU.add, (hs_b[0], hs_b[1]), (hs_b[0],))
            tt(y_t[:, c, :], hs_t[0][:], gl_t[:], ALU.mult, (hs_b[0], gl_b), (yb[c],))

        F1(0)
        F1(1)
        F2(0, 0)
        F2(0, 1)
        F2b(0)
        for c in range(DC):
            F3(c)
            T(c)
            if c + 2 < DC:
                F1(c + 2)
            if c + 1 < DC:
                F2(c + 1, 0)
                F2(c + 1, 1)
                F2b(c + 1)
        out_proj(y_t, yb)

    def attention(j, ph):
        l = cur["l"]
        sbp = lambda name, shape, dtype=F32: ph.enter_context(nc.sbuf_tensor(_un(name), list(shape), dtype))
        NKB = 12
        V_t = sbp("V", [P, NKB, D], BF16); V_b = [Buf() for _ in range(NKB)]
        o_t = sbp("oall", [P, DC, TOK], BF16); o_b = [Buf() for _ in range(DC)]
        q_t = [sbp("q%d" % i, [P, TOK], BF16) for i in range(2)]; q_b = [Buf(), Buf()]
        k_t = [sbp("k%d" % i, [P, TOK + 256], BF16) for i in range(2)]; k_b = [Buf(), Buf()]
        cos_t = sbp("cos", [P, TOK], BF16); sin_t = sbp("sin", [P, TOK], BF16); rope_b = Buf()
        oc_t = [sbp("oc%d" % i_, [P, 512]) for i_ in range(2)]; oc_b = [Buf(), Buf()]
        r1_t, r1_b = tmp_t[0], tmp_b[0]
        r2_t, r2_b = tmp_t[1], tmp_b[1]
        ko_t = [sbp("ko0", [P, TOK])] * 2; ko_b = [Buf()] * 2
        vo_t = [sbp("vo%d" % i, [P, 512]) for i in range(2)]; vo_b = [Buf(), Buf()]
        pT_t = [sbp("pT%d" % i, [P, 512], BF16) for i in range(4)]; pT_b = [Buf() for _ in range(4)]
        of_t, of_b = tmp_t[2], tmp_b[2]
        lam_t = sbp("lamt", [P, 8]); lam_b = Buf()
        norm_mod(1, h_out)
        emit(gchan[1], lambda: nc.gpsimd.dma_start(out=cos_t[:], in_=cos_d), (), (rope_b,), via=POOLQ)
        emit(gchan[2], lambda: nc.gpsimd.dma_start(out=sin_t[:], in_=sin_d), (), (rope_b,), via=POOLQ)
        emit(gchan[0], lambda: nc.gpsimd.dma_start(out=V_t[:, 10:12, :], in_=vc_d), (), (V_b[10], V_b[11]), via=POOLQ)
        lqk_t = sbp("lqk", [P, 256]); lqk_b = Buf()
        dma(lqk_t[:], lqk_d, (), (lqk_b,))
        lq = lqk_t[:, 0:128]
        lk = lqk_t[:, 128:256]
        lp_t = lqk_t[:, 0:128]
        tt(lp_t, lq, lk, ALU.mult, (lqk_b,), (lqk_b, lam_b))
        emit(DVE, lambda: nc.vector.reduce_sum(out=lam_t[:, 0:2], in_=lp_t.rearrange("p (m d) -> p m d", m=2),
                                               axis=mybir.AxisListType.X), (lqk_b, lam_b), (lam_b,))
        act(lam_t[:, 2:4], lam_t[:, 0:2], AF.Exp, (lam_b,), (lam_b,))
        tt(lam_t[:, 4:5], lam_t[:, 3:4], lam_t[:, 2:3], ALU.subtract, (lam_b,), (lam_b,))
        ts(lam_t[:, 4:5], lam_t[:, 4:5], -LAM_INIT, None, ALU.add, ALU.bypass, (lam_b,), (lam_b,))
        ts(lam_t[:, 5:6], small_t[:, _SM["sub_g"]:_SM["sub_g"] + 1], 1.0 - LAM_INIT, None, ALU.mult, ALU.bypass,
           (small_b,), (lam_b,))
        neglam = lam_t[:, 4:5]
        gsub = lam_t[:, 5:6]
        it = 0
        if ATT_LEVEL < 2:
            return
        for half in range(2):
            w, wb = ws_next("wv")
            wv = w[:, 0:4096].rearrange("p (k n) -> p k n", n=512)
            for tb in range(TOK // P):
                ti = min(tb // 4, 2)
                bk = it % 2
                for kc in range(DC):
                    mm(banks[bk][:, :], h_t[:, kc, tb * P:(tb + 1) * P], wv[:, kc, :], kc == 0, kc == DC - 1,
                       (wb, hb[kc][ti]), (bank_b[bk],), kc == DC - 1)
                act(V_t[:, tb, half * 512:(half + 1) * 512], banks[bk][:, :], AF.Copy, (bank_b[bk],), (V_b[tb],))
                vi = it % 2
                emit(DVE, lambda: nc.vector.tensor_copy(out=vo_t[vi][:], in_=banks[bk][:, :]), (bank_b[bk],), (vo_b[vi],))
                if ATT_LEVEL != 21:
                    dma(v_d[tb * P:(tb + 1) * P, half * 512:(half + 1) * 512], vo_t[vi][:], (vo_b[vi],), ())
                it += 1
        pend_fin = []
        fin_bank = [0]
        if ATT_LEVEL < 3 or ATT_LEVEL == 21:
            return
        for hd in range(8):
            w, wb = ws_next("qk")
            wv = w[:, 0:4096].rearrange("p (f kc m) -> p f kc m", f=4, kc=DC)
            hi = hd % 2
            emit(gchan[1 + hi], lambda: nc.gpsimd.dma_start(out=k_t[hi][:, TOK:TOK + 256], in_=kc_d[:, hd, :]),
                 (), (k_b[hi],), via=POOLQ)
            for which in range(2):
                dst = q_t[hi] if which == 0 else k_t[hi]
                dst_b = q_b[hi] if which == 0 else k_b[hi]
                for ti, (t0, n) in enumerate(TILES):
                    b0, b1 = it % 2, 2 + it % 2
                    it += 1
                    for f, bk in ((0, b0), (1, b1)):
                        for kc in range(DC):
                            mm(banks[bk][:, 0:n], wv[:, 2 * which + f, kc, :], h_t[:, kc, t0:t0 + n], kc == 0,
                               kc == DC - 1, (wb, hb[kc][ti]), (bank_b[bk],), kc == DC - 1)
                    if which == 1:
                        act(ko_t[hi][:, t0:t0 + n], banks[b0][:, 0:n], AF.Copy, (bank_b[b0],), (ko_b[hi],))
                    tt(r1_t[:, 0:n], banks[b0][:, 0:n], cos_t[:, t0:t0 + n], ALU.mult, (bank_b[b0], rope_b), (r1_b,))
                    tt(r2_t[:, 0:n], banks[b1][:, 0:n], sin_t[:, t0:t0 + n], ALU.mult, (bank_b[b1], rope_b), (r2_b,))
                    tt(dst[:, t0:t0 + n], r1_t[:, 0:n], r2_t[:, 0:n], ALU.add, (r1_b, r2_b), (dst_b,))
            dma(kT_d[:, hd, :], ko_t[hi][:], (ko_b[hi],), ())
            for ti, (t0, n) in enumerate(TILES if ATT_LEVEL >= 4 else []):
                nq = n // SEG
                inflight = []
                for kbs in range(NKB + 1):
                    if kbs == 5 and pend_fin:
                        fin_bank[0] = (it + 1) % 4
                        pend_fin.pop(0)()
                    if kbs < NKB:
                        kb = kbs
                        kseg = kb // 2 if kb < 10 else 5
                        sbs = [it % 4, (it + 1) % 4]
                        it += 2
                        for m in range(2):
                            mm(banks[sbs[m]][:, 0:n], k_t[hi][m * 64:(m + 1) * 64, kb * P:(kb + 1) * P],
                               q_t[hi][m * 64:(m + 1) * 64, t0:t0 + n], True, True, (k_b[hi], q_b[hi]), (bank_b[sbs[m]],), True)
                        for m in range(2):
                            for qs in range(nq):
                                qseg = 2 * ti + qs
                                mcol = PC_MB + kseg * 5 + qseg
                                act(pT_t[sbs[m]][:, qs * SEG:(qs + 1) * SEG], banks[sbs[m]][:, qs * SEG:(qs + 1) * SEG], AF.Exp,
                                    (bank_b[sbs[m]], pc_b), (pT_b[sbs[m]],), bias=pc_t[:, mcol:mcol + 1], scale=0.125)
                        inflight.append(sbs)
                    if kbs >= 1:
                        kb = kbs - 1
                        for m in range(2):
                            pi = inflight[kb][m]
                            mm(banks[4 + m][:, 0:n], V_t[:, kb, hd * P:(hd + 1) * P], pT_t[pi][:, 0:n], kb == 0, kb == NKB - 1,
                               (V_b[kb], pT_b[pi]), (bank_b[4 + m],), kb == NKB - 1)
                            mm(banks[6 + m][:, 0:n], ones_bf[:], pT_t[pi][:, 0:n], kb == 0, kb == NKB - 1,
                               (const_b, pT_b[pi]), (bank_b[6 + m],), kb == NKB - 1)
                if ATT_LEVEL < 5:
                    continue
                A_t, A_b, B_t, B_b = rs_t[0], rs_b[0], rs_t[1], rs_b[1]
                C_t, C_b, D_t, D_b = oc_t[0], oc_b[0], oc_t[1], oc_b[1]
                act(A_t[:, 0:n], banks[6][:, 0:n], AF.Ln, (bank_b[6],), (A_b,))
                emit(DVE, lambda: nc.vector.tensor_copy(out=C_t[:, 0:n], in_=banks[4][:, 0:n]), (bank_b[4],), (C_b,))
                act(B_t[:, 0:n], banks[7][:, 0:n], AF.Ln, (bank_b[7],), (B_b,))
                emit(DVE, lambda: nc.vector.tensor_copy(out=D_t[:, 0:n], in_=banks[5][:, 0:n]), (bank_b[5],), (D_b,))
                act(A_t[:, 0:n], A_t[:, 0:n], AF.Exp, (A_b,), (A_b,), scale=-1.0)
                act(B_t[:, 0:n], B_t[:, 0:n], AF.Exp, (B_b,), (B_b,), scale=-1.0)
                tt(C_t[:, 0:n], C_t[:, 0:n], A_t[:, 0:n], ALU.mult, (C_b, A_b), (C_b,))
                tt(D_t[:, 0:n], D_t[:, 0:n], B_t[:, 0:n], ALU.mult, (D_b, B_b), (D_b,))
                stt(C_t[:, 0:n], D_t[:, 0:n], neglam, C_t[:, 0:n], ALU.mult, ALU.add, (C_b, D_b, lam_b), (C_b,))
                act(sq_t[0][:, 0, 0:n], C_t[:, 0:n], AF.Square, (C_b,), (sq_b[0],))

                def fin2(hd=hd, t0=t0, n=n):
                    nb_ = fin_bank[0]
                    mm(banks[nb_][:, 0:n], ones_bf[:], sq_t[0][:, 0, 0:n], True, True, (sq_b[0], const_b), (bank_b[nb_],), True)
                    act(A_t[:, 0:n], banks[nb_][:, 0:n], AF.Ln, (bank_b[nb_], const_b), (A_b,), bias=eps_t[:], scale=1.0 / P)
                    act(A_t[:, 0:n], A_t[:, 0:n], AF.Exp, (A_b,), (A_b,), scale=-0.5)
                    stt(o_t[:, hd, t0:t0 + n], C_t[:, 0:n], gsub, A_t[:, 0:n], ALU.mult, ALU.mult,
                        (C_b, lam_b, A_b), (o_b[hd],))
                pend_fin.append(fin2)
        while pend_fin:
            fin_bank[0] = it % 4
            pend_fin.pop(0)()
        if ATT_LEVEL >= 5:
            out_proj(o_t, o_b)

    def pooling(j, ph):
        l = cur["l"]
        sbp = lambda name, shape, dtype=F32: ph.enter_context(nc.sbuf_tensor(_un(name), list(shape), dtype))
        hp_t = sbp("hp", [P, 2, NSEG, HPW]); hp_b = Buf()
        L_t = [sbp("L%d" % i, [P, 2, NSEG, HPW]) for i in range(2)]; L_b = [Buf(), Buf()]
        ic_t = [sbp("ic%d" % i, [P, TOK]) for i in range(2)]; ic_b = [Buf(), Buf()]
        d_t = sbp("dd", [P, 2, TOK], BF16); d_b = Buf()
        df_t = sbp("df", [P, 2, NSEG, SEG]); df_b = Buf()
        gsc_t = sbp("gsc", [P, DC, 2]); gsc_b = Buf()
        rsa_t = sbp("rsa", [P, TOK]); rsa_b = Buf()
        sf = pc_t[:, PC_SF:PC_SF + 1]
        for cd in range(2):
            tt(gsc_t[:, :, cd], HG[:, 1, :, cd], small_t[:, _SM["c_scale"]:_SM["c_scale"] + 8], ALU.mult,
               (gs_b, small_b), (gsc_b,))
        for ti, (t0, n) in enumerate(TILES):
            rs, rsb = rms_stats(ti)
            emit(DVE, lambda: nc.vector.tensor_copy(out=rsa_t[:, t0:t0 + n], in_=rs[:, 0:n]), (rsb,), (rsa_b,))
        w, wb = ws_next("pool")
        wv = w[:, 0:2048].rearrange("p (g ki mo m) -> p g ki mo m", g=4, ki=2, mo=2)
        emit(DVE, lambda: nc.vector.memset(hp_t[:], 0.0), (), (hp_b,))
        it = 0
        for g in range(4):
            dma(ic_t[g % 2][:], icnt_d[:, g, :], (), (ic_b[g % 2],))
            for ci in range(2):
                c = 2 * g + ci
                for ti, (t0, n) in enumerate(TILES):
                    cd = 0 if ti < 2 else 1
                    ns = n // SEG
                    jt = tmp_i[0] % 3
                    tmp_i[0] += 1
                    tt(tmp_t[jt][:, 0:n], x_t[:, c, t0:t0 + n], rsa_t[:, t0:t0 + n], ALU.mult, (xb[c][ti], rsa_b), (tmp_b[jt],))
                    act(hp_t[:, ci, 2 * ti:2 * ti + ns, 8:8 + SEG], tmp_t[jt][:, 0:n].rearrange("p (s t) -> p s t", t=SEG),
                        AF.Identity, (tmp_b[jt], gs_b, mod_b[l]), (hp_b,),
                        bias=MOD[:, l, 3 * 8 + c:3 * 8 + c + 1, cd], scale=GS[:, 1, c:c + 1, cd])
            for ci in range(2):
                ts(hp_t[:, ci, 1:4, 0:8], hp_t[:, ci, 0:3, SEG:SEG + 8], sf, None, ALU.mult, ALU.bypass, (hp_b, pc_b), (hp_b,))
                ts(hp_t[:, ci, 0:3, SEG + 8:SEG + 16], hp_t[:, ci, 1:4, 8:16], sf, None, ALU.mult, ALU.bypass,
                   (hp_b, pc_b), (hp_b,))
            src, src_b = hp_t, hp_b
            lo, hi = 0, HPW
            for lev in range(g + 1):
                dst, dst_b = L_t[lev % 2], L_b[lev % 2]
                sh = 1 if lev == 0 else 2 ** (lev - 1)
                if lev == 0:
                    nlo, nhi = lo + 1, hi
                    for ci in range(2):
                        tt(dst[:, ci, :, nlo:nhi], src[:, ci, :, nlo - 1:nhi - 1], src[:, ci, :, nlo:nhi], ALU.add,
                           (src_b,), (dst_b,))
                else:
                    nlo, nhi = lo + sh, hi - sh
                    for ci in range(2):
                        tt(dst[:, ci, :, nlo:nhi], src[:, ci, :, nlo - sh:nhi - sh], src[:, ci, :, nlo + sh:nhi + sh],
                           ALU.add, (src_b,), (dst_b,))
                lo, hi = nlo, nhi
                src, src_b = dst, dst_b
            assert lo <= 8 and hi >= 8 + SEG
            ic3 = ic_t[g % 2][:, :].rearrange("p (s t) -> p s t", t=SEG)
            for ci in range(2):
                tt(df_t[:, ci], src[:, ci, :, 8:8 + SEG], ic3, ALU.mult, (src_b, ic_b[g % 2]), (df_b,))
                tt(d_t[:, ci, :].rearrange("p (s t) -> p s t", t=SEG), df_t[:, ci], hp_t[:, ci, :, 8:8 + SEG], ALU.subtract,
                   (df_b, hp_b), (d_b,))
            for mo in range(2):
                c = 2 * g + mo
                for ti, (t0, n) in enumerate(TILES):
                    bk = 4 + it % 2
                    it += 1
                    for ki in range(2):
                        mm(banks[bk][:, 0:n], wv[:, g, ki, mo, :], d_t[:, ki, t0:t0 + n], ki == 0, ki == 1,
                           (wb, d_b), (bank_b[bk],), ki == 1)
                    resid_add(c, ti, banks[bk][:, 0:n], bank_b[bk], 1, extra_reads=(gsc_b,),
                              gate_ap=lambda c_, cd_: gsc_t[:, c_, cd_:cd_ + 1])

    for n_ in range(6):
        mod_piece(0, n_)
    nst = 0
    for l in range(DEPTH):
        cur["l"] = l
        if nst >= STAGES:
            break
        prep_mods(l, 0)
        with ExitStack() as ph:
            ffn(0, ph)
            barrier()
        dbg_dump()
        nst += 1
        if nst >= STAGES:
            break
        kind, j = l % 3, l // 3
        prep_mods(l, 1)
        with ExitStack() as ph:
            if kind == 0:
                rglru(j, ph)
            elif kind == 1:
                attention(j, ph)
            else:
                pooling(j, ph)
            barrier()
        dbg_dump()
        nst += 1
        if nst >= STAGES:
            break
        prep_mods(l, 2)
        with ExitStack() as ph:
            ffn(2, ph)
            barrier()
        dbg_dump()
        nst += 1
    assert STAGES < 12 or wst["next"] == n_pieces, (wst["next"], n_pieces)

    with ExitStack() as ph:
        yo_t = [ph.enter_context(nc.sbuf_tensor(_un("yo"), [P, 512], F32)) for i in range(3)]
        yo_b = [Buf() for _ in range(3)]
        oi = 0
        for ti, (t0, n) in enumerate(TILES):
            rs, rsb = rms_stats(ti)
            for c in range(DC):
                k3 = oi % 3
                oi += 1
                stt(yo_t[k3][:, 0:n], x_t[:, c, t0:t0 + n], small_t[:, _SM["fin_g"] + c:_SM["fin_g"] + c + 1], rs[:, 0:n],
                    ALU.mult, ALU.mult, (xb[c][ti], small_b, rsb), (yo_b[k3],))
                dma(yT_d[:, c, t0:t0 + n], yo_t[k3][:, 0:n], (yo_b[k3],), ())
        dma(st_d, fin_t[:].rearrange("p j s d c -> p (j s d c)"), (fin_b,), ())
        for e in chan:
            if e.count > SP.seen.get(e, 0):
                nc.sync.wait_ge(e.sem, e.count)
                SP.seen[e] = e.count
        barrier()
    es.close()
    return nc, wst["issued"]


_CACHE = {}


def kernel(**inputs):
    tags = _piece_tags()
    key = (STAGES, DEBUG)
    if key not in _CACHE:
        n_used = len(tags)
        if STAGES < 12:
            _, n_used = _build(len(tags), sum(e for _, e in tags))
        wtotal = sum(e for _, e in tags[:n_used])
        _CACHE[key] = (_build(n_used, wtotal)[0], n_used)
    nc, n_used = _CACHE[key]
    import time as _t
    _t0 = _t.time()
    in_maps, _, _ = _host_prep(inputs, n_used)
    _t1 = _t.time()
    res = run_bass_kernel_spmd(nc, in_maps, core_ids=list(range(NCORES)))
    if DEBUG:
        print('prep %.1fs run %.1fs' % (_t1 - _t0, _t.time() - _t1))
    outs = res.results
    B, S = 32, 256
    y_prompt = np.zeros((B, S, D), np.float32)
    y_sample = np.zeros((2, 1024, D), np.float32)
    new_state = np.zeros((B, 2, 2, D), np.float32)
    new_k = np.zeros((B, 1, S, 8, 128), np.float32)
    new_v = np.zeros((B, 1, S, 8, 128), np.float32)
    for c in range(NCORES):
        r = outs[c]
        yT = np.asarray(r["yT"]).transpose(1, 0, 2).reshape(D, TOK)
        kT = np.asarray(r["kT"]).transpose(1, 0, 2).reshape(D, TOK)
        vv = np.asarray(r["vout"])
        st = np.asarray(r["stout"]).reshape(P, 2, NSEG, 2, DC)
        y = yT.T
        kk = kT.T
        if c < 6:
            segs = [(s, 5 * c + s) for s in range(5)]
        else:
            segs = [(4, 30 + (c - 6))]
            y_sample[c - 6] = y[0:1024]
        for s, b in segs:
            y_prompt[b] = y[s * S:(s + 1) * S]
            new_k[b, 0] = kk[s * S:(s + 1) * S].reshape(S, 8, 128)
            new_v[b, 0] = vv[s * S:(s + 1) * S].reshape(S, 8, 128)
            new_state[b] = st[:, :, s, :, :].transpose(1, 2, 3, 0).reshape(2, 2, D)
    if DEBUG:
        kernel.dbg = [np.asarray(outs[c]["dbg"]) for c in range(NCORES)]
    return (y_prompt, y_sample, new_state, new_k, new_v)
```

```python
import math
from contextlib import ExitStack

import numpy as np
import concourse.bass as bass
import concourse.mybir as mybir
from concourse.bass_utils import run_bass_kernel_spmd

F32 = mybir.dt.float32
BF16 = mybir.dt.bfloat16
AF = mybir.ActivationFunctionType
ALU = mybir.AluOpType

NCORES = 8
P = 128
D = 1024
DC = 8
TOK = 1280
NSEG = 5
SEG = 256
DFF = 2816
FC = 22
DEPTH = 4
TILES = [(0, 512), (512, 512), (1024, 256)]
EPS = 1e-6
LAM_INIT = 0.8 - 0.6 * math.exp(-0.3 * 1)
NEG = -30000.0
SLOT_E = 4096
NSLOT = 5
LOOKAHEAD = 2
XRW = 259
HPW = 272
POOL_W = (2, 4, 8, 16)

DEBUG = False
SAME_ENG_ALL = True
ATT_LEVEL = 5
NDBG = 12
STAGES = 12


class _Eng:
    def __init__(self, name, obj, sem, step):
        self.name, self.obj, self.sem, self.step = name, obj, sem, step
        self.count = 0
        self.seen = {}
        self.pend_r = []
        self.pend_w = []
        self.nosame = False


class Buf:
    __slots__ = ("name", "lw", "rd", "excl")

    def __init__(self, name="", excl=False):
        self.name = name
        self.lw = None
        self.rd = {}
        self.excl = excl


def emit(eng, fn, reads=(), writes=(), signal=True, via=None):
    q = via if via is not None else eng
    need = {}

    def add(e, v, raw):
        if e is q and (q.nosame or (not raw and not SAME_ENG_ALL)):
            return
        if v > need.get(e, 0):
            need[e] = v

    for b in reads:
        if b.lw is not None:
            add(b.lw[0], b.lw[1], True)
        if b.excl:
            for e, v in b.rd.items():
                if e is not q:
                    add(e, v, False)
    for b in writes:
        if b.lw is not None:
            add(b.lw[0], b.lw[1], False)
        for e, v in b.rd.items():
            add(e, v, False)
    if via is not None and eng.count > 0:
        add(eng, eng.count, True)
    for e, v in need.items():
        if v > q.seen.get(e, 0):
            q.obj.wait_ge(e.sem, v)
            q.seen[e] = v
    ins = fn()
    eng.pend_r.extend(reads)
    eng.pend_w.extend(writes)
    if signal:
        eng.count += eng.step
        ins.then_inc(eng.sem, eng.step)
        c = eng.count
        for b in eng.pend_w:
            b.lw = (eng, c)
            b.rd = {}
        for b in eng.pend_r:
            if b.lw is not None and b.lw[0] is eng and b.lw[1] == c:
                continue
            if c > b.rd.get(eng, 0):
                b.rd[eng] = c
        eng.pend_r = []
        eng.pend_w = []
    return ins


def _fm(v):
    v = np.asarray(v, np.float32)
    lead = v.shape[:-1]
    r = v.reshape(*lead, DC, P)
    r = np.moveaxis(r, -1, 0)
    return np.ascontiguousarray(r)


def _rope_partner():
    p = np.arange(P)
    within = p % 64
    half = (within % 32) // 16
    partner = np.where(half == 0, p + 16, p - 16)
    sign = np.where(half == 0, -1.0, 1.0).astype(np.float32)
    axis = within // 32
    f = within % 16
    return partner, sign, axis, f


def _sched():
    out = []
    modq = [(0, n) for n in range(6, 18)] + [(l, n) for l in range(1, DEPTH) for n in range(18)]
    for n in range(6):
        out.append(("mod", 4096, (0, n)))

    def ffn(l, i):
        for jp in range(FC // 2):
            out.append(("fin", 4096, (l, i, jp)))
            if modq:
                out.append(("mod", 4096, modq.pop(0)))
        for dc in range(DC):
            out.append(("fout", FC * P, (l, i, dc)))
            if modq:
                out.append(("mod", 4096, modq.pop(0)))

    for l in range(DEPTH):
        ffn(l, 0)
        kind, j = l % 3, l // 3
        if kind == 0:
            for cp in range(4):
                out.append(("ain", 4096, (j, cp)))
                out.append(("agate", 1024, (j, cp)))
            for half in range(2):
                out.append(("aout", 4096, ("a", j, half)))
        elif kind == 1:
            for half in range(2):
                out.append(("wv", 4096, (j, half)))
            for hd in range(8):
                out.append(("qk", 4096, (j, hd)))
            for half in range(2):
                out.append(("aout", 4096, ("b", j, half)))
        else:
            out.append(("pool", 2048, (j,)))
        ffn(l, 1)
    assert not modq
    return out


def _piece_array(inp, tag, key):
    if tag == "mod":
        l, n = key
        w = inp["w_mod"][l]
        return w[:, n * 512:(n + 1) * 512].reshape(DC, P, 512).transpose(1, 0, 2).reshape(P, -1)
    if tag == "fin":
        l, i, jp = key
        wi = inp["w_ffn_in"][l, i].reshape(DC, P, 2, FC, P)
        return wi[:, :, :, 2 * jp:2 * jp + 2, :].transpose(1, 3, 2, 0, 4).reshape(P, -1)
    if tag == "fout":
        l, i, dc = key
        wo = inp["w_ffn_out"][l, i].reshape(FC, P, DC, P)
        return wo[:, :, dc, :].transpose(1, 0, 2).reshape(P, -1)
    if tag == "ain":
        j, cp = key
        w_in = inp["a_w_in"][j].reshape(DC, P, 2, DC, P)
        return w_in[:, :, :, 2 * cp:2 * cp + 2, :].transpose(1, 3, 2, 0, 4).reshape(P, -1)
    if tag == "agate":
        j, cp = key
        gws = [inp["a_gate_w_a"][j, 0], inp["a_gate_w_x"][j, 0], inp["a_gate_w_a"][j, 1], inp["a_gate_w_x"][j, 1]]
        g = np.zeros((P, 2, 4, P), np.float32)
        for cc in range(2):
            c = 2 * cp + cc
            for qi in range(4):
                for hb in range(2):
                    g[hb * 64:(hb + 1) * 64, cc, qi, hb * 64:(hb + 1) * 64] = gws[qi][2 * c + hb]
        return g.reshape(P, -1)
    if tag == "aout":
        which, j, half = key
        w = inp["a_w_out"][j] if which == "a" else inp["b_w_o"][j]
        wo = w.reshape(DC, P, DC, P)
        return wo[:, :, 4 * half:4 * half + 4, :].transpose(1, 2, 0, 3).reshape(P, -1)
    if tag == "wv":
        j, half = key
        wv = inp["b_w_qkv"][j][:, 2048:3072]
        return wv[:, half * 512:(half + 1) * 512].reshape(DC, P, 512).transpose(1, 0, 2).reshape(P, -1)
    if tag == "qk":
        j, hd = key
        w = inp["b_w_qkv"][j]
        partner, _, _, _ = _rope_partner()
        q = w[:, 0:1024].reshape(DC, P, 8, P)[:, :, hd, :]
        k = w[:, 1024:2048].reshape(DC, P, 8, P)[:, :, hd, :]
        blk = np.stack([q, q[:, :, partner], k, k[:, :, partner]], 0)
        return blk.transpose(2, 0, 1, 3).reshape(P, -1)
    if tag == "pool":
        (j,) = key
        wp = inp["c_w_pool"][j].reshape(4, 2, P, 2, P)
        return wp.transpose(2, 0, 1, 3, 4).reshape(P, -1)
    raise KeyError(tag)


def _weight_plan(inp, n_used):
    return [(t, _piece_array(inp, t, key)) for t, e, key in _sched()[:n_used]]


def _piece_tags():
    return [(t, e) for t, e, _ in _sched()]


def _core_tokens(inp, c):
    xp, xs = inp["x_prompt"], inp["x_sample"]
    if c < 6:
        x = xp[5 * c:5 * c + 5].reshape(TOK, D)
        prompts = [5 * c + s for s in range(5)]
        sample = None
    else:
        b = c - 6
        x = np.concatenate([xs[b], xp[30 + b]], 0)
        prompts = [None] * 4 + [30 + b]
        sample = b
    return x, prompts, sample


def _host_prep(inp, n_used):
    inp = {k: np.asarray(v) for k, v in inp.items()}
    plan = _weight_plan(inp, n_used)
    tags = _piece_tags()[:n_used]
    assert len(plan) == len(tags)
    for (t0, a), (t1, e) in zip(plan, tags):
        assert t0 == t1 and a.shape == (P, e), (t0, t1, a.shape, e)
    wstream = np.ascontiguousarray(np.concatenate([a for _, a in plan], axis=1), dtype=np.float32)

    partner, sign, axis, f = _rope_partner()
    inv = (10000.0 ** (-np.arange(16, dtype=np.float32) / 16)).astype(np.float32)
    t = np.arange(1024)
    pos = np.stack([t // 64, t % 64], 0).astype(np.float32)
    ang = pos[axis][:, :] * inv[f][:, None]
    ang = ang.astype(np.float32)
    cos_s = np.cos(ang).astype(np.float32)
    sin_s = (np.sin(ang).astype(np.float32) * sign[:, None]).astype(np.float32)

    sm = {
        "norm_g": _fm(inp["norm_g"]).reshape(P, -1),
        "b_mod": _fm(inp["b_mod"].reshape(DEPTH, 9, D)).reshape(P, -1),
        "conv_w": _fm(inp["a_conv_w"]).reshape(P, -1),
        "conv_b": _fm(inp["a_conv_b"]).reshape(P, -1),
        "gb_a": _fm(inp["a_gate_b_a"]).reshape(P, -1),
        "gb_x": _fm(inp["a_gate_b_x"]).reshape(P, -1),
        "lam_a": _fm(inp["a_lambda"]).reshape(P, -1),
        "c_scale": _fm(inp["c_scale"]).reshape(P, -1),
        "fin_g": _fm(inp["final_norm_g"]).reshape(P, -1),
    }
    sub_g = np.asarray(inp["b_subln_g"][0], np.float32).reshape(P, 1)
    lqk = np.concatenate([np.asarray(inp["b_lam_q"][0], np.float32).reshape(1, 128),
                          np.asarray(inp["b_lam_k"][0], np.float32).reshape(1, 128)], axis=1)
    lqk = np.ascontiguousarray(np.broadcast_to(lqk, (P, 256)))
    small = np.concatenate([sm["norm_g"], sm["b_mod"], sm["conv_w"], sm["conv_b"], sm["gb_a"],
                            sm["gb_x"], sm["lam_a"], sm["c_scale"], sm["fin_g"], sub_g], axis=1)
    ident = np.eye(P, dtype=np.float32)

    in_maps = []
    for c in range(NCORES):
        x, prompts, sample = _core_tokens(inp, c)
        xT = np.ascontiguousarray(x.T.reshape(DC, P, TOK).transpose(1, 0, 2))
        if sample is None:
            cond = np.stack([inp["c_ctx"], inp["c_ctx"]], 0)
            h0 = np.zeros((P, 2 * 2 * DC), np.float32)
            sf = 0.0
            kc = np.zeros((P, 8, 256), np.float32)
            vc = np.zeros((P, 2, D), np.float32)
            cos_t = np.ones((P, TOK), np.float32)
            sin_t = np.zeros((P, TOK), np.float32)
        else:
            cond = np.stack([inp["c"][sample], inp["c_ctx"]], 0)
            h0 = _fm(inp["state_rglru"][sample]).reshape(P, -1)
            sf = 1.0
            ck = inp["cache_k_diff"][sample, 0]
            kc = np.ascontiguousarray(ck.transpose(2, 1, 0))
            cv = inp["cache_v_diff"][sample, 0].reshape(256, D)
            vc = np.ascontiguousarray(cv.reshape(2, P, D).transpose(1, 0, 2))
            cos_t = np.concatenate([cos_s, np.ones((P, 256), np.float32)], 1)
            sin_t = np.concatenate([sin_s, np.zeros((P, 256), np.float32)], 1)
        condT = np.ascontiguousarray(cond.T.reshape(DC, P, 2).transpose(1, 0, 2)).reshape(P, -1)
        mb = np.full((6, 5), NEG, np.float32)
        for qs in range(5):
            for ks in range(6):
                if sample is None:
                    ok = (ks == qs)
                else:
                    ok = (qs < 4 and (ks < 4 or ks == 5)) or (qs == 4 and ks == 4)
                if ok:
                    mb[ks, qs] = 0.0
        mbias = np.broadcast_to(mb.reshape(1, 30), (P, 30)).astype(np.float32)
        m01 = (mbias == 0.0).astype(np.float32)
        ic = np.zeros((4, TOK), np.float32)
        for g, win in enumerate(POOL_W):
            for s in range(NSEG):
                if sample is not None and s < 4:
                    T, tt = 1024, s * 256 + np.arange(256)
                else:
                    T, tt = 256, np.arange(256)
                lo = np.clip(tt - win // 2, 0, T)
                hi = np.clip(tt + win // 2, 0, T)
                ic[g, s * 256:(s + 1) * 256] = 1.0 / (hi - lo).astype(np.float32)
        icnt = np.ascontiguousarray(np.broadcast_to(ic[None], (P, 4, TOK))).astype(np.float32)
        flags = np.full((P, 1), sf, np.float32)
        percore = np.concatenate([condT, h0, flags, mbias, m01], axis=1).astype(np.float32)
        in_maps.append({
            "xT": xT, "wstream": wstream, "small": small, "ident": ident, "percore": percore, "lqk": lqk,
            "kcache": kc, "vcache": vc, "cos_t": cos_t, "sin_t": sin_t, "icnt": icnt,
        })
    return in_maps, len(tags), wstream.shape[1]


_SM = {}
_o = 0
for _n, _w in [("norm_g", 96), ("b_mod", 288), ("conv_w", 64), ("conv_b", 16), ("gb_a", 32), ("gb_x", 32),
               ("lam_a", 32), ("c_scale", 8), ("fin_g", 8), ("sub_g", 1)]:
    _SM[_n] = _o
    _o += _w
SMALL_W = _o
PC_COND, PC_H0, PC_SF, PC_MB, PC_M01 = 0, 16, 48, 49, 79
PERCORE_W = 109


def _build(n_pieces, wtotal):
    tags = _piece_tags()[:n_pieces]
    nc = bass.Bass("TRN2", target_bir_lowering=False)
    dt = lambda name, shape, kind="ExternalInput": nc.dram_tensor(name, list(shape), F32, kind=kind).ap()
    xT_d = dt("xT", [P, DC, TOK])
    ws_d = dt("wstream", [P, wtotal])
    small_d = dt("small", [P, SMALL_W])
    ident_d = dt("ident", [P, P])
    lqk_d = dt("lqk", [P, 256])
    pc_d = dt("percore", [P, PERCORE_W])
    kc_d = dt("kcache", [P, 8, 256])
    vc_d = dt("vcache", [P, 2, D])
    cos_d = dt("cos_t", [P, TOK])
    sin_d = dt("sin_t", [P, TOK])
    icnt_d = dt("icnt", [P, 4, TOK])
    yT_d = dt("yT", [P, DC, TOK], "ExternalOutput")
    kT_d = dt("kT", [P, DC, TOK], "ExternalOutput")
    v_d = dt("vout", [TOK, D], "ExternalOutput")
    st_d = dt("stout", [P, 2 * NSEG * 2 * DC], "ExternalOutput")
    if DEBUG:
        dbg_d = dt("dbg", [NDBG, P, DC, TOK], "ExternalOutput")

    es = ExitStack()
    es.enter_context(nc.allow_low_precision("bf16 matmul operands, fp32 accumulation"))
    _uid = [0]

    def _un(name):
        _uid[0] += 1
        return "t%d_%s" % (_uid[0], name)

    sb = lambda name, shape, dtype=F32: es.enter_context(nc.sbuf_tensor(_un(name), list(shape), dtype))

    def mk_eng(name, obj, step):
        return _Eng(name, obj, es.enter_context(nc.semaphore("s_" + name)), step)

    PE = mk_eng("pe", nc.tensor, 1)
    PE.nosame = True
    ACT = mk_eng("act", nc.scalar, 1)
    DVE = mk_eng("dve", nc.vector, 1)
    POOLQ = mk_eng("pool", nc.gpsimd, 1)
    SP = mk_eng("sp", nc.sync, 1)
    slot_eng = [mk_eng("ws%d" % i, None, 16) for i in range(NSLOT)]
    NCH = 12
    chan = [mk_eng("ch%d" % i, None, 16) for i in range(NCH)]
    chan_i = [0]
    gchan = [mk_eng("gch%d" % i, None, 16) for i in range(3)]

    def dma(out_ap, in_ap, reads=(), writes=(), q=None):
        q = q or SP
        ch = chan[chan_i[0] % NCH]
        chan_i[0] += 1
        return emit(ch, lambda: q.obj.dma_start(out=out_ap, in_=in_ap), reads, writes, via=q), ch

    def act(out, in_, func, reads, writes, bias=None, scale=1.0):
        kw = {}
        if bias is not None:
            kw["bias"] = bias
        return emit(ACT, lambda: nc.scalar.activation(out=out, in_=in_, func=func, scale=scale, **kw), reads, writes)

    def tt(out, a, b, op, reads, writes):
        return emit(DVE, lambda: nc.vector.tensor_tensor(out=out, in0=a, in1=b, op=op), reads, writes)

    def ptt(out, a, b, op, reads, writes):
        return emit(POOLQ, lambda: nc.gpsimd.tensor_tensor(out=out, in0=a, in1=b, op=op), reads, writes)

    def ts(out, a, s1, s2, op0, op1, reads, writes):
        return emit(DVE, lambda: nc.vector.tensor_scalar(out=out, in0=a, scalar1=s1, scalar2=s2, op0=op0, op1=op1),
                    reads, writes)

    def stt(out, a, s, b, op0, op1, reads, writes):
        return emit(DVE, lambda: nc.vector.scalar_tensor_tensor(out=out, in0=a, scalar=s, in1=b, op0=op0, op1=op1),
                    reads, writes)

    def mm(out, lhsT, rhs, start, stop, reads, writes, signal):
        return emit(PE, lambda: nc.tensor.matmul(out, lhsT, rhs, start=start, stop=stop), reads, writes, signal=signal)

    x_t = sb("x", [P, DC, TOK])
    xb = [[Buf("x%d_%d" % (c, i)) for i in range(3)] for c in range(DC)]
    h_t = sb("h", [P, DC, TOK], BF16)
    hb = [[Buf("h%d_%d" % (c, i)) for i in range(3)] for c in range(DC)]
    slots = [sb("wslot%d" % i, [P, SLOT_E], BF16) for i in range(NSLOT)]
    slot_b = [Buf("slot%d" % i) for i in range(NSLOT)]
    small_t = sb("small", [P, SMALL_W]); small_b = Buf("small")
    pc_t = sb("percore", [P, PERCORE_W]); pc_b = Buf("pc")
    ident_t = sb("ident", [P, P]); ident_b = Buf("ident")
    ones_bf = sb("ones_bf", [P, P], BF16)
    eps_t = sb("eps", [P, 1])
    one_t = sb("one", [P, 1])
    const_b = Buf("const")
    scond_t = sb("scond", [P, DC, 2], BF16); scond_b = Buf("scond")
    MOD = sb("mod", [P, DEPTH, 72, 2]); mod_b = [Buf("mod%d" % l) for l in range(DEPTH)]
    GS = sb("gs", [P, 3, DC, 2]); HG = sb("hg", [P, 3, DC, 2]); gs_b = Buf("gs")
    fin_t = sb("fin", [P, 2, NSEG, 2, DC]); fin_b = Buf("fin")
    rs_t = [sb("rs%d" % i, [P, 512]) for i in range(2)]; rs_b = [Buf("rs0"), Buf("rs1")]
    sq_t = [sb("sq0", [P, DC, 512], BF16)] * 2; sq_b = [Buf("sq0")] * 2
    tmp_t = [sb("tmp%d" % i, [P, 512]) for i in range(3)]; tmp_b = [Buf("tmp%d" % i) for i in range(3)]
    modT_t = sb("modT", [2, 512]); modT_b = Buf("modT")
    dmasem_b = Buf("dmasem")

    banks = [es.enter_context(nc.psum_tensor("bank%d" % i, [P, 512], F32)) for i in range(8)]
    bank_b = [Buf("bank%d" % i, excl=True) for i in range(8)]

    sm = lambda name, off=0, w=1: small_t[:, _SM[name] + off:_SM[name] + off + w]

    wst = {"issued": 0, "next": 0, "off": 0}
    offs = []
    o = 0
    for tg, e in tags:
        offs.append(o)
        o += e
    assert o == wtotal

    WS_LIMIT = [10 ** 9]

    def ws_issue_upto(k):
        while wst["issued"] <= min(k, n_pieces - 1) and wst["issued"] < WS_LIMIT[0]:
            i = wst["issued"]
            tg, e = tags[i]
            s = i % NSLOT
            emit(slot_eng[s], lambda: nc.gpsimd.dma_start(out=slots[s][:, 0:e], in_=ws_d[:, offs[i]:offs[i] + e]),
                 (), (slot_b[s],), via=POOLQ)
            wst["issued"] += 1

    def ws_next(tag):
        i = wst["next"]
        assert tags[i][0] == tag, (i, tags[i], tag)
        ws_issue_upto(i + LOOKAHEAD)
        wst["next"] += 1
        s = i % NSLOT
        return slots[s], slot_b[s]

    emit(DVE, lambda: nc.vector.memset(ones_bf[:], 1.0), (), (const_b,))
    emit(DVE, lambda: nc.vector.memset(eps_t[:], EPS), (), (const_b,))
    emit(DVE, lambda: nc.vector.memset(one_t[:], 1.0), (), (const_b,))
    emit(DVE, lambda: nc.vector.memset(fin_t[:], 0.0), (), (fin_b,))
    dma(small_t[:], small_d, (), (small_b,))
    dma(pc_t[:], pc_d, (), (pc_b,))
    dma(ident_t[:], ident_d, (), (ident_b,))
    ws_issue_upto(LOOKAHEAD)
    for c in range(DC):
        dma(x_t[:, c, :], xT_d[:, c, :], (), tuple(xb[c]))
    act(scond_t[:], pc_t[:, PC_COND:PC_COND + 16].rearrange("p (c n) -> p c n", n=2), AF.Silu, (pc_b,), (scond_b,))

    dbg_i = [0]

    def dbg_dump():
        if not DEBUG:
            return
        i = dbg_i[0]
        dbg_i[0] += 1
        if i >= NDBG:
            return
        for c in range(DC):
            dma(dbg_d[i, :, c, :], x_t[:, c, :], tuple(xb[c]), ())

    modq_dev = [(0, n) for n in range(6, 18)] + [(l_, n) for l_ in range(1, DEPTH) for n in range(18)]
    mod_done = set()

    def mod_piece(l, n):
        w, wb = ws_next("mod")
        wv = w[:, 0:4096].rearrange("p (k n) -> p k n", n=512)
        for kc in range(DC):
            mm(banks[7][0:2, :], scond_t[:, kc, :], wv[:, kc, :], kc == 0, kc == DC - 1,
               (scond_b, wb), (bank_b[7],), kc == DC - 1)
        act(modT_t[:], banks[7][0:2, :], AF.Copy, (bank_b[7],), (modT_b,))
        for i4 in range(4):
            emit(PE, lambda: nc.tensor.transpose(banks[6][:, 2 * i4:2 * i4 + 2], modT_t[0:2, i4 * P:(i4 + 1) * P],
                                                 ident_t[0:2, 0:2]),
                 (modT_b, ident_b), (bank_b[6],), signal=(i4 == 3))
        fc0 = n * 4
        for cd in range(2):
            tt(MOD[:, l, fc0:fc0 + 4, cd], banks[6][:, 0:8].rearrange("p (f n) -> p f n", n=2)[:, :, cd],
               small_t[:, _SM["b_mod"] + l * 72 + fc0:_SM["b_mod"] + l * 72 + fc0 + 4], ALU.add,
               (bank_b[6], small_b), (mod_b[l],))
        mod_done.add((l, n))

    def mod_step():
        if modq_dev:
            mod_piece(*modq_dev.pop(0))

    def prep_mods(l, k):
        for n in range(6 * (k + 1)):
            assert (l, n) in mod_done, (l, k, n)
        for cd in range(2):
            stt(GS[:, k, :, cd], MOD[:, l, (3 * k + 1) * 8:(3 * k + 2) * 8, cd], 1.0,
                small_t[:, _SM["norm_g"] + (l * 3 + k) * 8:_SM["norm_g"] + (l * 3 + k) * 8 + 8],
                ALU.add, ALU.mult, (mod_b[l], small_b), (gs_b,))
            ts(HG[:, k, :, cd], MOD[:, l, (3 * k + 2) * 8:(3 * k + 3) * 8, cd], 0.5 if k != 1 else 1.0, None,
               ALU.mult, ALU.bypass, (mod_b[l],), (gs_b,))

    nrm_i = [0]

    def rms_stats(ti, nfeat_chunks=DC, src=None, src_bufs=None, inv_n=1.0 / D):
        t0, n = TILES[ti]
        i = nrm_i[0] % 2
        nrm_i[0] += 1
        for c in range(nfeat_chunks):
            s_ap = x_t[:, c, t0:t0 + n] if src is None else src[c]
            s_b = xb[c][ti] if src is None else src_bufs[c]
            act(sq_t[i][:, c, 0:n], s_ap, AF.Square, (s_b,), (sq_b[i],))
        for c in range(nfeat_chunks):
            mm(banks[6][:, 0:n], ones_bf[:], sq_t[i][:, c, 0:n], c == 0, c == nfeat_chunks - 1,
               (sq_b[i], const_b), (bank_b[6],), c == nfeat_chunks - 1)
        act(rs_t[i][:, 0:n], banks[6][:, 0:n], AF.Ln, (bank_b[6], const_b), (rs_b[i],), bias=eps_t[:], scale=inv_n)
        act(rs_t[i][:, 0:n], rs_t[i][:, 0:n], AF.Exp, (rs_b[i],), (rs_b[i],), scale=-0.5)
        return rs_t[i], rs_b[i]

    tmp_i = [0]

    def norm_mod(k, out_fn, only_ti=None):
        l = cur["l"]
        for ti, (t0, n) in enumerate(TILES):
            if only_ti is not None and ti != only_ti:
                continue
            cd = 0 if ti < 2 else 1
            rs, rsb = rms_stats(ti)
            for c in range(DC):
                j = tmp_i[0] % 3
                tmp_i[0] += 1
                tt(tmp_t[j][:, 0:n], x_t[:, c, t0:t0 + n], rs[:, 0:n], ALU.mult, (xb[c][ti], rsb), (tmp_b[j],))
                o_ap, o_b = out_fn(c, ti)
                act(o_ap, tmp_t[j][:, 0:n], AF.Identity, (tmp_b[j], gs_b, mod_b[l]), (o_b,),
                    bias=MOD[:, l, (3 * k) * 8 + c:(3 * k) * 8 + c + 1, cd], scale=GS[:, k, c:c + 1, cd])

    def h_out(c, ti):
        t0, n = TILES[ti]
        return h_t[:, c, t0:t0 + n], hb[c][ti]

    def resid_add(c, ti, ps_ap, ps_b, k, extra_reads=(), gate_ap=None):
        t0, n = TILES[ti]
        cd = 0 if ti < 2 else 1
        g = HG[:, k, c:c + 1, cd] if gate_ap is None else gate_ap(c, cd)
        stt(x_t[:, c, t0:t0 + n], ps_ap, g, x_t[:, c, t0:t0 + n], ALU.mult, ALU.add,
            (ps_b, gs_b, xb[c][ti]) + tuple(extra_reads), (xb[c][ti],))

    cur = {"l": 0}

    def ffn(k, ph):
        l = cur["l"]
        u_t = ph.enter_context(nc.sbuf_tensor(_un("u"), [P, FC, TOK], BF16))
        ub = [[Buf() for _ in range(3)] for _ in range(FC)]
        sa_t = [ph.enter_context(nc.sbuf_tensor(_un("sa"), [P, 512], F32)) for i in range(2)]
        sa_b = [Buf(), Buf()]
        it = 0
        for jp in range(FC // 2):
            w, wb = ws_next("fin")
            wv = w[:, 0:4096].rearrange("p (jj hf kc m) -> p jj hf kc m", jj=2, hf=2, kc=DC)
            for jj in range(2):
                j = 2 * jp + jj
                for ti, (t0, n) in enumerate(TILES):
                    if jp == 0 and jj == 0:
                        norm_mod(k, h_out, only_ti=ti)
                    pa, pb_ = it % 2, 2 + it % 2
                    it += 1
                    for hf, bk in ((0, pa), (1, pb_)):
                        for kc in range(DC):
                            mm(banks[bk][:, 0:n], wv[:, jj, hf, kc, :], h_t[:, kc, t0:t0 + n], kc == 0, kc == DC - 1,
                               (wb, hb[kc][ti]), (bank_b[bk],), kc == DC - 1)
                    si = it % 2
                    act(sa_t[si][:, 0:n], banks[pa][:, 0:n], AF.Silu, (bank_b[pa],), (sa_b[si],))
                    tt(u_t[:, j, t0:t0 + n], sa_t[si][:, 0:n], banks[pb_][:, 0:n], ALU.mult,
                       (sa_b[si], bank_b[pb_]), (ub[j][ti],))
            mod_step()
        for dc in range(DC):
            w, wb = ws_next("fout")
            wv = w[:, 0:FC * P].rearrange("p (j m) -> p j m", m=P)
            for ti, (t0, n) in enumerate(TILES):
                bk = 4 + it % 2
                it += 1
                for j in range(FC):
                    mm(banks[bk][:, 0:n], wv[:, j, :], u_t[:, j, t0:t0 + n], j == 0, j == FC - 1,
                       (wb, ub[j][ti]), (bank_b[bk],), j == FC - 1)
                resid_add(dc, ti, banks[bk][:, 0:n], bank_b[bk], k)
            mod_step()

    def barrier():
        engs = [PE, ACT, DVE, POOLQ, SP]
        allp = engs + slot_eng + chan + gchan
        for q in engs:
            for e in allp:
                if e is q or e.count == 0:
                    continue
                if e.count > q.seen.get(e, 0):
                    q.obj.wait_ge(e.sem, e.count)
                    q.seen[e] = e.count

    def out_proj(y_t, yb, k=1):
        it = 0
        for half in range(2):
            w, wb = ws_next("aout")
            wv = w[:, 0:4096].rearrange("p (dc kc m) -> p dc kc m", dc=4, kc=DC)
            for d4 in range(4):
                dc = half * 4 + d4
                for ti, (t0, n) in enumerate(TILES):
                    bk = 4 + it % 2
                    it += 1
                    for kc in range(DC):
                        mm(banks[bk][:, 0:n], wv[:, d4, kc, :], y_t[:, kc, t0:t0 + n], kc == 0, kc == DC - 1,
                           (wb, yb[kc]), (bank_b[bk],), kc == DC - 1)
                    resid_add(dc, ti, banks[bk][:, 0:n], bank_b[bk], k)

    def rglru(j, ph):
        l = cur["l"]
        sbp = lambda name, shape, dtype=F32: ph.enter_context(nc.sbuf_tensor(_un(name), list(shape), dtype))
        y_t = sbp("y", [P, DC, TOK], BF16); yb = [Buf() for _ in range(DC)]
        gl2_t = [sbp("gl%d" % i_, [P, TOK], BF16) for i_ in range(2)]; gl2_b = [Buf(), Buf()]
        xrp_t = sbp("xrp", [P, NSEG, XRW]); xrp_b = Buf()
        xc_t = sbp("xc", [P, TOK]); xc_b = Buf()
        xcb2_t = [sbp("xcb%d" % i_, [P, TOK], BF16) for i_ in range(2)]; xcb2_b = [Buf(), Buf()]
        r2_t = [sbp("r%d" % i_, [P, TOK]) for i_ in range(2)]; r2_b = [Buf(), Buf()]
        i2_t = [sbp("i%d" % i_, [P, TOK]) for i_ in range(2)]; i2_b = [Buf(), Buf()]
        a2_t = [sbp("a%d" % i_, [P, TOK]) for i_ in range(2)]; a2_b = [Buf(), Buf()]
        hs_t = [sbp("hs%d" % d, [P, TOK]) for d in range(2)]; hs_b = [Buf(), Buf()]
        c8_t = sbp("c8", [P, 2, DC]); c8_b = Buf()
        ini_t = sbp("ini", [P, 16]); ini_b = Buf()
        sf = pc_t[:, PC_SF:PC_SF + 1]
        norm_mod(1, h_out)
        lam_ap = small_t[:, _SM["lam_a"] + j * 16:_SM["lam_a"] + j * 16 + 16].rearrange("p (d c) -> p d c", d=2)
        yy_t = sbp("yy", [P, 2, DC]); pl_t = sbp("pl", [P, 2, DC]); mk_t = sbp("mk", [P, 2, DC])
        act(yy_t[:], lam_ap, AF.Exp, (small_b,), (c8_b,), scale=-1.0)
        act(c8_t[:], yy_t[:], AF.Ln, (c8_b, const_b), (c8_b,), bias=one_t[:])
        ts(pl_t[:], yy_t[:], -0.25, 1.0 / 3.0, ALU.mult, ALU.add, (c8_b,), (c8_b,))
        tt(pl_t[:], pl_t[:], yy_t[:], ALU.mult, (c8_b,), (c8_b,))
        ts(pl_t[:], pl_t[:], -0.5, None, ALU.add, ALU.bypass, (c8_b,), (c8_b,))
        tt(pl_t[:], pl_t[:], yy_t[:], ALU.mult, (c8_b,), (c8_b,))
        ts(pl_t[:], pl_t[:], 1.0, None, ALU.add, ALU.bypass, (c8_b,), (c8_b,))
        tt(pl_t[:], pl_t[:], yy_t[:], ALU.mult, (c8_b,), (c8_b,))
        ts(mk_t[:], yy_t[:], 0.1, None, ALU.is_lt, ALU.bypass, (c8_b,), (c8_b,))
        tt(pl_t[:], pl_t[:], c8_t[:], ALU.subtract, (c8_b,), (c8_b,))
        tt(pl_t[:], pl_t[:], mk_t[:], ALU.mult, (c8_b,), (c8_b,))
        tt(c8_t[:], c8_t[:], pl_t[:], ALU.add, (c8_b,), (c8_b,))
        ts(c8_t[:], c8_t[:], -8.0, None, ALU.mult, ALU.bypass, (c8_b,), (c8_b,))
        emit(DVE, lambda: nc.vector.memset(xrp_t[:], 0.0), (), (xrp_b,))
        itc = [0]
        wts = {}
        xc3 = xc_t[:, :].rearrange("p (s t) -> p s t", t=SEG)

        def F1(c):
            cp, cc = c // 2, c % 2
            if cc == 0:
                w, wb = ws_next("ain")
                wg, wgb = ws_next("agate")
                wts[cp] = (w[:, 0:4096].rearrange("p (cc g kc m) -> p cc g kc m", cc=2, g=2, kc=DC), wb,
                           wg[:, 0:1024].rearrange("p (cc q m) -> p cc q m", cc=2, q=4), wgb)
            wv, wb, wgv, wgb = wts[cp]
            gl_t, gl_b = gl2_t[c % 2], gl2_b[c % 2]
            xcb_t, xcb_b = xcb2_t[c % 2], xcb2_b[c % 2]
            for ti, (t0, n) in enumerate(TILES):
                for g in range(2):
                    bk = (0 if g == 0 else 2) + itc[0] % 2
                    for kc in range(DC):
                        mm(banks[bk][:, 0:n], wv[:, cc, g, kc, :], h_t[:, kc, t0:t0 + n], kc == 0, kc == DC - 1,
                           (wb, hb[kc][ti]), (bank_b[bk],), kc == DC - 1)
                    if g == 0:
                        act(gl_t[:, t0:t0 + n], banks[bk][:, 0:n], AF.Gelu, (bank_b[bk],), (gl_b,))
                    else:
                        ns = n // SEG
                        emit(DVE, lambda: nc.vector.tensor_copy(
                            out=xrp_t[:, 2 * ti:2 * ti + ns, 2:2 + SEG],
                            in_=banks[bk][:, 0:n].rearrange("p (s t) -> p s t", t=SEG)), (bank_b[bk],), (xrp_b,))
                itc[0] += 1
            ts(xrp_t[:, 1:4, 0:2], xrp_t[:, 0:3, SEG:SEG + 2], sf, None, ALU.mult, ALU.bypass, (xrp_b, pc_b), (xrp_b,))
            ts(xrp_t[:, 0:3, SEG + 2:SEG + 3], xrp_t[:, 1:4, 2:3], sf, None, ALU.mult, ALU.bypass, (xrp_b, pc_b), (xrp_b,))
            cw = lambda kk: small_t[:, _SM["conv_w"] + (j * 4 + kk) * 8 + c:_SM["conv_w"] + (j * 4 + kk) * 8 + c + 1]
            cb = small_t[:, _SM["conv_b"] + j * 8 + c:_SM["conv_b"] + j * 8 + c + 1]
            ts(xc3, xrp_t[:, :, 0:SEG], cw(0), cb, ALU.mult, ALU.add, (xrp_b, small_b), (xc_b,))
            for kk in range(1, 4):
                stt(xc3, xrp_t[:, :, kk:kk + SEG], cw(kk), xc3, ALU.mult, ALU.add, (xrp_b, small_b, xc_b), (xc_b,))
            act(xcb_t[:], xc_t[:], AF.Copy, (xc_b,), (xcb_b,))

        def F2(c, d):
            cp, cc = c // 2, c % 2
            wv, wb, wgv, wgb = wts[cp]
            xcb_t, xcb_b = xcb2_t[c % 2], xcb2_b[c % 2]
            r_t, r_b, i_t, i_b, a_t, a_b = r2_t[d], r2_b[d], i2_t[d], i2_b[d], a2_t[d], a2_b[d]
            gba = small_t[:, _SM["gb_a"] + (j * 2 + d) * 8 + c:_SM["gb_a"] + (j * 2 + d) * 8 + c + 1]
            gbx = small_t[:, _SM["gb_x"] + (j * 2 + d) * 8 + c:_SM["gb_x"] + (j * 2 + d) * 8 + c + 1]
            for ti, (t0, n) in enumerate(TILES):
                br, bi = 4 + itc[0] % 2, 6 + itc[0] % 2
                itc[0] += 1
                mm(banks[br][:, 0:n], wgv[:, cc, 2 * d, :], xcb_t[:, t0:t0 + n], True, True,
                   (wgb, xcb_b), (bank_b[br],), True)
                mm(banks[bi][:, 0:n], wgv[:, cc, 2 * d + 1, :], xcb_t[:, t0:t0 + n], True, True,
                   (wgb, xcb_b), (bank_b[bi],), True)
                act(r_t[:, t0:t0 + n], banks[br][:, 0:n], AF.Sigmoid, (bank_b[br], small_b), (r_b,), bias=gba)
                act(i_t[:, t0:t0 + n], banks[bi][:, 0:n], AF.Sigmoid, (bank_b[bi], small_b), (i_b,), bias=gbx)

        def F2b(c):
            for d in range(2):
                r_t, r_b, a_t, a_b = r2_t[d], r2_b[d], a2_t[d], a2_b[d]
                act(a_t[:], r_t[:], AF.Exp, (r_b, c8_b), (a_b,), scale=c8_t[:, d, c:c + 1])
                act(r_t[:], a_t[:], AF.Square, (a_b,), (r_b,))

        def F3(c):
            xcb_t, xcb_b = xcb2_t[c % 2], xcb2_b[c % 2]
            for d in range(2):
                r_t, r_b = r2_t[d], r2_b[d]
                act(r_t[:], r_t[:], AF.Sqrt, (r_b, const_b), (r_b,), bias=one_t[:], scale=-1.0)
            for d in range(2):
                r_t, r_b, i_t, i_b = r2_t[d], r2_b[d], i2_t[d], i2_b[d]
                tt(i_t[:], i_t[:], xcb_t[:], ALU.mult, (i_b, xcb_b), (i_b,))
                tt(i_t[:], i_t[:], r_t[:], ALU.mult, (i_b, r_b), (i_b,))

        def T(c):
            gl_t, gl_b = gl2_t[c % 2], gl2_b[c % 2]
            for d in range(2):
                i_t, i_b, a_t, a_b = i2_t[d], i2_b[d], a2_t[d], a2_b[d]
                hs = hs_t[d]
                h0 = pc_t[:, PC_H0 + (j * 2 + d) * 8 + c:PC_H0 + (j * 2 + d) * 8 + c + 1]
                order = [0, 1, 2, 3, 4] if d == 0 else [3, 2, 1, 0, 4]
                for oi, s in enumerate(order):
                    lo, hi = s * SEG, (s + 1) * SEG
                    if s == 4:
                        init = 0.0
                        rd = ()
                    elif oi == 0:
                        init = h0
                        rd = (pc_b,)
                    else:
                        prev = order[oi - 1]
                        pcol = prev * SEG + (SEG - 1 if d == 0 else 0)
                        ic_ = (itc[0] + oi) % 16
                        ts(ini_t[:, ic_:ic_ + 1], hs[:, pcol:pcol + 1], sf, None, ALU.mult, ALU.bypass,
                           (hs_b[d], pc_b), (ini_b,))
                        init = ini_t[:, ic_:ic_ + 1]
                        rd = (ini_b,)
                    if d == 0:
                        a_ap, b_ap, o_ap = a_t[:, lo:hi], i_t[:, lo:hi], hs[:, lo:hi]
                    else:
                        a_ap, b_ap, o_ap = a_t[:, lo:hi][:, ::-1], i_t[:, lo:hi][:, ::-1], hs[:, lo:hi][:, ::-1]
                    emit(DVE, lambda: nc.vector.tensor_tensor_scan(o_ap, a_ap, b_ap, init, ALU.mult, ALU.add),
                         (a_b, i_b) + rd, (hs_b[d],))
                itc[0] += 5
                col0 = SEG - 1 if d == 0 else 0
                emit(DVE, lambda: nc.vector.tensor_copy(
                    out=fin_t[:, j, :, d, c],
                    in_=hs[:, :].rearrange("p (s t) -> p s t", t=SEG)[:, :, col0]), (hs_b[d],), (fin_b,))
            tt(hs_t[0][:], hs_t[0][:], hs_t[1][:], ALU.add, (hs_b[0], hs_b[1]), (hs_b[0],))
            tt(y_t[:, c, :], hs_t[0][:], gl_t[:], ALU.mult, (hs_b[0], gl_b), (yb[c],))

        F1(0)
        F2(0, 0)
        F2(0, 1)
        F2b(0)
        for c in range(DC):
            if c + 1 < DC:
                F1(c + 1)
            F3(c)
            T(c)
            if c + 1 < DC:
                F2(c + 1, 0)
                F2(c + 1, 1)
                F2b(c + 1)
        out_proj(y_t, yb)

    def attention(j, ph):
        l = cur["l"]
        sbp = lambda name, shape, dtype=F32: ph.enter_context(nc.sbuf_tensor(_un(name), list(shape), dtype))
        NKB = 12
        V_t = sbp("V", [P, NKB, D], BF16); V_b = [Buf() for _ in range(NKB)]
        o_t = sbp("oall", [P, DC, TOK], BF16); o_b = [Buf() for _ in range(DC)]
        q_t = [sbp("q%d" % i, [P, TOK], BF16) for i in range(2)]; q_b = [Buf(), Buf()]
        k_t = [sbp("k%d" % i, [P, TOK + 256], BF16) for i in range(2)]; k_b = [Buf(), Buf()]
        cos_t = sbp("cos", [P, TOK], BF16); sin_t = sbp("sin", [P, TOK], BF16); rope_b = Buf()
        oc_t = [sbp("oc%d" % i_, [P, 512]) for i_ in range(2)]; oc_b = [Buf(), Buf()]
        r1_t, r1_b = tmp_t[0], tmp_b[0]
        r2_t, r2_b = tmp_t[1], tmp_b[1]
        ko_t = [sbp("ko0", [P, TOK])] * 2; ko_b = [Buf()] * 2
        vo_t = [sbp("vo%d" % i, [P, 512]) for i in range(2)]; vo_b = [Buf(), Buf()]
        pT_t = [sbp("pT%d" % i, [P, 512], BF16) for i in range(4)]; pT_b = [Buf() for _ in range(4)]
        of_t, of_b = tmp_t[2], tmp_b[2]
        lam_t = sbp("lamt", [P, 8]); lam_b = Buf()
        norm_mod(1, h_out)
        emit(gchan[1], lambda: nc.gpsimd.dma_start(out=cos_t[:], in_=cos_d), (), (rope_b,), via=POOLQ)
        emit(gchan[2], lambda: nc.gpsimd.dma_start(out=sin_t[:], in_=sin_d), (), (rope_b,), via=POOLQ)
        emit(gchan[0], lambda: nc.gpsimd.dma_start(out=V_t[:, 10:12, :], in_=vc_d), (), (V_b[10], V_b[11]), via=POOLQ)
        lqk_t = sbp("lqk", [P, 256]); lqk_b = Buf()
        dma(lqk_t[:], lqk_d, (), (lqk_b,))
        lq = lqk_t[:, 0:128]
        lk = lqk_t[:, 128:256]
        lp_t = lqk_t[:, 0:128]
        tt(lp_t, lq, lk, ALU.mult, (lqk_b,), (lqk_b, lam_b))
        emit(DVE, lambda: nc.vector.reduce_sum(out=lam_t[:, 0:2], in_=lp_t.rearrange("p (m d) -> p m d", m=2),
                                               axis=mybir.AxisListType.X), (lqk_b, lam_b), (lam_b,))
        act(lam_t[:, 2:4], lam_t[:, 0:2], AF.Exp, (lam_b,), (lam_b,))
        tt(lam_t[:, 4:5], lam_t[:, 3:4], lam_t[:, 2:3], ALU.subtract, (lam_b,), (lam_b,))
        ts(lam_t[:, 4:5], lam_t[:, 4:5], -LAM_INIT, None, ALU.add, ALU.bypass, (lam_b,), (lam_b,))
        ts(lam_t[:, 5:6], small_t[:, _SM["sub_g"]:_SM["sub_g"] + 1], 1.0 - LAM_INIT, None, ALU.mult, ALU.bypass,
           (small_b,), (lam_b,))
        neglam = lam_t[:, 4:5]
        gsub = lam_t[:, 5:6]
        it = 0
        if ATT_LEVEL < 2:
            return
        for half in range(2):
            w, wb = ws_next("wv")
            wv = w[:, 0:4096].rearrange("p (k n) -> p k n", n=512)
            for tb in range(TOK // P):
                ti = min(tb // 4, 2)
                bk = it % 2
                for kc in range(DC):
                    mm(banks[bk][:, :], h_t[:, kc, tb * P:(tb + 1) * P], wv[:, kc, :], kc == 0, kc == DC - 1,
                       (wb, hb[kc][ti]), (bank_b[bk],), kc == DC - 1)
                act(V_t[:, tb, half * 512:(half + 1) * 512], banks[bk][:, :], AF.Copy, (bank_b[bk],), (V_b[tb],))
                vi = it % 2
                emit(DVE, lambda: nc.vector.tensor_copy(out=vo_t[vi][:], in_=banks[bk][:, :]), (bank_b[bk],), (vo_b[vi],))
                if ATT_LEVEL != 21:
                    dma(v_d[tb * P:(tb + 1) * P, half * 512:(half + 1) * 512], vo_t[vi][:], (vo_b[vi],), ())
                it += 1
        pend_fin = []
        fin_bank = [0]
        if ATT_LEVEL < 3 or ATT_LEVEL == 21:
            return
        for hd in range(8):
            w, wb = ws_next("qk")
            wv = w[:, 0:4096].rearrange("p (f kc m) -> p f kc m", f=4, kc=DC)
            hi = hd % 2
            emit(gchan[1 + hi], lambda: nc.gpsimd.dma_start(out=k_t[hi][:, TOK:TOK + 256], in_=kc_d[:, hd, :]),
                 (), (k_b[hi],), via=POOLQ)
            for which in range(2):
                dst = q_t[hi] if which == 0 else k_t[hi]
                dst_b = q_b[hi] if which == 0 else k_b[hi]
                for ti, (t0, n) in enumerate(TILES):
                    b0, b1 = it % 2, 2 + it % 2
                    it += 1
                    for f, bk in ((0, b0), (1, b1)):
                        for kc in range(DC):
                            mm(banks[bk][:, 0:n], wv[:, 2 * which + f, kc, :], h_t[:, kc, t0:t0 + n], kc == 0,
                               kc == DC - 1, (wb, hb[kc][ti]), (bank_b[bk],), kc == DC - 1)
                    if which == 1:
                        act(ko_t[hi][:, t0:t0 + n], banks[b0][:, 0:n], AF.Copy, (bank_b[b0],), (ko_b[hi],))
                    tt(r1_t[:, 0:n], banks[b0][:, 0:n], cos_t[:, t0:t0 + n], ALU.mult, (bank_b[b0], rope_b), (r1_b,))
                    tt(r2_t[:, 0:n], banks[b1][:, 0:n], sin_t[:, t0:t0 + n], ALU.mult, (bank_b[b1], rope_b), (r2_b,))
                    tt(dst[:, t0:t0 + n], r1_t[:, 0:n], r2_t[:, 0:n], ALU.add, (r1_b, r2_b), (dst_b,))
            dma(kT_d[:, hd, :], ko_t[hi][:], (ko_b[hi],), ())
            for ti, (t0, n) in enumerate(TILES if ATT_LEVEL >= 4 else []):
                nq = n // SEG
                inflight = []
                for kbs in range(NKB + 1):
                    if kbs == 5 and pend_fin:
                        fin_bank[0] = (it + 1) % 4
                        pend_fin.pop(0)()
                    if kbs < NKB:
                        kb = kbs
                        kseg = kb // 2 if kb < 10 else 5
                        sbs = [it % 4, (it + 1) % 4]
                        it += 2
                        for m in range(2):
                            mm(banks[sbs[m]][:, 0:n], k_t[hi][m * 64:(m + 1) * 64, kb * P:(kb + 1) * P],
                               q_t[hi][m * 64:(m + 1) * 64, t0:t0 + n], True, True, (k_b[hi], q_b[hi]), (bank_b[sbs[m]],), True)
                        for m in range(2):
                            act(pT_t[sbs[m]][:, 0:n], banks[sbs[m]][:, 0:n], AF.Exp, (bank_b[sbs[m]],), (pT_b[sbs[m]],),
                                scale=0.125)
                            for qs in range(nq):
                                qseg = 2 * ti + qs
                                mcol = PC_M01 + kseg * 5 + qseg
                                ts(pT_t[sbs[m]][:, qs * SEG:(qs + 1) * SEG], pT_t[sbs[m]][:, qs * SEG:(qs + 1) * SEG],
                                   pc_t[:, mcol:mcol + 1], None, ALU.mult, ALU.bypass, (pT_b[sbs[m]], pc_b), (pT_b[sbs[m]],))
                        inflight.append(sbs)
                    if kbs >= 1:
                        kb = kbs - 1
                        for m in range(2):
                            pi = inflight[kb][m]
                            mm(banks[4 + m][:, 0:n], V_t[:, kb, hd * P:(hd + 1) * P], pT_t[pi][:, 0:n], kb == 0, kb == NKB - 1,
                               (V_b[kb], pT_b[pi]), (bank_b[4 + m],), kb == NKB - 1)
                            mm(banks[6 + m][:, 0:n], ones_bf[:], pT_t[pi][:, 0:n], kb == 0, kb == NKB - 1,
                               (const_b, pT_b[pi]), (bank_b[6 + m],), kb == NKB - 1)
                if ATT_LEVEL < 5:
                    continue
                A_t, A_b, B_t, B_b = rs_t[0], rs_b[0], rs_t[1], rs_b[1]
                C_t, C_b, D_t, D_b = oc_t[0], oc_b[0], oc_t[1], oc_b[1]
                act(A_t[:, 0:n], banks[6][:, 0:n], AF.Ln, (bank_b[6],), (A_b,))
                emit(DVE, lambda: nc.vector.tensor_copy(out=C_t[:, 0:n], in_=banks[4][:, 0:n]), (bank_b[4],), (C_b,))
                act(B_t[:, 0:n], banks[7][:, 0:n], AF.Ln, (bank_b[7],), (B_b,))
                emit(DVE, lambda: nc.vector.tensor_copy(out=D_t[:, 0:n], in_=banks[5][:, 0:n]), (bank_b[5],), (D_b,))
                act(A_t[:, 0:n], A_t[:, 0:n], AF.Exp, (A_b,), (A_b,), scale=-1.0)
                act(B_t[:, 0:n], B_t[:, 0:n], AF.Exp, (B_b,), (B_b,), scale=-1.0)
                tt(C_t[:, 0:n], C_t[:, 0:n], A_t[:, 0:n], ALU.mult, (C_b, A_b), (C_b,))
                tt(D_t[:, 0:n], D_t[:, 0:n], B_t[:, 0:n], ALU.mult, (D_b, B_b), (D_b,))
                stt(C_t[:, 0:n], D_t[:, 0:n], neglam, C_t[:, 0:n], ALU.mult, ALU.add, (C_b, D_b, lam_b), (C_b,))
                act(sq_t[0][:, 0, 0:n], C_t[:, 0:n], AF.Square, (C_b,), (sq_b[0],))

                def fin2(hd=hd, t0=t0, n=n):
                    nb_ = fin_bank[0]
                    mm(banks[nb_][:, 0:n], ones_bf[:], sq_t[0][:, 0, 0:n], True, True, (sq_b[0], const_b), (bank_b[nb_],), True)
                    act(A_t[:, 0:n], banks[nb_][:, 0:n], AF.Ln, (bank_b[nb_], const_b), (A_b,), bias=eps_t[:], scale=1.0 / P)
                    act(A_t[:, 0:n], A_t[:, 0:n], AF.Exp, (A_b,), (A_b,), scale=-0.5)
                    stt(o_t[:, hd, t0:t0 + n], C_t[:, 0:n], gsub, A_t[:, 0:n], ALU.mult, ALU.mult,
                        (C_b, lam_b, A_b), (o_b[hd],))
                pend_fin.append(fin2)
        while pend_fin:
            fin_bank[0] = it % 4
            pend_fin.pop(0)()
        if ATT_LEVEL >= 5:
            out_proj(o_t, o_b)

    def pooling(j, ph):
        l = cur["l"]
        sbp = lambda name, shape, dtype=F32: ph.enter_context(nc.sbuf_tensor(_un(name), list(shape), dtype))
        hp_t = sbp("hp", [P, 2, NSEG, HPW]); hp_b = Buf()
        L_t = [sbp("L%d" % i, [P, 2, NSEG, HPW]) for i in range(2)]; L_b = [Buf(), Buf()]
        ic_t = [sbp("ic%d" % i, [P, TOK]) for i in range(2)]; ic_b = [Buf(), Buf()]
        d_t = sbp("dd", [P, 2, TOK], BF16); d_b = Buf()
        df_t = sbp("df", [P, 2, NSEG, SEG]); df_b = Buf()
        gsc_t = sbp("gsc", [P, DC, 2]); gsc_b = Buf()
        rsa_t = sbp("rsa", [P, TOK]); rsa_b = Buf()
        sf = pc_t[:, PC_SF:PC_SF + 1]
        for cd in range(2):
            tt(gsc_t[:, :, cd], HG[:, 1, :, cd], small_t[:, _SM["c_scale"]:_SM["c_scale"] + 8], ALU.mult,
               (gs_b, small_b), (gsc_b,))
        for ti, (t0, n) in enumerate(TILES):
            rs, rsb = rms_stats(ti)
            emit(DVE, lambda: nc.vector.tensor_copy(out=rsa_t[:, t0:t0 + n], in_=rs[:, 0:n]), (rsb,), (rsa_b,))
        w, wb = ws_next("pool")
        wv = w[:, 0:2048].rearrange("p (g ki mo m) -> p g ki mo m", g=4, ki=2, mo=2)
        emit(DVE, lambda: nc.vector.memset(hp_t[:], 0.0), (), (hp_b,))
        it = 0
        for g in range(4):
            dma(ic_t[g % 2][:], icnt_d[:, g, :], (), (ic_b[g % 2],))
            for ci in range(2):
                c = 2 * g + ci
                for ti, (t0, n) in enumerate(TILES):
                    cd = 0 if ti < 2 else 1
                    ns = n // SEG
                    jt = tmp_i[0] % 3
                    tmp_i[0] += 1
                    tt(tmp_t[jt][:, 0:n], x_t[:, c, t0:t0 + n], rsa_t[:, t0:t0 + n], ALU.mult, (xb[c][ti], rsa_b), (tmp_b[jt],))
                    act(hp_t[:, ci, 2 * ti:2 * ti + ns, 8:8 + SEG], tmp_t[jt][:, 0:n].rearrange("p (s t) -> p s t", t=SEG),
                        AF.Identity, (tmp_b[jt], gs_b, mod_b[l]), (hp_b,),
                        bias=MOD[:, l, 3 * 8 + c:3 * 8 + c + 1, cd], scale=GS[:, 1, c:c + 1, cd])
            for ci in range(2):
                ts(hp_t[:, ci, 1:4, 0:8], hp_t[:, ci, 0:3, SEG:SEG + 8], sf, None, ALU.mult, ALU.bypass, (hp_b, pc_b), (hp_b,))
                ts(hp_t[:, ci, 0:3, SEG + 8:SEG + 16], hp_t[:, ci, 1:4, 8:16], sf, None, ALU.mult, ALU.bypass,
                   (hp_b, pc_b), (hp_b,))
            src, src_b = hp_t, hp_b
            lo, hi = 0, HPW
            for lev in range(g + 1):
                dst, dst_b = L_t[lev % 2], L_b[lev % 2]
                sh = 1 if lev == 0 else 2 ** (lev - 1)
                if lev == 0:
                    nlo, nhi = lo + 1, hi
                    for ci in range(2):
                        tt(dst[:, ci, :, nlo:nhi], src[:, ci, :, nlo - 1:nhi - 1], src[:, ci, :, nlo:nhi], ALU.add,
                           (src_b,), (dst_b,))
                else:
                    nlo, nhi = lo + sh, hi - sh
                    for ci in range(2):
                        tt(dst[:, ci, :, nlo:nhi], src[:, ci, :, nlo - sh:nhi - sh], src[:, ci, :, nlo + sh:nhi + sh],
                           ALU.add, (src_b,), (dst_b,))
                lo, hi = nlo, nhi
                src, src_b = dst, dst_b
            assert lo <= 8 and hi >= 8 + SEG
            ic3 = ic_t[g % 2][:, :].rearrange("p (s t) -> p s t", t=SEG)
            for ci in range(2):
                tt(df_t[:, ci], src[:, ci, :, 8:8 + SEG], ic3, ALU.mult, (src_b, ic_b[g % 2]), (df_b,))
                tt(d_t[:, ci, :].rearrange("p (s t) -> p s t", t=SEG), df_t[:, ci], hp_t[:, ci, :, 8:8 + SEG], ALU.subtract,
                   (df_b, hp_b), (d_b,))
            for mo in range(2):
                c = 2 * g + mo
                for ti, (t0, n) in enumerate(TILES):
                    bk = 4 + it % 2
                    it += 1
                    for ki in range(2):
                        mm(banks[bk][:, 0:n], wv[:, g, ki, mo, :], d_t[:, ki, t0:t0 + n], ki == 0, ki == 1,
                           (wb, d_b), (bank_b[bk],), ki == 1)
                    resid_add(c, ti, banks[bk][:, 0:n], bank_b[bk], 1, extra_reads=(gsc_b,),
                              gate_ap=lambda c_, cd_: gsc_t[:, c_, cd_:cd_ + 1])

    for n_ in range(6):
        mod_piece(0, n_)
    nst = 0
    for l in range(DEPTH):
        cur["l"] = l
        if nst >= STAGES:
            break
        prep_mods(l, 0)
        with ExitStack() as ph:
            ffn(0, ph)
            barrier()
        dbg_dump()
        nst += 1
        if nst >= STAGES:
            break
        kind, j = l % 3, l // 3
        prep_mods(l, 1)
        with ExitStack() as ph:
            if kind == 0:
                rglru(j, ph)
            elif kind == 1:
                attention(j, ph)
            else:
                pooling(j, ph)
            barrier()
        dbg_dump()
        nst += 1
        if nst >= STAGES:
            break
        prep_mods(l, 2)
        with ExitStack() as ph:
            ffn(2, ph)
            barrier()
        dbg_dump()
        nst += 1
    assert STAGES < 12 or wst["next"] == n_pieces, (wst["next"], n_pieces)

    with ExitStack() as ph:
        yo_t = [ph.enter_context(nc.sbuf_tensor(_un("yo"), [P, 512], F32)) for i in range(3)]
        yo_b = [Buf() for _ in range(3)]
        oi = 0
        for ti, (t0, n) in enumerate(TILES):
            rs, rsb = rms_stats(ti)
            for c in range(DC):
                k3 = oi % 3
                oi += 1
                stt(yo_t[k3][:, 0:n], x_t[:, c, t0:t0 + n], small_t[:, _SM["fin_g"] + c:_SM["fin_g"] + c + 1], rs[:, 0:n],
                    ALU.mult, ALU.mult, (xb[c][ti], small_b, rsb), (yo_b[k3],))
                dma(yT_d[:, c, t0:t0 + n], yo_t[k3][:, 0:n], (yo_b[k3],), ())
        dma(st_d, fin_t[:].rearrange("p j s d c -> p (j s d c)"), (fin_b,), ())
        for e in chan:
            if e.count > SP.seen.get(e, 0):
                nc.sync.wait_ge(e.sem, e.count)
                SP.seen[e] = e.count
        barrier()
    es.close()
    return nc, wst["issued"]


_CACHE = {}


def kernel(**inputs):
    tags = _piece_tags()
    key = (STAGES, DEBUG)
    if key not in _CACHE:
        n_used = len(tags)
        if STAGES < 12:
            _, n_used = _build(len(tags), sum(e for _, e in tags))
        wtotal = sum(e for _, e in tags[:n_used])
        _CACHE[key] = (_build(n_used, wtotal)[0], n_used)
    nc, n_used = _CACHE[key]
    import time as _t
    _t0 = _t.time()
    in_maps, _, _ = _host_prep(inputs, n_used)
    _t1 = _t.time()
    res = run_bass_kernel_spmd(nc, in_maps, core_ids=list(range(NCORES)))
    if DEBUG:
        print('prep %.1fs run %.1fs' % (_t1 - _t0, _t.time() - _t1))
    outs = res.results
    B, S = 32, 256
    y_prompt = np.zeros((B, S, D), np.float32)
    y_sample = np.zeros((2, 1024, D), np.float32)
    new_state = np.zeros((B, 2, 2, D), np.float32)
    new_k = np.zeros((B, 1, S, 8, 128), np.float32)
    new_v = np.zeros((B, 1, S, 8, 128), np.float32)
    for c in range(NCORES):
        r = outs[c]
        yT = np.asarray(r["yT"]).transpose(1, 0, 2).reshape(D, TOK)
        kT = np.asarray(r["kT"]).transpose(1, 0, 2).reshape(D, TOK)
        vv = np.asarray(r["vout"])
        st = np.asarray(r["stout"]).reshape(P, 2, NSEG, 2, DC)
        y = yT.T
        kk = kT.T
        if c < 6:
            segs = [(s, 5 * c + s) for s in range(5)]
        else:
            segs = [(4, 30 + (c - 6))]
            y_sample[c - 6] = y[0:1024]
        for s, b in segs:
            y_prompt[b] = y[s * S:(s + 1) * S]
            new_k[b, 0] = kk[s * S:(s + 1) * S].reshape(S, 8, 128)
            new_v[b, 0] = vv[s * S:(s + 1) * S].reshape(S, 8, 128)
            new_state[b] = st[:, :, s, :, :].transpose(1, 2, 3, 0).reshape(2, 2, D)
    if DEBUG:
        kernel.dbg = [np.asarray(outs[c]["dbg"]) for c in range(NCORES)]
    return (y_prompt, y_sample, new_state, new_k, new_v)
```

```python
import math
from contextlib import ExitStack

import numpy as np
import concourse.bass as bass
import concourse.mybir as mybir
from concourse.bass_utils import run_bass_kernel_spmd

F32 = mybir.dt.float32
BF16 = mybir.dt.bfloat16
AF = mybir.ActivationFunctionType
ALU = mybir.AluOpType

NCORES = 8
P = 128
D = 1024
DC = 8
TOK = 1280
NSEG = 5
SEG = 256
DFF = 2816
FC = 22
DEPTH = 4
TILES = [(0, 512), (512, 512), (1024, 256)]
EPS = 1e-6
LAM_INIT = 0.8 - 0.6 * math.exp(-0.3 * 1)
NEG = -30000.0
SLOT_E = 4096
NSLOT = 5
LOOKAHEAD = 2
XRW = 259
HPW = 272
POOL_W = (2, 4, 8, 16)

DEBUG = False
SAME_ENG_ALL = True
ATT_LEVEL = 5
NDBG = 12
STAGES = 12


class _Eng:
    def __init__(self, name, obj, sem, step):
        self.name, self.obj, self.sem, self.step = name, obj, sem, step
        self.count = 0
        self.seen = {}
        self.pend_r = []
        self.pend_w = []
        self.nosame = False


class Buf:
    __slots__ = ("name", "lw", "rd", "excl")

    def __init__(self, name="", excl=False):
        self.name = name
        self.lw = None
        self.rd = {}
        self.excl = excl


def emit(eng, fn, reads=(), writes=(), signal=True, via=None):
    q = via if via is not None else eng
    need = {}

    def add(e, v, raw):
        if e is q and (q.nosame or (not raw and not SAME_ENG_ALL)):
            return
        if v > need.get(e, 0):
            need[e] = v

    for b in reads:
        if b.lw is not None:
            add(b.lw[0], b.lw[1], True)
        if b.excl:
            for e, v in b.rd.items():
                if e is not q:
                    add(e, v, False)
    for b in writes:
        if b.lw is not None:
            add(b.lw[0], b.lw[1], False)
        for e, v in b.rd.items():
            add(e, v, False)
    if via is not None and eng.count > 0:
        add(eng, eng.count, True)
    for e, v in need.items():
        if v > q.seen.get(e, 0):
            q.obj.wait_ge(e.sem, v)
            q.seen[e] = v
    ins = fn()
    eng.pend_r.extend(reads)
    eng.pend_w.extend(writes)
    if signal:
        eng.count += eng.step
        ins.then_inc(eng.sem, eng.step)
        c = eng.count
        for b in eng.pend_w:
            b.lw = (eng, c)
            b.rd = {}
        for b in eng.pend_r:
            if b.lw is not None and b.lw[0] is eng and b.lw[1] == c:
                continue
            if c > b.rd.get(eng, 0):
                b.rd[eng] = c
        eng.pend_r = []
        eng.pend_w = []
    return ins


def _fm(v):
    v = np.asarray(v, np.float32)
    lead = v.shape[:-1]
    r = v.reshape(*lead, DC, P)
    r = np.moveaxis(r, -1, 0)
    return np.ascontiguousarray(r)


def _rope_partner():
    p = np.arange(P)
    within = p % 64
    half = (within % 32) // 16
    partner = np.where(half == 0, p + 16, p - 16)
    sign = np.where(half == 0, -1.0, 1.0).astype(np.float32)
    axis = within // 32
    f = within % 16
    return partner, sign, axis, f


def _sched():
    out = []
    modq = [(0, n) for n in range(6, 18)] + [(l, n) for l in range(1, DEPTH) for n in range(18)]
    for n in range(6):
        out.append(("mod", 4096, (0, n)))

    def ffn(l, i):
        for jp in range(FC // 2):
            out.append(("fin", 4096, (l, i, jp)))
            if modq:
                out.append(("mod", 4096, modq.pop(0)))
        for dc in range(DC):
            out.append(("fout", FC * P, (l, i, dc)))
            if modq:
                out.append(("mod", 4096, modq.pop(0)))

    for l in range(DEPTH):
        ffn(l, 0)
        kind, j = l % 3, l // 3
        if kind == 0:
            for cp in range(4):
                out.append(("ain", 4096, (j, cp)))
                out.append(("agate", 1024, (j, cp)))
            for half in range(2):
                out.append(("aout", 4096, ("a", j, half)))
        elif kind == 1:
            for half in range(2):
                out.append(("wv", 4096, (j, half)))
            for hd in range(8):
                out.append(("qk", 4096, (j, hd)))
            for half in range(2):
                out.append(("aout", 4096, ("b", j, half)))
        else:
            out.append(("pool", 2048, (j,)))
        ffn(l, 1)
    assert not modq
    return out


def _piece_array(inp, tag, key):
    if tag == "mod":
        l, n = key
        w = inp["w_mod"][l]
        return w[:, n * 512:(n + 1) * 512].reshape(DC, P, 512).transpose(1, 0, 2).reshape(P, -1)
    if tag == "fin":
        l, i, jp = key
        wi = inp["w_ffn_in"][l, i].reshape(DC, P, 2, FC, P)
        return wi[:, :, :, 2 * jp:2 * jp + 2, :].transpose(1, 3, 2, 0, 4).reshape(P, -1)
    if tag == "fout":
        l, i, dc = key
        wo = inp["w_ffn_out"][l, i].reshape(FC, P, DC, P)
        return wo[:, :, dc, :].transpose(1, 0, 2).reshape(P, -1)
    if tag == "ain":
        j, cp = key
        w_in = inp["a_w_in"][j].reshape(DC, P, 2, DC, P)
        return w_in[:, :, :, 2 * cp:2 * cp + 2, :].transpose(1, 3, 2, 0, 4).reshape(P, -1)
    if tag == "agate":
        j, cp = key
        gws = [inp["a_gate_w_a"][j, 0], inp["a_gate_w_x"][j, 0], inp["a_gate_w_a"][j, 1], inp["a_gate_w_x"][j, 1]]
        g = np.zeros((P, 2, 4, P), np.float32)
        for cc in range(2):
            c = 2 * cp + cc
            for qi in range(4):
                for hb in range(2):
                    g[hb * 64:(hb + 1) * 64, cc, qi, hb * 64:(hb + 1) * 64] = gws[qi][2 * c + hb]
        return g.reshape(P, -1)
    if tag == "aout":
        which, j, half = key
        w = inp["a_w_out"][j] if which == "a" else inp["b_w_o"][j]
        wo = w.reshape(DC, P, DC, P)
        return wo[:, :, 4 * half:4 * half + 4, :].transpose(1, 2, 0, 3).reshape(P, -1)
    if tag == "wv":
        j, half = key
        wv = inp["b_w_qkv"][j][:, 2048:3072]
        return wv[:, half * 512:(half + 1) * 512].reshape(DC, P, 512).transpose(1, 0, 2).reshape(P, -1)
    if tag == "qk":
        j, hd = key
        w = inp["b_w_qkv"][j]
        partner, _, _, _ = _rope_partner()
        q = w[:, 0:1024].reshape(DC, P, 8, P)[:, :, hd, :]
        k = w[:, 1024:2048].reshape(DC, P, 8, P)[:, :, hd, :]
        blk = np.stack([q, q[:, :, partner], k, k[:, :, partner]], 0)
        return blk.transpose(2, 0, 1, 3).reshape(P, -1)
    if tag == "pool":
        (j,) = key
        wp = inp["c_w_pool"][j].reshape(4, 2, P, 2, P)
        return wp.transpose(2, 0, 1, 3, 4).reshape(P, -1)
    raise KeyError(tag)


def _weight_plan(inp, n_used):
    return [(t, _piece_array(inp, t, key)) for t, e, key in _sched()[:n_used]]


def _piece_tags():
    return [(t, e) for t, e, _ in _sched()]


def _core_tokens(inp, c):
    xp, xs = inp["x_prompt"], inp["x_sample"]
    if c < 6:
        x = xp[5 * c:5 * c + 5].reshape(TOK, D)
        prompts = [5 * c + s for s in range(5)]
        sample = None
    else:
        b = c - 6
        x = np.concatenate([xs[b], xp[30 + b]], 0)
        prompts = [None] * 4 + [30 + b]
        sample = b
    return x, prompts, sample


def _host_prep(inp, n_used):
    inp = {k: np.asarray(v) for k, v in inp.items()}
    plan = _weight_plan(inp, n_used)
    tags = _piece_tags()[:n_used]
    assert len(plan) == len(tags)
    for (t0, a), (t1, e) in zip(plan, tags):
        assert t0 == t1 and a.shape == (P, e), (t0, t1, a.shape, e)
    wstream = np.ascontiguousarray(np.concatenate([a for _, a in plan], axis=1), dtype=np.float32)

    partner, sign, axis, f = _rope_partner()
    inv = (10000.0 ** (-np.arange(16, dtype=np.float32) / 16)).astype(np.float32)
    t = np.arange(1024)
    pos = np.stack([t // 64, t % 64], 0).astype(np.float32)
    ang = pos[axis][:, :] * inv[f][:, None]
    ang = ang.astype(np.float32)
    cos_s = np.cos(ang).astype(np.float32)
    sin_s = (np.sin(ang).astype(np.float32) * sign[:, None]).astype(np.float32)

    sm = {
        "norm_g": _fm(inp["norm_g"]).reshape(P, -1),
        "b_mod": _fm(inp["b_mod"].reshape(DEPTH, 9, D)).reshape(P, -1),
        "conv_w": _fm(inp["a_conv_w"]).reshape(P, -1),
        "conv_b": _fm(inp["a_conv_b"]).reshape(P, -1),
        "gb_a": _fm(inp["a_gate_b_a"]).reshape(P, -1),
        "gb_x": _fm(inp["a_gate_b_x"]).reshape(P, -1),
        "lam_a": _fm(inp["a_lambda"]).reshape(P, -1),
        "c_scale": _fm(inp["c_scale"]).reshape(P, -1),
        "fin_g": _fm(inp["final_norm_g"]).reshape(P, -1),
    }
    sub_g = np.asarray(inp["b_subln_g"][0], np.float32).reshape(P, 1)
    lqk = np.concatenate([np.asarray(inp["b_lam_q"][0], np.float32).reshape(1, 128),
                          np.asarray(inp["b_lam_k"][0], np.float32).reshape(1, 128)], axis=1)
    lqk = np.ascontiguousarray(np.broadcast_to(lqk, (P, 256)))
    small = np.concatenate([sm["norm_g"], sm["b_mod"], sm["conv_w"], sm["conv_b"], sm["gb_a"],
                            sm["gb_x"], sm["lam_a"], sm["c_scale"], sm["fin_g"], sub_g], axis=1)
    ident = np.eye(P, dtype=np.float32)

    in_maps = []
    for c in range(NCORES):
        x, prompts, sample = _core_tokens(inp, c)
        xT = np.ascontiguousarray(x.T.reshape(DC, P, TOK).transpose(1, 0, 2))
        if sample is None:
            cond = np.stack([inp["c_ctx"], inp["c_ctx"]], 0)
            h0 = np.zeros((P, 2 * 2 * DC), np.float32)
            sf = 0.0
            kc = np.zeros((P, 8, 256), np.float32)
            vc = np.zeros((P, 2, D), np.float32)
            cos_t = np.ones((P, TOK), np.float32)
            sin_t = np.zeros((P, TOK), np.float32)
        else:
            cond = np.stack([inp["c"][sample], inp["c_ctx"]], 0)
            h0 = _fm(inp["state_rglru"][sample]).reshape(P, -1)
            sf = 1.0
            ck = inp["cache_k_diff"][sample, 0]
            kc = np.ascontiguousarray(ck.transpose(2, 1, 0))
            cv = inp["cache_v_diff"][sample, 0].reshape(256, D)
            vc = np.ascontiguousarray(cv.reshape(2, P, D).transpose(1, 0, 2))
            cos_t = np.concatenate([cos_s, np.ones((P, 256), np.float32)], 1)
            sin_t = np.concatenate([sin_s, np.zeros((P, 256), np.float32)], 1)
        condT = np.ascontiguousarray(cond.T.reshape(DC, P, 2).transpose(1, 0, 2)).reshape(P, -1)
        mb = np.full((6, 5), NEG, np.float32)
        for qs in range(5):
            for ks in range(6):
                if sample is None:
                    ok = (ks == qs)
                else:
                    ok = (qs < 4 and (ks < 4 or ks == 5)) or (qs == 4 and ks == 4)
                if ok:
                    mb[ks, qs] = 0.0
        mbias = np.broadcast_to(mb.reshape(1, 30), (P, 30)).astype(np.float32)
        m01 = (mbias == 0.0).astype(np.float32)
        ic = np.zeros((4, TOK), np.float32)
        for g, win in enumerate(POOL_W):
            for s in range(NSEG):
                if sample is not None and s < 4:
                    T, tt = 1024, s * 256 + np.arange(256)
                else:
                    T, tt = 256, np.arange(256)
                lo = np.clip(tt - win // 2, 0, T)
                hi = np.clip(tt + win // 2, 0, T)
                ic[g, s * 256:(s + 1) * 256] = 1.0 / (hi - lo).astype(np.float32)
        icnt = np.ascontiguousarray(np.broadcast_to(ic[None], (P, 4, TOK))).astype(np.float32)
        flags = np.full((P, 1), sf, np.float32)
        percore = np.concatenate([condT, h0, flags, mbias, m01], axis=1).astype(np.float32)
        in_maps.append({
            "xT": xT, "wstream": wstream, "small": small, "ident": ident, "percore": percore, "lqk": lqk,
            "kcache": kc, "vcache": vc, "cos_t": cos_t, "sin_t": sin_t, "icnt": icnt,
        })
    return in_maps, len(tags), wstream.shape[1]


_SM = {}
_o = 0
for _n, _w in [("norm_g", 96), ("b_mod", 288), ("conv_w", 64), ("conv_b", 16), ("gb_a", 32), ("gb_x", 32),
               ("lam_a", 32), ("c_scale", 8), ("fin_g", 8), ("sub_g", 1)]:
    _SM[_n] = _o
    _o += _w
SMALL_W = _o
PC_COND, PC_H0, PC_SF, PC_MB, PC_M01 = 0, 16, 48, 49, 79
PERCORE_W = 109


def _build(n_pieces, wtotal):
    tags = _piece_tags()[:n_pieces]
    nc = bass.Bass("TRN2", target_bir_lowering=False)
    dt = lambda name, shape, kind="ExternalInput": nc.dram_tensor(name, list(shape), F32, kind=kind).ap()
    xT_d = dt("xT", [P, DC, TOK])
    ws_d = dt("wstream", [P, wtotal])
    small_d = dt("small", [P, SMALL_W])
    ident_d = dt("ident", [P, P])
    lqk_d = dt("lqk", [P, 256])
    pc_d = dt("percore", [P, PERCORE_W])
    kc_d = dt("kcache", [P, 8, 256])
    vc_d = dt("vcache", [P, 2, D])
    cos_d = dt("cos_t", [P, TOK])
    sin_d = dt("sin_t", [P, TOK])
    icnt_d = dt("icnt", [P, 4, TOK])
    yT_d = dt("yT", [P, DC, TOK], "ExternalOutput")
    kT_d = dt("kT", [P, DC, TOK], "ExternalOutput")
    v_d = dt("vout", [TOK, D], "ExternalOutput")
    st_d = dt("stout", [P, 2 * NSEG * 2 * DC], "ExternalOutput")
    if DEBUG:
        dbg_d = dt("dbg", [NDBG, P, DC, TOK], "ExternalOutput")

    es = ExitStack()
    es.enter_context(nc.allow_low_precision("bf16 matmul operands, fp32 accumulation"))
    _uid = [0]

    def _un(name):
        _uid[0] += 1
        return "t%d_%s" % (_uid[0], name)

    sb = lambda name, shape, dtype=F32: es.enter_context(nc.sbuf_tensor(_un(name), list(shape), dtype))

    def mk_eng(name, obj, step):
        return _Eng(name, obj, es.enter_context(nc.semaphore("s_" + name)), step)

    PE = mk_eng("pe", nc.tensor, 1)
    PE.nosame = True
    ACT = mk_eng("act", nc.scalar, 1)
    DVE = mk_eng("dve", nc.vector, 1)
    POOLQ = mk_eng("pool", nc.gpsimd, 1)
    SP = mk_eng("sp", nc.sync, 1)
    slot_eng = [mk_eng("ws%d" % i, None, 16) for i in range(NSLOT)]
    NCH = 12
    chan = [mk_eng("ch%d" % i, None, 16) for i in range(NCH)]
    chan_i = [0]
    gchan = [mk_eng("gch%d" % i, None, 16) for i in range(3)]

    def dma(out_ap, in_ap, reads=(), writes=(), q=None):
        q = q or SP
        ch = chan[chan_i[0] % NCH]
        chan_i[0] += 1
        return emit(ch, lambda: q.obj.dma_start(out=out_ap, in_=in_ap), reads, writes, via=q), ch

    def act(out, in_, func, reads, writes, bias=None, scale=1.0):
        kw = {}
        if bias is not None:
            kw["bias"] = bias
        return emit(ACT, lambda: nc.scalar.activation(out=out, in_=in_, func=func, scale=scale, **kw), reads, writes)

    def tt(out, a, b, op, reads, writes):
        return emit(DVE, lambda: nc.vector.tensor_tensor(out=out, in0=a, in1=b, op=op), reads, writes)

    def ptt(out, a, b, op, reads, writes):
        return emit(POOLQ, lambda: nc.gpsimd.tensor_tensor(out=out, in0=a, in1=b, op=op), reads, writes)

    def ts(out, a, s1, s2, op0, op1, reads, writes):
        return emit(DVE, lambda: nc.vector.tensor_scalar(out=out, in0=a, scalar1=s1, scalar2=s2, op0=op0, op1=op1),
                    reads, writes)

    def stt(out, a, s, b, op0, op1, reads, writes):
        return emit(DVE, lambda: nc.vector.scalar_tensor_tensor(out=out, in0=a, scalar=s, in1=b, op0=op0, op1=op1),
                    reads, writes)

    def mm(out, lhsT, rhs, start, stop, reads, writes, signal):
        return emit(PE, lambda: nc.tensor.matmul(out, lhsT, rhs, start=start, stop=stop), reads, writes, signal=signal)

    x_t = sb("x", [P, DC, TOK])
    xb = [[Buf("x%d_%d" % (c, i)) for i in range(3)] for c in range(DC)]
    h_t = sb("h", [P, DC, TOK], BF16)
    hb = [[Buf("h%d_%d" % (c, i)) for i in range(3)] for c in range(DC)]
    slots = [sb("wslot%d" % i, [P, SLOT_E], BF16) for i in range(NSLOT)]
    slot_b = [Buf("slot%d" % i) for i in range(NSLOT)]
    small_t = sb("small", [P, SMALL_W]); small_b = Buf("small")
    pc_t = sb("percore", [P, PERCORE_W]); pc_b = Buf("pc")
    ident_t = sb("ident", [P, P]); ident_b = Buf("ident")
    ones_bf = sb("ones_bf", [P, P], BF16)
    eps_t = sb("eps", [P, 1])
    one_t = sb("one", [P, 1])
    const_b = Buf("const")
    scond_t = sb("scond", [P, DC, 2], BF16); scond_b = Buf("scond")
    MOD = sb("mod", [P, DEPTH, 72, 2]); mod_b = [Buf("mod%d" % l) for l in range(DEPTH)]
    GS = sb("gs", [P, 3, DC, 2]); HG = sb("hg", [P, 3, DC, 2]); gs_b = Buf("gs")
    fin_t = sb("fin", [P, 2, NSEG, 2, DC]); fin_b = Buf("fin")
    rs_t = [sb("rs%d" % i, [P, 512]) for i in range(2)]; rs_b = [Buf("rs0"), Buf("rs1")]
    sq_t = [sb("sq0", [P, DC, 512], BF16)] * 2; sq_b = [Buf("sq0")] * 2
    tmp_t = [sb("tmp%d" % i, [P, 512]) for i in range(3)]; tmp_b = [Buf("tmp%d" % i) for i in range(3)]
    modT_t = sb("modT", [2, 512]); modT_b = Buf("modT")
    dmasem_b = Buf("dmasem")

    banks = [es.enter_context(nc.psum_tensor("bank%d" % i, [P, 512], F32)) for i in range(8)]
    bank_b = [Buf("bank%d" % i, excl=True) for i in range(8)]

    sm = lambda name, off=0, w=1: small_t[:, _SM[name] + off:_SM[name] + off + w]

    wst = {"issued": 0, "next": 0, "off": 0}
    offs = []
    o = 0
    for tg, e in tags:
        offs.append(o)
        o += e
    assert o == wtotal

    WS_LIMIT = [10 ** 9]

    def ws_issue_upto(k):
        while wst["issued"] <= min(k, n_pieces - 1) and wst["issued"] < WS_LIMIT[0]:
            i = wst["issued"]
            tg, e = tags[i]
            s = i % NSLOT
            emit(slot_eng[s], lambda: nc.gpsimd.dma_start(out=slots[s][:, 0:e], in_=ws_d[:, offs[i]:offs[i] + e]),
                 (), (slot_b[s],), via=POOLQ)
            wst["issued"] += 1

    def ws_next(tag):
        i = wst["next"]
        assert tags[i][0] == tag, (i, tags[i], tag)
        ws_issue_upto(i + LOOKAHEAD)
        wst["next"] += 1
        s = i % NSLOT
        return slots[s], slot_b[s]

    emit(DVE, lambda: nc.vector.memset(ones_bf[:], 1.0), (), (const_b,))
    emit(DVE, lambda: nc.vector.memset(eps_t[:], EPS), (), (const_b,))
    emit(DVE, lambda: nc.vector.memset(one_t[:], 1.0), (), (const_b,))
    emit(DVE, lambda: nc.vector.memset(fin_t[:], 0.0), (), (fin_b,))
    dma(small_t[:], small_d, (), (small_b,))
    dma(pc_t[:], pc_d, (), (pc_b,))
    dma(ident_t[:], ident_d, (), (ident_b,))
    ws_issue_upto(LOOKAHEAD)
    for c in range(DC):
        dma(x_t[:, c, :], xT_d[:, c, :], (), tuple(xb[c]))
    act(scond_t[:], pc_t[:, PC_COND:PC_COND + 16].rearrange("p (c n) -> p c n", n=2), AF.Silu, (pc_b,), (scond_b,))

    dbg_i = [0]

    def dbg_dump():
        if not DEBUG:
            return
        i = dbg_i[0]
        dbg_i[0] += 1
        if i >= NDBG:
            return
        for c in range(DC):
            dma(dbg_d[i, :, c, :], x_t[:, c, :], tuple(xb[c]), ())

    modq_dev = [(0, n) for n in range(6, 18)] + [(l_, n) for l_ in range(1, DEPTH) for n in range(18)]
    mod_done = set()

    def mod_piece(l, n):
        w, wb = ws_next("mod")
        wv = w[:, 0:4096].rearrange("p (k n) -> p k n", n=512)
        for kc in range(DC):
            mm(banks[7][0:2, :], scond_t[:, kc, :], wv[:, kc, :], kc == 0, kc == DC - 1,
               (scond_b, wb), (bank_b[7],), kc == DC - 1)
        act(modT_t[:], banks[7][0:2, :], AF.Copy, (bank_b[7],), (modT_b,))
        for i4 in range(4):
            emit(PE, lambda: nc.tensor.transpose(banks[6][:, 2 * i4:2 * i4 + 2], modT_t[0:2, i4 * P:(i4 + 1) * P],
                                                 ident_t[0:2, 0:2]),
                 (modT_b, ident_b), (bank_b[6],), signal=(i4 == 3))
        fc0 = n * 4
        for cd in range(2):
            tt(MOD[:, l, fc0:fc0 + 4, cd], banks[6][:, 0:8].rearrange("p (f n) -> p f n", n=2)[:, :, cd],
               small_t[:, _SM["b_mod"] + l * 72 + fc0:_SM["b_mod"] + l * 72 + fc0 + 4], ALU.add,
               (bank_b[6], small_b), (mod_b[l],))
        mod_done.add((l, n))

    def mod_step():
        if modq_dev:
            mod_piece(*modq_dev.pop(0))

    def prep_mods(l, k):
        for n in range(6 * (k + 1)):
            assert (l, n) in mod_done, (l, k, n)
        for cd in range(2):
            stt(GS[:, k, :, cd], MOD[:, l, (3 * k + 1) * 8:(3 * k + 2) * 8, cd], 1.0,
                small_t[:, _SM["norm_g"] + (l * 3 + k) * 8:_SM["norm_g"] + (l * 3 + k) * 8 + 8],
                ALU.add, ALU.mult, (mod_b[l], small_b), (gs_b,))
            ts(HG[:, k, :, cd], MOD[:, l, (3 * k + 2) * 8:(3 * k + 3) * 8, cd], 0.5 if k != 1 else 1.0, None,
               ALU.mult, ALU.bypass, (mod_b[l],), (gs_b,))

    nrm_i = [0]

    def rms_stats(ti, nfeat_chunks=DC, src=None, src_bufs=None, inv_n=1.0 / D):
        t0, n = TILES[ti]
        i = nrm_i[0] % 2
        nrm_i[0] += 1
        for c in range(nfeat_chunks):
            s_ap = x_t[:, c, t0:t0 + n] if src is None else src[c]
            s_b = xb[c][ti] if src is None else src_bufs[c]
            act(sq_t[i][:, c, 0:n], s_ap, AF.Square, (s_b,), (sq_b[i],))
        for c in range(nfeat_chunks):
            mm(banks[6][:, 0:n], ones_bf[:], sq_t[i][:, c, 0:n], c == 0, c == nfeat_chunks - 1,
               (sq_b[i], const_b), (bank_b[6],), c == nfeat_chunks - 1)
        act(rs_t[i][:, 0:n], banks[6][:, 0:n], AF.Ln, (bank_b[6], const_b), (rs_b[i],), bias=eps_t[:], scale=inv_n)
        act(rs_t[i][:, 0:n], rs_t[i][:, 0:n], AF.Exp, (rs_b[i],), (rs_b[i],), scale=-0.5)
        return rs_t[i], rs_b[i]

    tmp_i = [0]

    def norm_mod(k, out_fn, only_ti=None):
        l = cur["l"]
        for ti, (t0, n) in enumerate(TILES):
            if only_ti is not None and ti != only_ti:
                continue
            cd = 0 if ti < 2 else 1
            rs, rsb = rms_stats(ti)
            for c in range(DC):
                j = tmp_i[0] % 3
                tmp_i[0] += 1
                tt(tmp_t[j][:, 0:n], x_t[:, c, t0:t0 + n], rs[:, 0:n], ALU.mult, (xb[c][ti], rsb), (tmp_b[j],))
                o_ap, o_b = out_fn(c, ti)
                act(o_ap, tmp_t[j][:, 0:n], AF.Identity, (tmp_b[j], gs_b, mod_b[l]), (o_b,),
                    bias=MOD[:, l, (3 * k) * 8 + c:(3 * k) * 8 + c + 1, cd], scale=GS[:, k, c:c + 1, cd])

    def h_out(c, ti):
        t0, n = TILES[ti]
        return h_t[:, c, t0:t0 + n], hb[c][ti]

    def resid_add(c, ti, ps_ap, ps_b, k, extra_reads=(), gate_ap=None):
        t0, n = TILES[ti]
        cd = 0 if ti < 2 else 1
        g = HG[:, k, c:c + 1, cd] if gate_ap is None else gate_ap(c, cd)
        stt(x_t[:, c, t0:t0 + n], ps_ap, g, x_t[:, c, t0:t0 + n], ALU.mult, ALU.add,
            (ps_b, gs_b, xb[c][ti]) + tuple(extra_reads), (xb[c][ti],))

    cur = {"l": 0}

    def ffn(k, ph):
        l = cur["l"]
        u_t = ph.enter_context(nc.sbuf_tensor(_un("u"), [P, FC, TOK], BF16))
        ub = [[Buf() for _ in range(3)] for _ in range(FC)]
        sa_t = [ph.enter_context(nc.sbuf_tensor(_un("sa"), [P, 512], F32)) for i in range(2)]
        sa_b = [Buf(), Buf()]
        it = 0
        for jp in range(FC // 2):
            w, wb = ws_next("fin")
            wv = w[:, 0:4096].rearrange("p (jj hf kc m) -> p jj hf kc m", jj=2, hf=2, kc=DC)
            for jj in range(2):
                j = 2 * jp + jj
                for ti, (t0, n) in enumerate(TILES):
                    if jp == 0 and jj == 0:
                        norm_mod(k, h_out, only_ti=ti)
                    pa, pb_ = it % 2, 2 + it % 2
                    it += 1
                    for hf, bk in ((0, pa), (1, pb_)):
                        for kc in range(DC):
                            mm(banks[bk][:, 0:n], wv[:, jj, hf, kc, :], h_t[:, kc, t0:t0 + n], kc == 0, kc == DC - 1,
                               (wb, hb[kc][ti]), (bank_b[bk],), kc == DC - 1)
                    si = it % 2
                    act(sa_t[si][:, 0:n], banks[pa][:, 0:n], AF.Silu, (bank_b[pa],), (sa_b[si],))
                    tt(u_t[:, j, t0:t0 + n], sa_t[si][:, 0:n], banks[pb_][:, 0:n], ALU.mult,
                       (sa_b[si], bank_b[pb_]), (ub[j][ti],))
            mod_step()
        for dc in range(DC):
            w, wb = ws_next("fout")
            wv = w[:, 0:FC * P].rearrange("p (j m) -> p j m", m=P)
            for ti, (t0, n) in enumerate(TILES):
                bk = 4 + it % 2
                it += 1
                for j in range(FC):
                    mm(banks[bk][:, 0:n], wv[:, j, :], u_t[:, j, t0:t0 + n], j == 0, j == FC - 1,
                       (wb, ub[j][ti]), (bank_b[bk],), j == FC - 1)
                resid_add(dc, ti, banks[bk][:, 0:n], bank_b[bk], k)
            mod_step()

    def barrier():
        engs = [PE, ACT, DVE, POOLQ, SP]
        allp = engs + slot_eng + chan + gchan
        for q in engs:
            for e in allp:
                if e is q or e.count == 0:
                    continue
                if e.count > q.seen.get(e, 0):
                    q.obj.wait_ge(e.sem, e.count)
                    q.seen[e] = e.count

    def out_proj(y_t, yb, k=1):
        it = 0
        for half in range(2):
            w, wb = ws_next("aout")
            wv = w[:, 0:4096].rearrange("p (dc kc m) -> p dc kc m", dc=4, kc=DC)
            for d4 in range(4):
                dc = half * 4 + d4
                for ti, (t0, n) in enumerate(TILES):
                    bk = 4 + it % 2
                    it += 1
                    for kc in range(DC):
                        mm(banks[bk][:, 0:n], wv[:, d4, kc, :], y_t[:, kc, t0:t0 + n], kc == 0, kc == DC - 1,
                           (wb, yb[kc]), (bank_b[bk],), kc == DC - 1)
                    resid_add(dc, ti, banks[bk][:, 0:n], bank_b[bk], k)

    def rglru(j, ph):
        l = cur["l"]
        sbp = lambda name, shape, dtype=F32: ph.enter_context(nc.sbuf_tensor(_un(name), list(shape), dtype))
        y_t = sbp("y", [P, DC, TOK], BF16); yb = [Buf() for _ in range(DC)]
        gl2_t = [sbp("gl%d" % i_, [P, TOK], BF16) for i_ in range(2)]; gl2_b = [Buf(), Buf()]
        xrp_t = sbp("xrp", [P, NSEG, XRW]); xrp_b = Buf()
        xc_t = sbp("xc", [P, TOK]); xc_b = Buf()
        xcb2_t = [sbp("xcb%d" % i_, [P, TOK], BF16) for i_ in range(2)]; xcb2_b = [Buf(), Buf()]
        r2_t = [sbp("r%d" % i_, [P, TOK]) for i_ in range(2)]; r2_b = [Buf(), Buf()]
        i2_t = [sbp("i%d" % i_, [P, TOK]) for i_ in range(2)]; i2_b = [Buf(), Buf()]
        a2_t = [sbp("a%d" % i_, [P, TOK]) for i_ in range(2)]; a2_b = [Buf(), Buf()]
        hs_t = [sbp("hs%d" % d, [P, TOK]) for d in range(2)]; hs_b = [Buf(), Buf()]
        c8_t = sbp("c8", [P, 2, DC]); c8_b = Buf()
        ini_t = sbp("ini", [P, 16]); ini_b = Buf()
        sf = pc_t[:, PC_SF:PC_SF + 1]
        norm_mod(1, h_out)
        lam_ap = small_t[:, _SM["lam_a"] + j * 16:_SM["lam_a"] + j * 16 + 16].rearrange("p (d c) -> p d c", d=2)
        yy_t = sbp("yy", [P, 2, DC]); pl_t = sbp("pl", [P, 2, DC]); mk_t = sbp("mk", [P, 2, DC])
        act(yy_t[:], lam_ap, AF.Exp, (small_b,), (c8_b,), scale=-1.0)
        act(c8_t[:], yy_t[:], AF.Ln, (c8_b, const_b), (c8_b,), bias=one_t[:])
        ts(pl_t[:], yy_t[:], -0.25, 1.0 / 3.0, ALU.mult, ALU.add, (c8_b,), (c8_b,))
        tt(pl_t[:], pl_t[:], yy_t[:], ALU.mult, (c8_b,), (c8_b,))
        ts(pl_t[:], pl_t[:], -0.5, None, ALU.add, ALU.bypass, (c8_b,), (c8_b,))
        tt(pl_t[:], pl_t[:], yy_t[:], ALU.mult, (c8_b,), (c8_b,))
        ts(pl_t[:], pl_t[:], 1.0, None, ALU.add, ALU.bypass, (c8_b,), (c8_b,))
        tt(pl_t[:], pl_t[:], yy_t[:], ALU.mult, (c8_b,), (c8_b,))
        ts(mk_t[:], yy_t[:], 0.1, None, ALU.is_lt, ALU.bypass, (c8_b,), (c8_b,))
        tt(pl_t[:], pl_t[:], c8_t[:], ALU.subtract, (c8_b,), (c8_b,))
        tt(pl_t[:], pl_t[:], mk_t[:], ALU.mult, (c8_b,), (c8_b,))
        tt(c8_t[:], c8_t[:], pl_t[:], ALU.add, (c8_b,), (c8_b,))
        ts(c8_t[:], c8_t[:], -8.0, None, ALU.mult, ALU.bypass, (c8_b,), (c8_b,))
        emit(DVE, lambda: nc.vector.memset(xrp_t[:], 0.0), (), (xrp_b,))
        itc = [0]
        wts = {}
        xc3 = xc_t[:, :].rearrange("p (s t) -> p s t", t=SEG)

        def F1(c):
            cp, cc = c // 2, c % 2
            if cc == 0:
                w, wb = ws_next("ain")
                wg, wgb = ws_next("agate")
                wts[cp] = (w[:, 0:4096].rearrange("p (cc g kc m) -> p cc g kc m", cc=2, g=2, kc=DC), wb,
                           wg[:, 0:1024].rearrange("p (cc q m) -> p cc q m", cc=2, q=4), wgb)
            wv, wb, wgv, wgb = wts[cp]
            gl_t, gl_b = gl2_t[c % 2], gl2_b[c % 2]
            xcb_t, xcb_b = xcb2_t[c % 2], xcb2_b[c % 2]
            for ti, (t0, n) in enumerate(TILES):
                for g in range(2):
                    bk = (0 if g == 0 else 2) + itc[0] % 2
                    for kc in range(DC):
                        mm(banks[bk][:, 0:n], wv[:, cc, g, kc, :], h_t[:, kc, t0:t0 + n], kc == 0, kc == DC - 1,
                           (wb, hb[kc][ti]), (bank_b[bk],), kc == DC - 1)
                    if g == 0:
                        act(gl_t[:, t0:t0 + n], banks[bk][:, 0:n], AF.Gelu, (bank_b[bk],), (gl_b,))
                    else:
                        ns = n // SEG
                        emit(DVE, lambda: nc.vector.tensor_copy(
                            out=xrp_t[:, 2 * ti:2 * ti + ns, 2:2 + SEG],
                            in_=banks[bk][:, 0:n].rearrange("p (s t) -> p s t", t=SEG)), (bank_b[bk],), (xrp_b,))
                itc[0] += 1
            ts(xrp_t[:, 1:4, 0:2], xrp_t[:, 0:3, SEG:SEG + 2], sf, None, ALU.mult, ALU.bypass, (xrp_b, pc_b), (xrp_b,))
            ts(xrp_t[:, 0:3, SEG + 2:SEG + 3], xrp_t[:, 1:4, 2:3], sf, None, ALU.mult, ALU.bypass, (xrp_b, pc_b), (xrp_b,))
            cw = lambda kk: small_t[:, _SM["conv_w"] + (j * 4 + kk) * 8 + c:_SM["conv_w"] + (j * 4 + kk) * 8 + c + 1]
            cb = small_t[:, _SM["conv_b"] + j * 8 + c:_SM["conv_b"] + j * 8 + c + 1]
            ts(xc3, xrp_t[:, :, 0:SEG], cw(0), cb, ALU.mult, ALU.add, (xrp_b, small_b), (xc_b,))
            for kk in range(1, 4):
                stt(xc3, xrp_t[:, :, kk:kk + SEG], cw(kk), xc3, ALU.mult, ALU.add, (xrp_b, small_b, xc_b), (xc_b,))
            act(xcb_t[:], xc_t[:], AF.Copy, (xc_b,), (xcb_b,))

        def F2(c, d):
            cp, cc = c // 2, c % 2
            wv, wb, wgv, wgb = wts[cp]
            xcb_t, xcb_b = xcb2_t[c % 2], xcb2_b[c % 2]
            r_t, r_b, i_t, i_b, a_t, a_b = r2_t[d], r2_b[d], i2_t[d], i2_b[d], a2_t[d], a2_b[d]
            gba = small_t[:, _SM["gb_a"] + (j * 2 + d) * 8 + c:_SM["gb_a"] + (j * 2 + d) * 8 + c + 1]
            gbx = small_t[:, _SM["gb_x"] + (j * 2 + d) * 8 + c:_SM["gb_x"] + (j * 2 + d) * 8 + c + 1]
            for ti, (t0, n) in enumerate(TILES):
                br, bi = 4 + itc[0] % 2, 6 + itc[0] % 2
                itc[0] += 1
                mm(banks[br][:, 0:n], wgv[:, cc, 2 * d, :], xcb_t[:, t0:t0 + n], True, True,
                   (wgb, xcb_b), (bank_b[br],), True)
                mm(banks[bi][:, 0:n], wgv[:, cc, 2 * d + 1, :], xcb_t[:, t0:t0 + n], True, True,
                   (wgb, xcb_b), (bank_b[bi],), True)
                act(r_t[:, t0:t0 + n], banks[br][:, 0:n], AF.Sigmoid, (bank_b[br], small_b), (r_b,), bias=gba)
                act(i_t[:, t0:t0 + n], banks[bi][:, 0:n], AF.Sigmoid, (bank_b[bi], small_b), (i_b,), bias=gbx)

        def F2b(c):
            for d in range(2):
                r_t, r_b, a_t, a_b = r2_t[d], r2_b[d], a2_t[d], a2_b[d]
                act(a_t[:], r_t[:], AF.Exp, (r_b, c8_b), (a_b,), scale=c8_t[:, d, c:c + 1])
                act(r_t[:], a_t[:], AF.Square, (a_b,), (r_b,))

        def F3(c):
            xcb_t, xcb_b = xcb2_t[c % 2], xcb2_b[c % 2]
            for d in range(2):
                r_t, r_b = r2_t[d], r2_b[d]
                act(r_t[:], r_t[:], AF.Sqrt, (r_b, const_b), (r_b,), bias=one_t[:], scale=-1.0)
            for d in range(2):
                r_t, r_b, i_t, i_b = r2_t[d], r2_b[d], i2_t[d], i2_b[d]
                tt(i_t[:], i_t[:], xcb_t[:], ALU.mult, (i_b, xcb_b), (i_b,))
                tt(i_t[:], i_t[:], r_t[:], ALU.mult, (i_b, r_b), (i_b,))

        def T(c):
            gl_t, gl_b = gl2_t[c % 2], gl2_b[c % 2]
            for d in range(2):
                i_t, i_b, a_t, a_b = i2_t[d], i2_b[d], a2_t[d], a2_b[d]
                hs = hs_t[d]
                h0 = pc_t[:, PC_H0 + (j * 2 + d) * 8 + c:PC_H0 + (j * 2 + d) * 8 + c + 1]
                order = [0, 1, 2, 3, 4] if d == 0 else [3, 2, 1, 0, 4]
                for oi, s in enumerate(order):
                    lo, hi = s * SEG, (s + 1) * SEG
                    if s == 4:
                        init = 0.0
                        rd = ()
                    elif oi == 0:
                        init = h0
                        rd = (pc_b,)
                    else:
                        prev = order[oi - 1]
                        pcol = prev * SEG + (SEG - 1 if d == 0 else 0)
                        ic_ = (itc[0] + oi) % 16
                        ts(ini_t[:, ic_:ic_ + 1], hs[:, pcol:pcol + 1], sf, None, ALU.mult, ALU.bypass,
                           (hs_b[d], pc_b), (ini_b,))
                        init = ini_t[:, ic_:ic_ + 1]
                        rd = (ini_b,)
                    if d == 0:
                        a_ap, b_ap, o_ap = a_t[:, lo:hi], i_t[:, lo:hi], hs[:, lo:hi]
                    else:
                        a_ap, b_ap, o_ap = a_t[:, lo:hi][:, ::-1], i_t[:, lo:hi][:, ::-1], hs[:, lo:hi][:, ::-1]
                    emit(DVE, lambda: nc.vector.tensor_tensor_scan(o_ap, a_ap, b_ap, init, ALU.mult, ALU.add),
                         (a_b, i_b) + rd, (hs_b[d],))
                itc[0] += 5
                col0 = SEG - 1 if d == 0 else 0
                emit(DVE, lambda: nc.vector.tensor_copy(
                    out=fin_t[:, j, :, d, c],
                    in_=hs[:, :].rearrange("p (s t) -> p s t", t=SEG)[:, :, col0]), (hs_b[d],), (fin_b,))
            tt(hs_t[0][:], hs_t[0][:], hs_t[1][:], ALU.add, (hs_b[0], hs_b[1]), (hs_b[0],))
            tt(y_t[:, c, :], hs_t[0][:], gl_t[:], ALU.mult, (hs_b[0], gl_b), (yb[c],))

        F1(0)
        F2(0, 0)
        F2(0, 1)
        F2b(0)
        for c in range(DC):
            if c + 1 < DC:
                F1(c + 1)
            F3(c)
            T(c)
            if c + 1 < DC:
                F2(c + 1, 0)
                F2(c + 1, 1)
                F2b(c + 1)
        out_proj(y_t, yb)

    def attention(j, ph):
        l = cur["l"]
        sbp = lambda name, shape, dtype=F32: ph.enter_context(nc.sbuf_tensor(_un(name), list(shape), dtype))
        NKB = 12
        V_t = sbp("V", [P, NKB, D], BF16); V_b = [Buf() for _ in range(NKB)]
        o_t = sbp("oall", [P, DC, TOK], BF16); o_b = [Buf() for _ in range(DC)]
        q_t = [sbp("q%d" % i, [P, TOK], BF16) for i in range(2)]; q_b = [Buf(), Buf()]
        k_t = [sbp("k%d" % i, [P, TOK + 256], BF16) for i in range(2)]; k_b = [Buf(), Buf()]
        cos_t = sbp("cos", [P, TOK], BF16); sin_t = sbp("sin", [P, TOK], BF16); rope_b = Buf()
        oc_t = [sbp("oc%d" % i_, [P, 512]) for i_ in range(2)]; oc_b = [Buf(), Buf()]
        r1_t, r1_b = tmp_t[0], tmp_b[0]
        r2_t, r2_b = tmp_t[1], tmp_b[1]
        ko_t = [sbp("ko0", [P, TOK])] * 2; ko_b = [Buf()] * 2
        vo_t = [sbp("vo%d" % i, [P, 512]) for i in range(2)]; vo_b = [Buf(), Buf()]
        pT_t = [sbp("pT%d" % i, [P, 512], BF16) for i in range(4)]; pT_b = [Buf() for _ in range(4)]
        of_t, of_b = tmp_t[2], tmp_b[2]
        lam_t = sbp("lamt", [P, 8]); lam_b = Buf()
        norm_mod(1, h_out)
        emit(gchan[1], lambda: nc.gpsimd.dma_start(out=cos_t[:], in_=cos_d), (), (rope_b,), via=POOLQ)
        emit(gchan[2], lambda: nc.gpsimd.dma_start(out=sin_t[:], in_=sin_d), (), (rope_b,), via=POOLQ)
        emit(gchan[0], lambda: nc.gpsimd.dma_start(out=V_t[:, 10:12, :], in_=vc_d), (), (V_b[10], V_b[11]), via=POOLQ)
        lqk_t = sbp("lqk", [P, 256]); lqk_b = Buf()
        dma(lqk_t[:], lqk_d, (), (lqk_b,))
        lq = lqk_t[:, 0:128]
        lk = lqk_t[:, 128:256]
        lp_t = lqk_t[:, 0:128]
        tt(lp_t, lq, lk, ALU.mult, (lqk_b,), (lqk_b, lam_b))
        emit(DVE, lambda: nc.vector.reduce_sum(out=lam_t[:, 0:2], in_=lp_t.rearrange("p (m d) -> p m d", m=2),
                                               axis=mybir.AxisListType.X), (lqk_b, lam_b), (lam_b,))
        act(lam_t[:, 2:4], lam_t[:, 0:2], AF.Exp, (lam_b,), (lam_b,))
        tt(lam_t[:, 4:5], lam_t[:, 3:4], lam_t[:, 2:3], ALU.subtract, (lam_b,), (lam_b,))
        ts(lam_t[:, 4:5], lam_t[:, 4:5], -LAM_INIT, None, ALU.add, ALU.bypass, (lam_b,), (lam_b,))
        ts(lam_t[:, 5:6], small_t[:, _SM["sub_g"]:_SM["sub_g"] + 1], 1.0 - LAM_INIT, None, ALU.mult, ALU.bypass,
           (small_b,), (lam_b,))
        neglam = lam_t[:, 4:5]
        gsub = lam_t[:, 5:6]
        it = 0
        if ATT_LEVEL < 2:
            return
        for half in range(2):
            w, wb = ws_next("wv")
            wv = w[:, 0:4096].rearrange("p (k n) -> p k n", n=512)
            for tb in range(TOK // P):
                ti = min(tb // 4, 2)
                bk = it % 2
                for kc in range(DC):
                    mm(banks[bk][:, :], h_t[:, kc, tb * P:(tb + 1) * P], wv[:, kc, :], kc == 0, kc == DC - 1,
                       (wb, hb[kc][ti]), (bank_b[bk],), kc == DC - 1)
                act(V_t[:, tb, half * 512:(half + 1) * 512], banks[bk][:, :], AF.Copy, (bank_b[bk],), (V_b[tb],))
                vi = it % 2
                emit(DVE, lambda: nc.vector.tensor_copy(out=vo_t[vi][:], in_=banks[bk][:, :]), (bank_b[bk],), (vo_b[vi],))
                if ATT_LEVEL != 21:
                    dma(v_d[tb * P:(tb + 1) * P, half * 512:(half + 1) * 512], vo_t[vi][:], (vo_b[vi],), ())
                it += 1
        pend_fin = []
        fin_bank = [0]
        if ATT_LEVEL < 3 or ATT_LEVEL == 21:
            return
        for hd in range(8):
            w, wb = ws_next("qk")
            wv = w[:, 0:4096].rearrange("p (f kc m) -> p f kc m", f=4, kc=DC)
            hi = hd % 2
            emit(gchan[1 + hi], lambda: nc.gpsimd.dma_start(out=k_t[hi][:, TOK:TOK + 256], in_=kc_d[:, hd, :]),
                 (), (k_b[hi],), via=POOLQ)
            for which in range(2):
                dst = q_t[hi] if which == 0 else k_t[hi]
                dst_b = q_b[hi] if which == 0 else k_b[hi]
                for ti, (t0, n) in enumerate(TILES):
                    b0, b1 = it % 2, 2 + it % 2
                    it += 1
                    for f, bk in ((0, b0), (1, b1)):
                        for kc in range(DC):
                            mm(banks[bk][:, 0:n], wv[:, 2 * which + f, kc, :], h_t[:, kc, t0:t0 + n], kc == 0,
                               kc == DC - 1, (wb, hb[kc][ti]), (bank_b[bk],), kc == DC - 1)
                    if which == 1:
                        act(ko_t[hi][:, t0:t0 + n], banks[b0][:, 0:n], AF.Copy, (bank_b[b0],), (ko_b[hi],))
                    tt(r1_t[:, 0:n], banks[b0][:, 0:n], cos_t[:, t0:t0 + n], ALU.mult, (bank_b[b0], rope_b), (r1_b,))
                    tt(r2_t[:, 0:n], banks[b1][:, 0:n], sin_t[:, t0:t0 + n], ALU.mult, (bank_b[b1], rope_b), (r2_b,))
                    tt(dst[:, t0:t0 + n], r1_t[:, 0:n], r2_t[:, 0:n], ALU.add, (r1_b, r2_b), (dst_b,))
            dma(kT_d[:, hd, :], ko_t[hi][:], (ko_b[hi],), ())
            for ti, (t0, n) in enumerate(TILES if ATT_LEVEL >= 4 else []):
                nq = n // SEG
                inflight = []
                for kbs in range(NKB + 1):
                    if kbs in (2, 5, 8) and pend_fin:
                        fin_bank[0] = (it + 1) % 4
                        pend_fin.pop(0)()
                    if kbs < NKB:
                        kb = kbs
                        kseg = kb // 2 if kb < 10 else 5
                        sbs = [it % 4, (it + 1) % 4]
                        it += 2
                        for m in range(2):
                            mm(banks[sbs[m]][:, 0:n], k_t[hi][m * 64:(m + 1) * 64, kb * P:(kb + 1) * P],
                               q_t[hi][m * 64:(m + 1) * 64, t0:t0 + n], True, True, (k_b[hi], q_b[hi]), (bank_b[sbs[m]],), True)
                        for m in range(2):
                            act(pT_t[sbs[m]][:, 0:n], banks[sbs[m]][:, 0:n], AF.Exp, (bank_b[sbs[m]],), (pT_b[sbs[m]],),
                                scale=0.125)
                            for qs in range(nq):
                                qseg = 2 * ti + qs
                                mcol = PC_M01 + kseg * 5 + qseg
                                ts(pT_t[sbs[m]][:, qs * SEG:(qs + 1) * SEG], pT_t[sbs[m]][:, qs * SEG:(qs + 1) * SEG],
                                   pc_t[:, mcol:mcol + 1], None, ALU.mult, ALU.bypass, (pT_b[sbs[m]], pc_b), (pT_b[sbs[m]],))
                        inflight.append(sbs)
                    if kbs >= 1:
                        kb = kbs - 1
                        for m in range(2):
                            pi = inflight[kb][m]
                            mm(banks[4 + m][:, 0:n], V_t[:, kb, hd * P:(hd + 1) * P], pT_t[pi][:, 0:n], kb == 0, kb == NKB - 1,
                               (V_b[kb], pT_b[pi]), (bank_b[4 + m],), kb == NKB - 1)
                            mm(banks[6 + m][:, 0:n], ones_bf[:], pT_t[pi][:, 0:n], kb == 0, kb == NKB - 1,
                               (const_b, pT_b[pi]), (bank_b[6 + m],), kb == NKB - 1)
                if ATT_LEVEL < 5:
                    continue
                A_t, A_b, B_t, B_b = rs_t[0], rs_b[0], rs_t[1], rs_b[1]
                C_t, C_b, D_t, D_b = oc_t[0], oc_b[0], oc_t[1], oc_b[1]
                act(A_t[:, 0:n], banks[6][:, 0:n], AF.Ln, (bank_b[6],), (A_b,))
                emit(DVE, lambda: nc.vector.tensor_copy(out=C_t[:, 0:n], in_=banks[4][:, 0:n]), (bank_b[4],), (C_b,))
                act(B_t[:, 0:n], banks[7][:, 0:n], AF.Ln, (bank_b[7],), (B_b,))
                emit(DVE, lambda: nc.vector.tensor_copy(out=D_t[:, 0:n], in_=banks[5][:, 0:n]), (bank_b[5],), (D_b,))

                def fin_a(n=n):
                    act(A_t[:, 0:n], A_t[:, 0:n], AF.Exp, (A_b,), (A_b,), scale=-1.0)
                    act(B_t[:, 0:n], B_t[:, 0:n], AF.Exp, (B_b,), (B_b,), scale=-1.0)
                    tt(C_t[:, 0:n], C_t[:, 0:n], A_t[:, 0:n], ALU.mult, (C_b, A_b), (C_b,))
                    tt(D_t[:, 0:n], D_t[:, 0:n], B_t[:, 0:n], ALU.mult, (D_b, B_b), (D_b,))
                    stt(C_t[:, 0:n], D_t[:, 0:n], neglam, C_t[:, 0:n], ALU.mult, ALU.add, (C_b, D_b, lam_b), (C_b,))

                def fin_b(n=n):
                    nb_ = fin_bank[0]
                    act(sq_t[0][:, 0, 0:n], C_t[:, 0:n], AF.Square, (C_b,), (sq_b[0],))
                    mm(banks[nb_][:, 0:n], ones_bf[:], sq_t[0][:, 0, 0:n], True, True, (sq_b[0], const_b), (bank_b[nb_],), True)
                    act(A_t[:, 0:n], banks[nb_][:, 0:n], AF.Ln, (bank_b[nb_], const_b), (A_b,), bias=eps_t[:], scale=1.0 / P)

                def fin_c(hd=hd, t0=t0, n=n):
                    act(A_t[:, 0:n], A_t[:, 0:n], AF.Exp, (A_b,), (A_b,), scale=-0.5)
                    stt(o_t[:, hd, t0:t0 + n], C_t[:, 0:n], gsub, A_t[:, 0:n], ALU.mult, ALU.mult,
                        (C_b, lam_b, A_b), (o_b[hd],))
                pend_fin.extend([fin_a, fin_b, fin_c])
        while pend_fin:
            fin_bank[0] = it % 4
            pend_fin.pop(0)()
        if ATT_LEVEL >= 5:
            out_proj(o_t, o_b)

    def pooling(j, ph):
        l = cur["l"]
        sbp = lambda name, shape, dtype=F32: ph.enter_context(nc.sbuf_tensor(_un(name), list(shape), dtype))
        hp_t = sbp("hp", [P, 2, NSEG, HPW]); hp_b = Buf()
        L_t = [sbp("L%d" % i, [P, 2, NSEG, HPW]) for i in range(2)]; L_b = [Buf(), Buf()]
        ic_t = [sbp("ic%d" % i, [P, TOK]) for i in range(2)]; ic_b = [Buf(), Buf()]
        d_t = sbp("dd", [P, 2, TOK], BF16); d_b = Buf()
        df_t = sbp("df", [P, 2, NSEG, SEG]); df_b = Buf()
        gsc_t = sbp("gsc", [P, DC, 2]); gsc_b = Buf()
        rsa_t = sbp("rsa", [P, TOK]); rsa_b = Buf()
        sf = pc_t[:, PC_SF:PC_SF + 1]
        for cd in range(2):
            tt(gsc_t[:, :, cd], HG[:, 1, :, cd], small_t[:, _SM["c_scale"]:_SM["c_scale"] + 8], ALU.mult,
               (gs_b, small_b), (gsc_b,))
        for ti, (t0, n) in enumerate(TILES):
            rs, rsb = rms_stats(ti)
            emit(DVE, lambda: nc.vector.tensor_copy(out=rsa_t[:, t0:t0 + n], in_=rs[:, 0:n]), (rsb,), (rsa_b,))
        w, wb = ws_next("pool")
        wv = w[:, 0:2048].rearrange("p (g ki mo m) -> p g ki mo m", g=4, ki=2, mo=2)
        emit(DVE, lambda: nc.vector.memset(hp_t[:], 0.0), (), (hp_b,))
        it = 0
        for g in range(4):
            dma(ic_t[g % 2][:], icnt_d[:, g, :], (), (ic_b[g % 2],))
            for ci in range(2):
                c = 2 * g + ci
                for ti, (t0, n) in enumerate(TILES):
                    cd = 0 if ti < 2 else 1
                    ns = n // SEG
                    jt = tmp_i[0] % 3
                    tmp_i[0] += 1
                    tt(tmp_t[jt][:, 0:n], x_t[:, c, t0:t0 + n], rsa_t[:, t0:t0 + n], ALU.mult, (xb[c][ti], rsa_b), (tmp_b[jt],))
                    act(hp_t[:, ci, 2 * ti:2 * ti + ns, 8:8 + SEG], tmp_t[jt][:, 0:n].rearrange("p (s t) -> p s t", t=SEG),
                        AF.Identity, (tmp_b[jt], gs_b, mod_b[l]), (hp_b,),
                        bias=MOD[:, l, 3 * 8 + c:3 * 8 + c + 1, cd], scale=GS[:, 1, c:c + 1, cd])
            for ci in range(2):
                ts(hp_t[:, ci, 1:4, 0:8], hp_t[:, ci, 0:3, SEG:SEG + 8], sf, None, ALU.mult, ALU.bypass, (hp_b, pc_b), (hp_b,))
                ts(hp_t[:, ci, 0:3, SEG + 8:SEG + 16], hp_t[:, ci, 1:4, 8:16], sf, None, ALU.mult, ALU.bypass,
                   (hp_b, pc_b), (hp_b,))
            src, src_b = hp_t, hp_b
            lo, hi = 0, HPW
            for lev in range(g + 1):
                dst, dst_b = L_t[lev % 2], L_b[lev % 2]
                sh = 1 if lev == 0 else 2 ** (lev - 1)
                if lev == 0:
                    nlo, nhi = lo + 1, hi
                    for ci in range(2):
                        tt(dst[:, ci, :, nlo:nhi], src[:, ci, :, nlo - 1:nhi - 1], src[:, ci, :, nlo:nhi], ALU.add,
                           (src_b,), (dst_b,))
                else:
                    nlo, nhi = lo + sh, hi - sh
                    for ci in range(2):
                        tt(dst[:, ci, :, nlo:nhi], src[:, ci, :, nlo - sh:nhi - sh], src[:, ci, :, nlo + sh:nhi + sh],
                           ALU.add, (src_b,), (dst_b,))
                lo, hi = nlo, nhi
                src, src_b = dst, dst_b
            assert lo <= 8 and hi >= 8 + SEG
            ic3 = ic_t[g % 2][:, :].rearrange("p (s t) -> p s t", t=SEG)
            for ci in range(2):
                tt(df_t[:, ci], src[:, ci, :, 8:8 + SEG], ic3, ALU.mult, (src_b, ic_b[g % 2]), (df_b,))
                tt(d_t[:, ci, :].rearrange("p (s t) -> p s t", t=SEG), df_t[:, ci], hp_t[:, ci, :, 8:8 + SEG], ALU.subtract,
                   (df_b, hp_b), (d_b,))
            for mo in range(2):
                c = 2 * g + mo
                for ti, (t0, n) in enumerate(TILES):
                    bk = 4 + it % 2
                    it += 1
                    for ki in range(2):
                        mm(banks[bk][:, 0:n], wv[:, g, ki, mo, :], d_t[:, ki, t0:t0 + n], ki == 0, ki == 1,
                           (wb, d_b), (bank_b[bk],), ki == 1)
                    resid_add(c, ti, banks[bk][:, 0:n], bank_b[bk], 1, extra_reads=(gsc_b,),
                              gate_ap=lambda c_, cd_: gsc_t[:, c_, cd_:cd_ + 1])

    for n_ in range(6):
        mod_piece(0, n_)
    nst = 0
    for l in range(DEPTH):
        cur["l"] = l
        if nst >= STAGES:
            break
        prep_mods(l, 0)
        with ExitStack() as ph:
            ffn(0, ph)
            barrier()
        dbg_dump()
        nst += 1
        if nst >= STAGES:
            break
        kind, j = l % 3, l // 3
        prep_mods(l, 1)
        with ExitStack() as ph:
            if kind == 0:
                rglru(j, ph)
            elif kind == 1:
                attention(j, ph)
            else:
                pooling(j, ph)
            barrier()
        dbg_dump()
        nst += 1
        if nst >= STAGES:
            break
        prep_mods(l, 2)
        with ExitStack() as ph:
            ffn(2, ph)
            barrier()
        dbg_dump()
        nst += 1
    assert STAGES < 12 or wst["next"] == n_pieces, (wst["next"], n_pieces)

    with ExitStack() as ph:
        yo_t = [ph.enter_context(nc.sbuf_tensor(_un("yo"), [P, 512], F32)) for i in range(3)]
        yo_b = [Buf() for _ in range(3)]
        oi = 0
        for ti, (t0, n) in enumerate(TILES):
            rs, rsb = rms_stats(ti)
            for c in range(DC):
                k3 = oi % 3
                oi += 1
                stt(yo_t[k3][:, 0:n], x_t[:, c, t0:t0 + n], small_t[:, _SM["fin_g"] + c:_SM["fin_g"] + c + 1], rs[:, 0:n],
                    ALU.mult, ALU.mult, (xb[c][ti], small_b, rsb), (yo_b[k3],))
                dma(yT_d[:, c, t0:t0 + n], yo_t[k3][:, 0:n], (yo_b[k3],), ())
        dma(st_d, fin_t[:].rearrange("p j s d c -> p (j s d c)"), (fin_b,), ())
        for e in chan:
            if e.count > SP.seen.get(e, 0):
                nc.sync.wait_ge(e.sem, e.count)
                SP.seen[e] = e.count
        barrier()
    es.close()
    return nc, wst["issued"]


_CACHE = {}


def kernel(**inputs):
    tags = _piece_tags()
    key = (STAGES, DEBUG)
    if key not in _CACHE:
        n_used = len(tags)
        if STAGES < 12:
            _, n_used = _build(len(tags), sum(e for _, e in tags))
        wtotal = sum(e for _, e in tags[:n_used])
        _CACHE[key] = (_build(n_used, wtotal)[0], n_used)
    nc, n_used = _CACHE[key]
    import time as _t
    _t0 = _t.time()
    in_maps, _, _ = _host_prep(inputs, n_used)
    _t1 = _t.time()
    res = run_bass_kernel_spmd(nc, in_maps, core_ids=list(range(NCORES)))
    if DEBUG:
        print('prep %.1fs run %.1fs' % (_t1 - _t0, _t.time() - _t1))
    outs = res.results
    B, S = 32, 256
    y_prompt = np.zeros((B, S, D), np.float32)
    y_sample = np.zeros((2, 1024, D), np.float32)
    new_state = np.zeros((B, 2, 2, D), np.float32)
    new_k = np.zeros((B, 1, S, 8, 128), np.float32)
    new_v = np.zeros((B, 1, S, 8, 128), np.float32)
    for c in range(NCORES):
        r = outs[c]
        yT = np.asarray(r["yT"]).transpose(1, 0, 2).reshape(D, TOK)
        kT = np.asarray(r["kT"]).transpose(1, 0, 2).reshape(D, TOK)
        vv = np.asarray(r["vout"])
        st = np.asarray(r["stout"]).reshape(P, 2, NSEG, 2, DC)
        y = yT.T
        kk = kT.T
        if c < 6:
            segs = [(s, 5 * c + s) for s in range(5)]
        else:
            segs = [(4, 30 + (c - 6))]
            y_sample[c - 6] = y[0:1024]
        for s, b in segs:
            y_prompt[b] = y[s * S:(s + 1) * S]
            new_k[b, 0] = kk[s * S:(s + 1) * S].reshape(S, 8, 128)
            new_v[b, 0] = vv[s * S:(s + 1) * S].reshape(S, 8, 128)
            new_state[b] = st[:, :, s, :, :].transpose(1, 2, 3, 0).reshape(2, 2, D)
    if DEBUG:
        kernel.dbg = [np.asarray(outs[c]["dbg"]) for c in range(NCORES)]
    return (y_prompt, y_sample, new_state, new_k, new_v)
```

```python
import math
from contextlib import ExitStack

import numpy as np
import concourse.bass as bass
import concourse.mybir as mybir
from concourse.bass_utils import run_bass_kernel_spmd

F32 = mybir.dt.float32
BF16 = mybir.dt.bfloat16
AF = mybir.ActivationFunctionType
ALU = mybir.AluOpType

NCORES = 8
P = 128
D = 1024
DC = 8
TOK = 1280
NSEG = 5
SEG = 256
DFF = 2816
FC = 22
DEPTH = 4
TILES = [(0, 512), (512, 512), (1024, 256)]
EPS = 1e-6
LAM_INIT = 0.8 - 0.6 * math.exp(-0.3 * 1)
NEG = -30000.0
SLOT_E = 4096
NSLOT = 5
LOOKAHEAD = 2
XRW = 259
HPW = 272
POOL_W = (2, 4, 8, 16)

DEBUG = False
SAME_ENG_ALL = True
ATT_LEVEL = 5
NDBG = 12
STAGES = 12


class _Eng:
    def __init__(self, name, obj, sem, step):
        self.name, self.obj, self.sem, self.step = name, obj, sem, step
        self.count = 0
        self.seen = {}
        self.pend_r = []
        self.pend_w = []
        self.nosame = False


class Buf:
    __slots__ = ("name", "lw", "rd", "excl")

    def __init__(self, name="", excl=False):
        self.name = name
        self.lw = None
        self.rd = {}
        self.excl = excl


def emit(eng, fn, reads=(), writes=(), signal=True, via=None):
    q = via if via is not None else eng
    need = {}

    def add(e, v, raw):
        if e is q and (q.nosame or (not raw and not SAME_ENG_ALL)):
            return
        if v > need.get(e, 0):
            need[e] = v

    for b in reads:
        if b.lw is not None:
            add(b.lw[0], b.lw[1], True)
        if b.excl:
            for e, v in b.rd.items():
                if e is not q:
                    add(e, v, False)
    for b in writes:
        if b.lw is not None:
            add(b.lw[0], b.lw[1], False)
        for e, v in b.rd.items():
            add(e, v, False)
    if via is not None and eng.count > 0:
        add(eng, eng.count, True)
    for e, v in need.items():
        if v > q.seen.get(e, 0):
            q.obj.wait_ge(e.sem, v)
            q.seen[e] = v
    ins = fn()
    eng.pend_r.extend(reads)
    eng.pend_w.extend(writes)
    if signal:
        eng.count += eng.step
        ins.then_inc(eng.sem, eng.step)
        c = eng.count
        for b in eng.pend_w:
            b.lw = (eng, c)
            b.rd = {}
        for b in eng.pend_r:
            if b.lw is not None and b.lw[0] is eng and b.lw[1] == c:
                continue
            if c > b.rd.get(eng, 0):
                b.rd[eng] = c
        eng.pend_r = []
        eng.pend_w = []
    return ins


def _fm(v):
    v = np.asarray(v, np.float32)
    lead = v.shape[:-1]
    r = v.reshape(*lead, DC, P)
    r = np.moveaxis(r, -1, 0)
    return np.ascontiguousarray(r)


def _rope_partner():
    p = np.arange(P)
    within = p % 64
    half = (within % 32) // 16
    partner = np.where(half == 0, p + 16, p - 16)
    sign = np.where(half == 0, -1.0, 1.0).astype(np.float32)
    axis = within // 32
    f = within % 16
    return partner, sign, axis, f


def _sched():
    out = []
    modq = [(0, n) for n in range(6, 18)] + [(l, n) for l in range(1, DEPTH) for n in range(18)]
    for n in range(6):
        out.append(("mod", 4096, (0, n)))

    def ffn(l, i):
        for jp in range(FC // 2):
            out.append(("fin", 4096, (l, i, jp)))
            if modq:
                out.append(("mod", 4096, modq.pop(0)))
        for dc in range(DC):
            out.append(("fout", FC * P, (l, i, dc)))
            if modq:
                out.append(("mod", 4096, modq.pop(0)))

    for l in range(DEPTH):
        ffn(l, 0)
        kind, j = l % 3, l // 3
        if kind == 0:
            for cp in range(4):
                out.append(("ain", 4096, (j, cp)))
                out.append(("agate", 1024, (j, cp)))
            for half in range(2):
                out.append(("aout", 4096, ("a", j, half)))
        elif kind == 1:
            for half in range(2):
                out.append(("wv", 4096, (j, half)))
            for hd in range(8):
                out.append(("qk", 4096, (j, hd)))
            for half in range(2):
                out.append(("aout", 4096, ("b", j, half)))
        else:
            out.append(("pool", 2048, (j,)))
        ffn(l, 1)
    assert not modq
    return out


def _piece_array(inp, tag, key):
    if tag == "mod":
        l, n = key
        w = inp["w_mod"][l]
        return w[:, n * 512:(n + 1) * 512].reshape(DC, P, 512).transpose(1, 0, 2).reshape(P, -1)
    if tag == "fin":
        l, i, jp = key
        wi = inp["w_ffn_in"][l, i].reshape(DC, P, 2, FC, P)
        return wi[:, :, :, 2 * jp:2 * jp + 2, :].transpose(1, 3, 2, 0, 4).reshape(P, -1)
    if tag == "fout":
        l, i, dc = key
        wo = inp["w_ffn_out"][l, i].reshape(FC, P, DC, P)
        return wo[:, :, dc, :].transpose(1, 0, 2).reshape(P, -1)
    if tag == "ain":
        j, cp = key
        w_in = inp["a_w_in"][j].reshape(DC, P, 2, DC, P)
        return w_in[:, :, :, 2 * cp:2 * cp + 2, :].transpose(1, 3, 2, 0, 4).reshape(P, -1)
    if tag == "agate":
        j, cp = key
        gws = [inp["a_gate_w_a"][j, 0], inp["a_gate_w_x"][j, 0], inp["a_gate_w_a"][j, 1], inp["a_gate_w_x"][j, 1]]
        g = np.zeros((P, 2, 4, P), np.float32)
        for cc in range(2):
            c = 2 * cp + cc
            for qi in range(4):
                for hb in range(2):
                    g[hb * 64:(hb + 1) * 64, cc, qi, hb * 64:(hb + 1) * 64] = gws[qi][2 * c + hb]
        return g.reshape(P, -1)
    if tag == "aout":
        which, j, half = key
        w = inp["a_w_out"][j] if which == "a" else inp["b_w_o"][j]
        wo = w.reshape(DC, P, DC, P)
        return wo[:, :, 4 * half:4 * half + 4, :].transpose(1, 2, 0, 3).reshape(P, -1)
    if tag == "wv":
        j, half = key
        wv = inp["b_w_qkv"][j][:, 2048:3072]
        return wv[:, half * 512:(half + 1) * 512].reshape(DC, P, 512).transpose(1, 0, 2).reshape(P, -1)
    if tag == "qk":
        j, hd = key
        w = inp["b_w_qkv"][j]
        partner, _, _, _ = _rope_partner()
        q = w[:, 0:1024].reshape(DC, P, 8, P)[:, :, hd, :]
        k = w[:, 1024:2048].reshape(DC, P, 8, P)[:, :, hd, :]
        blk = np.stack([q, q[:, :, partner], k, k[:, :, partner]], 0)
        return blk.transpose(2, 0, 1, 3).reshape(P, -1)
    if tag == "pool":
        (j,) = key
        wp = inp["c_w_pool"][j].reshape(4, 2, P, 2, P)
        return wp.transpose(2, 0, 1, 3, 4).reshape(P, -1)
    raise KeyError(tag)


def _weight_plan(inp, n_used):
    return [(t, _piece_array(inp, t, key)) for t, e, key in _sched()[:n_used]]


def _piece_tags():
    return [(t, e) for t, e, _ in _sched()]


def _core_tokens(inp, c):
    xp, xs = inp["x_prompt"], inp["x_sample"]
    if c < 6:
        x = xp[5 * c:5 * c + 5].reshape(TOK, D)
        prompts = [5 * c + s for s in range(5)]
        sample = None
    else:
        b = c - 6
        x = np.concatenate([xs[b], xp[30 + b]], 0)
        prompts = [None] * 4 + [30 + b]
        sample = b
    return x, prompts, sample


def _host_prep(inp, n_used):
    inp = {k: np.asarray(v) for k, v in inp.items()}
    plan = _weight_plan(inp, n_used)
    tags = _piece_tags()[:n_used]
    assert len(plan) == len(tags)
    for (t0, a), (t1, e) in zip(plan, tags):
        assert t0 == t1 and a.shape == (P, e), (t0, t1, a.shape, e)
    wstream = np.ascontiguousarray(np.concatenate([a for _, a in plan], axis=1), dtype=np.float32)

    partner, sign, axis, f = _rope_partner()
    inv = (10000.0 ** (-np.arange(16, dtype=np.float32) / 16)).astype(np.float32)
    t = np.arange(1024)
    pos = np.stack([t // 64, t % 64], 0).astype(np.float32)
    ang = pos[axis][:, :] * inv[f][:, None]
    ang = ang.astype(np.float32)
    cos_s = np.cos(ang).astype(np.float32)
    sin_s = (np.sin(ang).astype(np.float32) * sign[:, None]).astype(np.float32)

    sm = {
        "norm_g": _fm(inp["norm_g"]).reshape(P, -1),
        "b_mod": _fm(inp["b_mod"].reshape(DEPTH, 9, D)).reshape(P, -1),
        "conv_w": _fm(inp["a_conv_w"]).reshape(P, -1),
        "conv_b": _fm(inp["a_conv_b"]).reshape(P, -1),
        "gb_a": _fm(inp["a_gate_b_a"]).reshape(P, -1),
        "gb_x": _fm(inp["a_gate_b_x"]).reshape(P, -1),
        "lam_a": _fm(inp["a_lambda"]).reshape(P, -1),
        "c_scale": _fm(inp["c_scale"]).reshape(P, -1),
        "fin_g": _fm(inp["final_norm_g"]).reshape(P, -1),
    }
    sub_g = np.asarray(inp["b_subln_g"][0], np.float32).reshape(P, 1)
    lqk = np.concatenate([np.asarray(inp["b_lam_q"][0], np.float32).reshape(1, 128),
                          np.asarray(inp["b_lam_k"][0], np.float32).reshape(1, 128)], axis=1)
    lqk = np.ascontiguousarray(np.broadcast_to(lqk, (P, 256)))
    small = np.concatenate([sm["norm_g"], sm["b_mod"], sm["conv_w"], sm["conv_b"], sm["gb_a"],
                            sm["gb_x"], sm["lam_a"], sm["c_scale"], sm["fin_g"], sub_g], axis=1)
    ident = np.eye(P, dtype=np.float32)

    in_maps = []
    for c in range(NCORES):
        x, prompts, sample = _core_tokens(inp, c)
        xT = np.ascontiguousarray(x.T.reshape(DC, P, TOK).transpose(1, 0, 2))
        if sample is None:
            cond = np.stack([inp["c_ctx"], inp["c_ctx"]], 0)
            h0 = np.zeros((P, 2 * 2 * DC), np.float32)
            sf = 0.0
            kc = np.zeros((P, 8, 256), np.float32)
            vc = np.zeros((P, 2, D), np.float32)
            cos_t = np.ones((P, TOK), np.float32)
            sin_t = np.zeros((P, TOK), np.float32)
        else:
            cond = np.stack([inp["c"][sample], inp["c_ctx"]], 0)
            h0 = _fm(inp["state_rglru"][sample]).reshape(P, -1)
            sf = 1.0
            ck = inp["cache_k_diff"][sample, 0]
            kc = np.ascontiguousarray(ck.transpose(2, 1, 0))
            cv = inp["cache_v_diff"][sample, 0].reshape(256, D)
            vc = np.ascontiguousarray(cv.reshape(2, P, D).transpose(1, 0, 2))
            cos_t = np.concatenate([cos_s, np.ones((P, 256), np.float32)], 1)
            sin_t = np.concatenate([sin_s, np.zeros((P, 256), np.float32)], 1)
        condT = np.ascontiguousarray(cond.T.reshape(DC, P, 2).transpose(1, 0, 2)).reshape(P, -1)
        mb = np.full((6, 5), NEG, np.float32)
        for qs in range(5):
            for ks in range(6):
                if sample is None:
                    ok = (ks == qs)
                else:
                    ok = (qs < 4 and (ks < 4 or ks == 5)) or (qs == 4 and ks == 4)
                if ok:
                    mb[ks, qs] = 0.0
        mbias = np.broadcast_to(mb.reshape(1, 30), (P, 30)).astype(np.float32)
        m01 = (mbias == 0.0).astype(np.float32)
        ic = np.zeros((4, TOK), np.float32)
        for g, win in enumerate(POOL_W):
            for s in range(NSEG):
                if sample is not None and s < 4:
                    T, tt = 1024, s * 256 + np.arange(256)
                else:
                    T, tt = 256, np.arange(256)
                lo = np.clip(tt - win // 2, 0, T)
                hi = np.clip(tt + win // 2, 0, T)
                ic[g, s * 256:(s + 1) * 256] = 1.0 / (hi - lo).astype(np.float32)
        icnt = np.ascontiguousarray(np.broadcast_to(ic[None], (P, 4, TOK))).astype(np.float32)
        flags = np.full((P, 1), sf, np.float32)
        percore = np.concatenate([condT, h0, flags, mbias, m01], axis=1).astype(np.float32)
        in_maps.append({
            "xT": xT, "wstream": wstream, "small": small, "ident": ident, "percore": percore, "lqk": lqk,
            "kcache": kc, "vcache": vc, "cos_t": cos_t, "sin_t": sin_t, "icnt": icnt,
        })
    return in_maps, len(tags), wstream.shape[1]


_SM = {}
_o = 0
for _n, _w in [("norm_g", 96), ("b_mod", 288), ("conv_w", 64), ("conv_b", 16), ("gb_a", 32), ("gb_x", 32),
               ("lam_a", 32), ("c_scale", 8), ("fin_g", 8), ("sub_g", 1)]:
    _SM[_n] = _o
    _o += _w
SMALL_W = _o
PC_COND, PC_H0, PC_SF, PC_MB, PC_M01 = 0, 16, 48, 49, 79
PERCORE_W = 109


def _build(n_pieces, wtotal):
    tags = _piece_tags()[:n_pieces]
    nc = bass.Bass("TRN2", target_bir_lowering=False)
    dt = lambda name, shape, kind="ExternalInput": nc.dram_tensor(name, list(shape), F32, kind=kind).ap()
    xT_d = dt("xT", [P, DC, TOK])
    ws_d = dt("wstream", [P, wtotal])
    small_d = dt("small", [P, SMALL_W])
    ident_d = dt("ident", [P, P])
    lqk_d = dt("lqk", [P, 256])
    pc_d = dt("percore", [P, PERCORE_W])
    kc_d = dt("kcache", [P, 8, 256])
    vc_d = dt("vcache", [P, 2, D])
    cos_d = dt("cos_t", [P, TOK])
    sin_d = dt("sin_t", [P, TOK])
    icnt_d = dt("icnt", [P, 4, TOK])
    yT_d = dt("yT", [P, DC, TOK], "ExternalOutput")
    kT_d = dt("kT", [P, DC, TOK], "ExternalOutput")
    v_d = dt("vout", [TOK, D], "ExternalOutput")
    st_d = dt("stout", [P, 2 * NSEG * 2 * DC], "ExternalOutput")
    if DEBUG:
        dbg_d = dt("dbg", [NDBG, P, DC, TOK], "ExternalOutput")

    es = ExitStack()
    es.enter_context(nc.allow_low_precision("bf16 matmul operands, fp32 accumulation"))
    _uid = [0]

    def _un(name):
        _uid[0] += 1
        return "t%d_%s" % (_uid[0], name)

    sb = lambda name, shape, dtype=F32: es.enter_context(nc.sbuf_tensor(_un(name), list(shape), dtype))

    def mk_eng(name, obj, step):
        return _Eng(name, obj, es.enter_context(nc.semaphore("s_" + name)), step)

    PE = mk_eng("pe", nc.tensor, 1)
    PE.nosame = True
    ACT = mk_eng("act", nc.scalar, 1)
    DVE = mk_eng("dve", nc.vector, 1)
    POOLQ = mk_eng("pool", nc.gpsimd, 1)
    SP = mk_eng("sp", nc.sync, 1)
    slot_eng = [mk_eng("ws%d" % i, None, 16) for i in range(NSLOT)]
    NCH = 12
    chan = [mk_eng("ch%d" % i, None, 16) for i in range(NCH)]
    chan_i = [0]
    gchan = [mk_eng("gch%d" % i, None, 16) for i in range(3)]

    def dma(out_ap, in_ap, reads=(), writes=(), q=None):
        q = q or SP
        ch = chan[chan_i[0] % NCH]
        chan_i[0] += 1
        return emit(ch, lambda: q.obj.dma_start(out=out_ap, in_=in_ap), reads, writes, via=q), ch

    def act(out, in_, func, reads, writes, bias=None, scale=1.0):
        kw = {}
        if bias is not None:
            kw["bias"] = bias
        return emit(ACT, lambda: nc.scalar.activation(out=out, in_=in_, func=func, scale=scale, **kw), reads, writes)

    def tt(out, a, b, op, reads, writes):
        return emit(DVE, lambda: nc.vector.tensor_tensor(out=out, in0=a, in1=b, op=op), reads, writes)

    def ptt(out, a, b, op, reads, writes):
        return emit(POOLQ, lambda: nc.gpsimd.tensor_tensor(out=out, in0=a, in1=b, op=op), reads, writes)

    def ts(out, a, s1, s2, op0, op1, reads, writes):
        return emit(DVE, lambda: nc.vector.tensor_scalar(out=out, in0=a, scalar1=s1, scalar2=s2, op0=op0, op1=op1),
                    reads, writes)

    def stt(out, a, s, b, op0, op1, reads, writes):
        return emit(DVE, lambda: nc.vector.scalar_tensor_tensor(out=out, in0=a, scalar=s, in1=b, op0=op0, op1=op1),
                    reads, writes)

    def mm(out, lhsT, rhs, start, stop, reads, writes, signal):
        return emit(PE, lambda: nc.tensor.matmul(out, lhsT, rhs, start=start, stop=stop), reads, writes, signal=signal)

    x_t = sb("x", [P, DC, TOK])
    xb = [[Buf("x%d_%d" % (c, i)) for i in range(3)] for c in range(DC)]
    h_t = sb("h", [P, DC, TOK], BF16)
    hb = [[Buf("h%d_%d" % (c, i)) for i in range(3)] for c in range(DC)]
    slots = [sb("wslot%d" % i, [P, SLOT_E], BF16) for i in range(NSLOT)]
    slot_b = [Buf("slot%d" % i) for i in range(NSLOT)]
    small_t = sb("small", [P, SMALL_W]); small_b = Buf("small")
    pc_t = sb("percore", [P, PERCORE_W]); pc_b = Buf("pc")
    ident_t = sb("ident", [P, P]); ident_b = Buf("ident")
    ones_bf = sb("ones_bf", [P, P], BF16)
    eps_t = sb("eps", [P, 1])
    one_t = sb("one", [P, 1])
    const_b = Buf("const")
    scond_t = sb("scond", [P, DC, 2], BF16); scond_b = Buf("scond")
    MOD = sb("mod", [P, DEPTH, 72, 2]); mod_b = [Buf("mod%d" % l) for l in range(DEPTH)]
    GS = sb("gs", [P, 3, DC, 2]); HG = sb("hg", [P, 3, DC, 2]); gs_b = Buf("gs")
    fin_t = sb("fin", [P, 2, NSEG, 2, DC]); fin_b = Buf("fin")
    rs_t = [sb("rs%d" % i, [P, 512]) for i in range(2)]; rs_b = [Buf("rs0"), Buf("rs1")]
    sq_t = [sb("sq0", [P, DC, 512], BF16)] * 2; sq_b = [Buf("sq0")] * 2
    tmp_t = [sb("tmp%d" % i, [P, 512]) for i in range(3)]; tmp_b = [Buf("tmp%d" % i) for i in range(3)]
    modT_t = sb("modT", [2, 512]); modT_b = Buf("modT")
    dmasem_b = Buf("dmasem")

    banks = [es.enter_context(nc.psum_tensor("bank%d" % i, [P, 512], F32)) for i in range(8)]
    bank_b = [Buf("bank%d" % i, excl=True) for i in range(8)]

    sm = lambda name, off=0, w=1: small_t[:, _SM[name] + off:_SM[name] + off + w]

    wst = {"issued": 0, "next": 0, "off": 0}
    offs = []
    o = 0
    for tg, e in tags:
        offs.append(o)
        o += e
    assert o == wtotal

    WS_LIMIT = [10 ** 9]

    def ws_issue_upto(k):
        while wst["issued"] <= min(k, n_pieces - 1) and wst["issued"] < WS_LIMIT[0]:
            i = wst["issued"]
            tg, e = tags[i]
            s = i % NSLOT
            emit(slot_eng[s], lambda: nc.gpsimd.dma_start(out=slots[s][:, 0:e], in_=ws_d[:, offs[i]:offs[i] + e]),
                 (), (slot_b[s],), via=POOLQ)
            wst["issued"] += 1

    def ws_next(tag):
        i = wst["next"]
        assert tags[i][0] == tag, (i, tags[i], tag)
        ws_issue_upto(i + LOOKAHEAD)
        wst["next"] += 1
        s = i % NSLOT
        return slots[s], slot_b[s]

    emit(DVE, lambda: nc.vector.memset(ones_bf[:], 1.0), (), (const_b,))
    emit(DVE, lambda: nc.vector.memset(eps_t[:], EPS), (), (const_b,))
    emit(DVE, lambda: nc.vector.memset(one_t[:], 1.0), (), (const_b,))
    emit(DVE, lambda: nc.vector.memset(fin_t[:], 0.0), (), (fin_b,))
    dma(small_t[:], small_d, (), (small_b,))
    dma(pc_t[:], pc_d, (), (pc_b,))
    dma(ident_t[:], ident_d, (), (ident_b,))
    ws_issue_upto(LOOKAHEAD)
    for c in range(DC):
        dma(x_t[:, c, :], xT_d[:, c, :], (), tuple(xb[c]))
    act(scond_t[:], pc_t[:, PC_COND:PC_COND + 16].rearrange("p (c n) -> p c n", n=2), AF.Silu, (pc_b,), (scond_b,))

    dbg_i = [0]

    def dbg_dump():
        if not DEBUG:
            return
        i = dbg_i[0]
        dbg_i[0] += 1
        if i >= NDBG:
            return
        for c in range(DC):
            dma(dbg_d[i, :, c, :], x_t[:, c, :], tuple(xb[c]), ())

    modq_dev = [(0, n) for n in range(6, 18)] + [(l_, n) for l_ in range(1, DEPTH) for n in range(18)]
    mod_done = set()

    mod_pend = []

    def mod_piece(l, n, defer=False):
        w, wb = ws_next("mod")
        wv = w[:, 0:4096].rearrange("p (k n) -> p k n", n=512)
        mod_flush()
        for kc in range(DC):
            mm(banks[7][0:2, :], scond_t[:, kc, :], wv[:, kc, :], kc == 0, kc == DC - 1,
               (scond_b, wb), (bank_b[7],), kc == DC - 1)
        act(modT_t[:], banks[7][0:2, :], AF.Copy, (bank_b[7],), (modT_b,))

        def part_b():
            for i4 in range(4):
                emit(PE, lambda: nc.tensor.transpose(banks[6][:, 2 * i4:2 * i4 + 2], modT_t[0:2, i4 * P:(i4 + 1) * P],
                                                     ident_t[0:2, 0:2]),
                     (modT_b, ident_b), (bank_b[6],), signal=(i4 == 3))
            fc0 = n * 4
            for cd in range(2):
                tt(MOD[:, l, fc0:fc0 + 4, cd], banks[6][:, 0:8].rearrange("p (f n) -> p f n", n=2)[:, :, cd],
                   small_t[:, _SM["b_mod"] + l * 72 + fc0:_SM["b_mod"] + l * 72 + fc0 + 4], ALU.add,
                   (bank_b[6], small_b), (mod_b[l],))
            mod_done.add((l, n))
        mod_pend.append(part_b)
        if not defer:
            mod_flush()

    def mod_flush():
        while mod_pend:
            mod_pend.pop(0)()

    def mod_step():
        if modq_dev:
            mod_piece(*modq_dev.pop(0), defer=True)
        else:
            mod_flush()

    def prep_mods(l, k):
        for n in range(6 * (k + 1)):
            assert (l, n) in mod_done, (l, k, n)
        for cd in range(2):
            stt(GS[:, k, :, cd], MOD[:, l, (3 * k + 1) * 8:(3 * k + 2) * 8, cd], 1.0,
                small_t[:, _SM["norm_g"] + (l * 3 + k) * 8:_SM["norm_g"] + (l * 3 + k) * 8 + 8],
                ALU.add, ALU.mult, (mod_b[l], small_b), (gs_b,))
            ts(HG[:, k, :, cd], MOD[:, l, (3 * k + 2) * 8:(3 * k + 3) * 8, cd], 0.5 if k != 1 else 1.0, None,
               ALU.mult, ALU.bypass, (mod_b[l],), (gs_b,))

    nrm_i = [0]

    def rms_stats(ti, nfeat_chunks=DC, src=None, src_bufs=None, inv_n=1.0 / D):
        t0, n = TILES[ti]
        i = nrm_i[0] % 2
        nrm_i[0] += 1
        for c in range(nfeat_chunks):
            s_ap = x_t[:, c, t0:t0 + n] if src is None else src[c]
            s_b = xb[c][ti] if src is None else src_bufs[c]
            act(sq_t[i][:, c, 0:n], s_ap, AF.Square, (s_b,), (sq_b[i],))
        for c in range(nfeat_chunks):
            mm(banks[6][:, 0:n], ones_bf[:], sq_t[i][:, c, 0:n], c == 0, c == nfeat_chunks - 1,
               (sq_b[i], const_b), (bank_b[6],), c == nfeat_chunks - 1)
        act(rs_t[i][:, 0:n], banks[6][:, 0:n], AF.Ln, (bank_b[6], const_b), (rs_b[i],), bias=eps_t[:], scale=inv_n)
        act(rs_t[i][:, 0:n], rs_t[i][:, 0:n], AF.Exp, (rs_b[i],), (rs_b[i],), scale=-0.5)
        return rs_t[i], rs_b[i]

    tmp_i = [0]

    def norm_mod(k, out_fn, only_ti=None):
        l = cur["l"]
        for ti, (t0, n) in enumerate(TILES):
            if only_ti is not None and ti != only_ti:
                continue
            cd = 0 if ti < 2 else 1
            rs, rsb = rms_stats(ti)
            for c in range(DC):
                j = tmp_i[0] % 3
                tmp_i[0] += 1
                tt(tmp_t[j][:, 0:n], x_t[:, c, t0:t0 + n], rs[:, 0:n], ALU.mult, (xb[c][ti], rsb), (tmp_b[j],))
                o_ap, o_b = out_fn(c, ti)
                act(o_ap, tmp_t[j][:, 0:n], AF.Identity, (tmp_b[j], gs_b, mod_b[l]), (o_b,),
                    bias=MOD[:, l, (3 * k) * 8 + c:(3 * k) * 8 + c + 1, cd], scale=GS[:, k, c:c + 1, cd])

    def h_out(c, ti):
        t0, n = TILES[ti]
        return h_t[:, c, t0:t0 + n], hb[c][ti]

    def resid_add(c, ti, ps_ap, ps_b, k, extra_reads=(), gate_ap=None):
        t0, n = TILES[ti]
        cd = 0 if ti < 2 else 1
        g = HG[:, k, c:c + 1, cd] if gate_ap is None else gate_ap(c, cd)
        stt(x_t[:, c, t0:t0 + n], ps_ap, g, x_t[:, c, t0:t0 + n], ALU.mult, ALU.add,
            (ps_b, gs_b, xb[c][ti]) + tuple(extra_reads), (xb[c][ti],))

    cur = {"l": 0}

    def ffn(k, ph):
        l = cur["l"]
        u_t = ph.enter_context(nc.sbuf_tensor(_un("u"), [P, FC, TOK], BF16))
        ub = [[Buf() for _ in range(3)] for _ in range(FC)]
        sa_t = [ph.enter_context(nc.sbuf_tensor(_un("sa"), [P, 512], F32)) for i in range(2)]
        sa_b = [Buf(), Buf()]
        it = 0
        for jp in range(FC // 2):
            w, wb = ws_next("fin")
            wv = w[:, 0:4096].rearrange("p (jj hf kc m) -> p jj hf kc m", jj=2, hf=2, kc=DC)
            for jj in range(2):
                j = 2 * jp + jj
                for ti, (t0, n) in enumerate(TILES):
                    if jp == 0 and jj == 0:
                        norm_mod(k, h_out, only_ti=ti)
                    pa, pb_ = it % 2, 2 + it % 2
                    it += 1
                    for hf, bk in ((0, pa), (1, pb_)):
                        for kc in range(DC):
                            mm(banks[bk][:, 0:n], wv[:, jj, hf, kc, :], h_t[:, kc, t0:t0 + n], kc == 0, kc == DC - 1,
                               (wb, hb[kc][ti]), (bank_b[bk],), kc == DC - 1)
                    si = it % 2
                    act(sa_t[si][:, 0:n], banks[pa][:, 0:n], AF.Silu, (bank_b[pa],), (sa_b[si],))
                    tt(u_t[:, j, t0:t0 + n], sa_t[si][:, 0:n], banks[pb_][:, 0:n], ALU.mult,
                       (sa_b[si], bank_b[pb_]), (ub[j][ti],))
            mod_step()
        for dc in range(DC):
            w, wb = ws_next("fout")
            wv = w[:, 0:FC * P].rearrange("p (j m) -> p j m", m=P)
            for ti, (t0, n) in enumerate(TILES):
                bk = 4 + it % 2
                it += 1
                for j in range(FC):
                    mm(banks[bk][:, 0:n], wv[:, j, :], u_t[:, j, t0:t0 + n], j == 0, j == FC - 1,
                       (wb, ub[j][ti]), (bank_b[bk],), j == FC - 1)
                resid_add(dc, ti, banks[bk][:, 0:n], bank_b[bk], k)
            mod_step()
        mod_flush()

    def barrier():
        engs = [PE, ACT, DVE, POOLQ, SP]
        allp = engs + slot_eng + chan + gchan
        for q in engs:
            for e in allp:
                if e is q or e.count == 0:
                    continue
                if e.count > q.seen.get(e, 0):
                    q.obj.wait_ge(e.sem, e.count)
                    q.seen[e] = e.count

    def out_proj(y_t, yb, k=1):
        it = 0
        for half in range(2):
            w, wb = ws_next("aout")
            wv = w[:, 0:4096].rearrange("p (dc kc m) -> p dc kc m", dc=4, kc=DC)
            for d4 in range(4):
                dc = half * 4 + d4
                for ti, (t0, n) in enumerate(TILES):
                    bk = 4 + it % 2
                    it += 1
                    for kc in range(DC):
                        mm(banks[bk][:, 0:n], wv[:, d4, kc, :], y_t[:, kc, t0:t0 + n], kc == 0, kc == DC - 1,
                           (wb, yb[kc]), (bank_b[bk],), kc == DC - 1)
                    resid_add(dc, ti, banks[bk][:, 0:n], bank_b[bk], k)

    def rglru(j, ph):
        l = cur["l"]
        sbp = lambda name, shape, dtype=F32: ph.enter_context(nc.sbuf_tensor(_un(name), list(shape), dtype))
        y_t = sbp("y", [P, DC, TOK], BF16); yb = [Buf() for _ in range(DC)]
        gl2_t = [sbp("gl%d" % i_, [P, TOK], BF16) for i_ in range(2)]; gl2_b = [Buf(), Buf()]
        xrp_t = sbp("xrp", [P, NSEG, XRW]); xrp_b = Buf()
        xc_t = sbp("xc", [P, TOK]); xc_b = Buf()
        xcb2_t = [sbp("xcb%d" % i_, [P, TOK], BF16) for i_ in range(2)]; xcb2_b = [Buf(), Buf()]
        r2_t = [sbp("r%d" % i_, [P, TOK]) for i_ in range(2)]; r2_b = [Buf(), Buf()]
        i2_t = [sbp("i%d" % i_, [P, TOK]) for i_ in range(2)]; i2_b = [Buf(), Buf()]
        a2_t = [sbp("a%d" % i_, [P, TOK]) for i_ in range(2)]; a2_b = [Buf(), Buf()]
        hs_t = [sbp("hs%d" % d, [P, TOK]) for d in range(2)]; hs_b = [Buf(), Buf()]
        c8_t = sbp("c8", [P, 2, DC]); c8_b = Buf()
        ini_t = sbp("ini", [P, 16]); ini_b = Buf()
        sf = pc_t[:, PC_SF:PC_SF + 1]
        norm_mod(1, h_out)
        lam_ap = small_t[:, _SM["lam_a"] + j * 16:_SM["lam_a"] + j * 16 + 16].rearrange("p (d c) -> p d c", d=2)
        yy_t = sbp("yy", [P, 2, DC]); pl_t = sbp("pl", [P, 2, DC]); mk_t = sbp("mk", [P, 2, DC])
        act(yy_t[:], lam_ap, AF.Exp, (small_b,), (c8_b,), scale=-1.0)
        act(c8_t[:], yy_t[:], AF.Ln, (c8_b, const_b), (c8_b,), bias=one_t[:])
        ts(pl_t[:], yy_t[:], -0.25, 1.0 / 3.0, ALU.mult, ALU.add, (c8_b,), (c8_b,))
        tt(pl_t[:], pl_t[:], yy_t[:], ALU.mult, (c8_b,), (c8_b,))
        ts(pl_t[:], pl_t[:], -0.5, None, ALU.add, ALU.bypass, (c8_b,), (c8_b,))
        tt(pl_t[:], pl_t[:], yy_t[:], ALU.mult, (c8_b,), (c8_b,))
        ts(pl_t[:], pl_t[:], 1.0, None, ALU.add, ALU.bypass, (c8_b,), (c8_b,))
        tt(pl_t[:], pl_t[:], yy_t[:], ALU.mult, (c8_b,), (c8_b,))
        ts(mk_t[:], yy_t[:], 0.1, None, ALU.is_lt, ALU.bypass, (c8_b,), (c8_b,))
        tt(pl_t[:], pl_t[:], c8_t[:], ALU.subtract, (c8_b,), (c8_b,))
        tt(pl_t[:], pl_t[:], mk_t[:], ALU.mult, (c8_b,), (c8_b,))
        tt(c8_t[:], c8_t[:], pl_t[:], ALU.add, (c8_b,), (c8_b,))
        ts(c8_t[:], c8_t[:], -8.0, None, ALU.mult, ALU.bypass, (c8_b,), (c8_b,))
        emit(DVE, lambda: nc.vector.memset(xrp_t[:], 0.0), (), (xrp_b,))
        itc = [0]
        wts = {}
        xc3 = xc_t[:, :].rearrange("p (s t) -> p s t", t=SEG)

        def F1(c):
            cp, cc = c // 2, c % 2
            if cc == 0:
                w, wb = ws_next("ain")
                wg, wgb = ws_next("agate")
                wts[cp] = (w[:, 0:4096].rearrange("p (cc g kc m) -> p cc g kc m", cc=2, g=2, kc=DC), wb,
                           wg[:, 0:1024].rearrange("p (cc q m) -> p cc q m", cc=2, q=4), wgb)
            wv, wb, wgv, wgb = wts[cp]
            gl_t, gl_b = gl2_t[c % 2], gl2_b[c % 2]
            xcb_t, xcb_b = xcb2_t[c % 2], xcb2_b[c % 2]
            for ti, (t0, n) in enumerate(TILES):
                for g in range(2):
                    bk = (0 if g == 0 else 2) + itc[0] % 2
                    for kc in range(DC):
                        mm(banks[bk][:, 0:n], wv[:, cc, g, kc, :], h_t[:, kc, t0:t0 + n], kc == 0, kc == DC - 1,
                           (wb, hb[kc][ti]), (bank_b[bk],), kc == DC - 1)
                    if g == 0:
                        act(gl_t[:, t0:t0 + n], banks[bk][:, 0:n], AF.Gelu, (bank_b[bk],), (gl_b,))
                    else:
                        ns = n // SEG
                        emit(DVE, lambda: nc.vector.tensor_copy(
                            out=xrp_t[:, 2 * ti:2 * ti + ns, 2:2 + SEG],
                            in_=banks[bk][:, 0:n].rearrange("p (s t) -> p s t", t=SEG)), (bank_b[bk],), (xrp_b,))
                itc[0] += 1
            ts(xrp_t[:, 1:4, 0:2], xrp_t[:, 0:3, SEG:SEG + 2], sf, None, ALU.mult, ALU.bypass, (xrp_b, pc_b), (xrp_b,))
            ts(xrp_t[:, 0:3, SEG + 2:SEG + 3], xrp_t[:, 1:4, 2:3], sf, None, ALU.mult, ALU.bypass, (xrp_b, pc_b), (xrp_b,))
            cw = lambda kk: small_t[:, _SM["conv_w"] + (j * 4 + kk) * 8 + c:_SM["conv_w"] + (j * 4 + kk) * 8 + c + 1]
            cb = small_t[:, _SM["conv_b"] + j * 8 + c:_SM["conv_b"] + j * 8 + c + 1]
            ts(xc3, xrp_t[:, :, 0:SEG], cw(0), cb, ALU.mult, ALU.add, (xrp_b, small_b), (xc_b,))
            for kk in range(1, 4):
                stt(xc3, xrp_t[:, :, kk:kk + SEG], cw(kk), xc3, ALU.mult, ALU.add, (xrp_b, small_b, xc_b), (xc_b,))
            act(xcb_t[:], xc_t[:], AF.Copy, (xc_b,), (xcb_b,))

        def F2(c, d):
            cp, cc = c // 2, c % 2
            wv, wb, wgv, wgb = wts[cp]
            xcb_t, xcb_b = xcb2_t[c % 2], xcb2_b[c % 2]
            r_t, r_b, i_t, i_b, a_t, a_b = r2_t[d], r2_b[d], i2_t[d], i2_b[d], a2_t[d], a2_b[d]
            gba = small_t[:, _SM["gb_a"] + (j * 2 + d) * 8 + c:_SM["gb_a"] + (j * 2 + d) * 8 + c + 1]
            gbx = small_t[:, _SM["gb_x"] + (j * 2 + d) * 8 + c:_SM["gb_x"] + (j * 2 + d) * 8 + c + 1]
            for ti, (t0, n) in enumerate(TILES):
                br, bi = 4 + itc[0] % 2, 6 + itc[0] % 2
                itc[0] += 1
                mm(banks[br][:, 0:n], wgv[:, cc, 2 * d, :], xcb_t[:, t0:t0 + n], True, True,
                   (wgb, xcb_b), (bank_b[br],), True)
                mm(banks[bi][:, 0:n], wgv[:, cc, 2 * d + 1, :], xcb_t[:, t0:t0 + n], True, True,
                   (wgb, xcb_b), (bank_b[bi],), True)
                act(r_t[:, t0:t0 + n], banks[br][:, 0:n], AF.Sigmoid, (bank_b[br], small_b), (r_b,), bias=gba)
                act(i_t[:, t0:t0 + n], banks[bi][:, 0:n], AF.Sigmoid, (bank_b[bi], small_b), (i_b,), bias=gbx)

        def F2b(c):
            for d in range(2):
                r_t, r_b, a_t, a_b = r2_t[d], r2_b[d], a2_t[d], a2_b[d]
                act(a_t[:], r_t[:], AF.Exp, (r_b, c8_b), (a_b,), scale=c8_t[:, d, c:c + 1])
                act(r_t[:], a_t[:], AF.Square, (a_b,), (r_b,))

        def F3(c):
            xcb_t, xcb_b = xcb2_t[c % 2], xcb2_b[c % 2]
            for d in range(2):
                r_t, r_b = r2_t[d], r2_b[d]
                act(r_t[:], r_t[:], AF.Sqrt, (r_b, const_b), (r_b,), bias=one_t[:], scale=-1.0)
            for d in range(2):
                r_t, r_b, i_t, i_b = r2_t[d], r2_b[d], i2_t[d], i2_b[d]
                tt(i_t[:], i_t[:], xcb_t[:], ALU.mult, (i_b, xcb_b), (i_b,))
                tt(i_t[:], i_t[:], r_t[:], ALU.mult, (i_b, r_b), (i_b,))

        def T(c):
            gl_t, gl_b = gl2_t[c % 2], gl2_b[c % 2]
            for d in range(2):
                i_t, i_b, a_t, a_b = i2_t[d], i2_b[d], a2_t[d], a2_b[d]
                hs = hs_t[d]
                h0 = pc_t[:, PC_H0 + (j * 2 + d) * 8 + c:PC_H0 + (j * 2 + d) * 8 + c + 1]
                order = [0, 1, 2, 3, 4] if d == 0 else [3, 2, 1, 0, 4]
                for oi, s in enumerate(order):
                    lo, hi = s * SEG, (s + 1) * SEG
                    if s == 4:
                        init = 0.0
                        rd = ()
                    elif oi == 0:
                        init = h0
                        rd = (pc_b,)
                    else:
                        prev = order[oi - 1]
                        pcol = prev * SEG + (SEG - 1 if d == 0 else 0)
                        ic_ = (itc[0] + oi) % 16
                        ts(ini_t[:, ic_:ic_ + 1], hs[:, pcol:pcol + 1], sf, None, ALU.mult, ALU.bypass,
                           (hs_b[d], pc_b), (ini_b,))
                        init = ini_t[:, ic_:ic_ + 1]
                        rd = (ini_b,)
                    if d == 0:
                        a_ap, b_ap, o_ap = a_t[:, lo:hi], i_t[:, lo:hi], hs[:, lo:hi]
                    else:
                        a_ap, b_ap, o_ap = a_t[:, lo:hi][:, ::-1], i_t[:, lo:hi][:, ::-1], hs[:, lo:hi][:, ::-1]
                    emit(DVE, lambda: nc.vector.tensor_tensor_scan(o_ap, a_ap, b_ap, init, ALU.mult, ALU.add),
                         (a_b, i_b) + rd, (hs_b[d],))
                itc[0] += 5
                col0 = SEG - 1 if d == 0 else 0
                emit(DVE, lambda: nc.vector.tensor_copy(
                    out=fin_t[:, j, :, d, c],
                    in_=hs[:, :].rearrange("p (s t) -> p s t", t=SEG)[:, :, col0]), (hs_b[d],), (fin_b,))
            tt(hs_t[0][:], hs_t[0][:], hs_t[1][:], ALU.add, (hs_b[0], hs_b[1]), (hs_b[0],))
            tt(y_t[:, c, :], hs_t[0][:], gl_t[:], ALU.mult, (hs_b[0], gl_b), (yb[c],))

        F1(0)
        F2(0, 0)
        F2(0, 1)
        F2b(0)
        for c in range(DC):
            if c + 1 < DC:
                F1(c + 1)
            F3(c)
            T(c)
            if c + 1 < DC:
                F2(c + 1, 0)
                F2(c + 1, 1)
                F2b(c + 1)
        out_proj(y_t, yb)

    def attention(j, ph):
        l = cur["l"]
        sbp = lambda name, shape, dtype=F32: ph.enter_context(nc.sbuf_tensor(_un(name), list(shape), dtype))
        NKB = 12
        V_t = sbp("V", [P, NKB, D], BF16); V_b = [Buf() for _ in range(NKB)]
        o_t = sbp("oall", [P, DC, TOK], BF16); o_b = [Buf() for _ in range(DC)]
        q_t = [sbp("q%d" % i, [P, TOK], BF16) for i in range(2)]; q_b = [Buf(), Buf()]
        k_t = [sbp("k%d" % i, [P, TOK + 256], BF16) for i in range(2)]; k_b = [Buf(), Buf()]
        cos_t = sbp("cos", [P, TOK], BF16); sin_t = sbp("sin", [P, TOK], BF16); rope_b = Buf()
        oc_t = [sbp("oc%d" % i_, [P, 512]) for i_ in range(2)]; oc_b = [Buf(), Buf()]
        r1_t, r1_b = tmp_t[0], tmp_b[0]
        r2_t, r2_b = tmp_t[1], tmp_b[1]
        ko_t = [sbp("ko0", [P, TOK])] * 2; ko_b = [Buf()] * 2
        vo_t = [sbp("vo%d" % i, [P, 512]) for i in range(2)]; vo_b = [Buf(), Buf()]
        pT_t = [sbp("pT%d" % i, [P, 512], BF16) for i in range(4)]; pT_b = [Buf() for _ in range(4)]
        of_t, of_b = tmp_t[2], tmp_b[2]
        lam_t = sbp("lamt", [P, 8]); lam_b = Buf()
        norm_mod(1, h_out)
        emit(gchan[1], lambda: nc.gpsimd.dma_start(out=cos_t[:], in_=cos_d), (), (rope_b,), via=POOLQ)
        emit(gchan[2], lambda: nc.gpsimd.dma_start(out=sin_t[:], in_=sin_d), (), (rope_b,), via=POOLQ)
        emit(gchan[0], lambda: nc.gpsimd.dma_start(out=V_t[:, 10:12, :], in_=vc_d), (), (V_b[10], V_b[11]), via=POOLQ)
        lqk_t = sbp("lqk", [P, 256]); lqk_b = Buf()
        dma(lqk_t[:], lqk_d, (), (lqk_b,))
        lq = lqk_t[:, 0:128]
        lk = lqk_t[:, 128:256]
        lp_t = lqk_t[:, 0:128]
        tt(lp_t, lq, lk, ALU.mult, (lqk_b,), (lqk_b, lam_b))
        emit(DVE, lambda: nc.vector.reduce_sum(out=lam_t[:, 0:2], in_=lp_t.rearrange("p (m d) -> p m d", m=2),
                                               axis=mybir.AxisListType.X), (lqk_b, lam_b), (lam_b,))
        act(lam_t[:, 2:4], lam_t[:, 0:2], AF.Exp, (lam_b,), (lam_b,))
        tt(lam_t[:, 4:5], lam_t[:, 3:4], lam_t[:, 2:3], ALU.subtract, (lam_b,), (lam_b,))
        ts(lam_t[:, 4:5], lam_t[:, 4:5], -LAM_INIT, None, ALU.add, ALU.bypass, (lam_b,), (lam_b,))
        ts(lam_t[:, 5:6], small_t[:, _SM["sub_g"]:_SM["sub_g"] + 1], 1.0 - LAM_INIT, None, ALU.mult, ALU.bypass,
           (small_b,), (lam_b,))
        neglam = lam_t[:, 4:5]
        gsub = lam_t[:, 5:6]
        it = 0
        if ATT_LEVEL < 2:
            return
        for half in range(2):
            w, wb = ws_next("wv")
            wv = w[:, 0:4096].rearrange("p (k n) -> p k n", n=512)
            for tb in range(TOK // P):
                ti = min(tb // 4, 2)
                bk = it % 2
                for kc in range(DC):
                    mm(banks[bk][:, :], h_t[:, kc, tb * P:(tb + 1) * P], wv[:, kc, :], kc == 0, kc == DC - 1,
                       (wb, hb[kc][ti]), (bank_b[bk],), kc == DC - 1)
                act(V_t[:, tb, half * 512:(half + 1) * 512], banks[bk][:, :], AF.Copy, (bank_b[bk],), (V_b[tb],))
                vi = it % 2
                emit(DVE, lambda: nc.vector.tensor_copy(out=vo_t[vi][:], in_=banks[bk][:, :]), (bank_b[bk],), (vo_b[vi],))
                if ATT_LEVEL != 21:
                    dma(v_d[tb * P:(tb + 1) * P, half * 512:(half + 1) * 512], vo_t[vi][:], (vo_b[vi],), ())
                it += 1
        pend_fin = []
        fin_bank = [0]
        if ATT_LEVEL < 3 or ATT_LEVEL == 21:
            return
        for hd in range(8):
            w, wb = ws_next("qk")
            wv = w[:, 0:4096].rearrange("p (f kc m) -> p f kc m", f=4, kc=DC)
            hi = hd % 2
            emit(gchan[1 + hi], lambda: nc.gpsimd.dma_start(out=k_t[hi][:, TOK:TOK + 256], in_=kc_d[:, hd, :]),
                 (), (k_b[hi],), via=POOLQ)
            for which in range(2):
                dst = q_t[hi] if which == 0 else k_t[hi]
                dst_b = q_b[hi] if which == 0 else k_b[hi]
                for ti, (t0, n) in enumerate(TILES):
                    b0, b1 = it % 2, 2 + it % 2
                    it += 1
                    for f, bk in ((0, b0), (1, b1)):
                        for kc in range(DC):
                            mm(banks[bk][:, 0:n], wv[:, 2 * which + f, kc, :], h_t[:, kc, t0:t0 + n], kc == 0,
                               kc == DC - 1, (wb, hb[kc][ti]), (bank_b[bk],), kc == DC - 1)
                    if which == 1:
                        act(ko_t[hi][:, t0:t0 + n], banks[b0][:, 0:n], AF.Copy, (bank_b[b0],), (ko_b[hi],))
                    tt(r1_t[:, 0:n], banks[b0][:, 0:n], cos_t[:, t0:t0 + n], ALU.mult, (bank_b[b0], rope_b), (r1_b,))
                    tt(r2_t[:, 0:n], banks[b1][:, 0:n], sin_t[:, t0:t0 + n], ALU.mult, (bank_b[b1], rope_b), (r2_b,))
                    tt(dst[:, t0:t0 + n], r1_t[:, 0:n], r2_t[:, 0:n], ALU.add, (r1_b, r2_b), (dst_b,))
            dma(kT_d[:, hd, :], ko_t[hi][:], (ko_b[hi],), ())
            for ti, (t0, n) in enumerate(TILES if ATT_LEVEL >= 4 else []):
                nq = n // SEG
                inflight = []
                for kbs in range(NKB + 1):
                    if kbs in (2, 5, 8) and pend_fin:
                        fin_bank[0] = (it + 1) % 4
                        pend_fin.pop(0)()
                    if kbs < NKB:
                        kb = kbs
                        kseg = kb // 2 if kb < 10 else 5
                        sbs = [it % 4, (it + 1) % 4]
                        it += 2
                        for m in range(2):
                            mm(banks[sbs[m]][:, 0:n], k_t[hi][m * 64:(m + 1) * 64, kb * P:(kb + 1) * P],
                               q_t[hi][m * 64:(m + 1) * 64, t0:t0 + n], True, True, (k_b[hi], q_b[hi]), (bank_b[sbs[m]],), True)
                        for m in range(2):
                            act(pT_t[sbs[m]][:, 0:n], banks[sbs[m]][:, 0:n], AF.Exp, (bank_b[sbs[m]],), (pT_b[sbs[m]],),
                                scale=0.125)
                            for qs in range(nq):
                                qseg = 2 * ti + qs
                                mcol = PC_M01 + kseg * 5 + qseg
                                ts(pT_t[sbs[m]][:, qs * SEG:(qs + 1) * SEG], pT_t[sbs[m]][:, qs * SEG:(qs + 1) * SEG],
                                   pc_t[:, mcol:mcol + 1], None, ALU.mult, ALU.bypass, (pT_b[sbs[m]], pc_b), (pT_b[sbs[m]],))
                        inflight.append(sbs)
                    if kbs >= 1:
                        kb = kbs - 1
                        for m in range(2):
                            pi = inflight[kb][m]
                            mm(banks[4 + m][:, 0:n], V_t[:, kb, hd * P:(hd + 1) * P], pT_t[pi][:, 0:n], kb == 0, kb == NKB - 1,
                               (V_b[kb], pT_b[pi]), (bank_b[4 + m],), kb == NKB - 1)
                            mm(banks[6 + m][:, 0:n], ones_bf[:], pT_t[pi][:, 0:n], kb == 0, kb == NKB - 1,
                               (const_b, pT_b[pi]), (bank_b[6 + m],), kb == NKB - 1)
                if ATT_LEVEL < 5:
                    continue
                A_t, A_b, B_t, B_b = rs_t[0], rs_b[0], rs_t[1], rs_b[1]
                C_t, C_b, D_t, D_b = oc_t[0], oc_b[0], oc_t[1], oc_b[1]
                act(A_t[:, 0:n], banks[6][:, 0:n], AF.Ln, (bank_b[6],), (A_b,))
                emit(DVE, lambda: nc.vector.tensor_copy(out=C_t[:, 0:n], in_=banks[4][:, 0:n]), (bank_b[4],), (C_b,))
                act(B_t[:, 0:n], banks[7][:, 0:n], AF.Ln, (bank_b[7],), (B_b,))
                emit(DVE, lambda: nc.vector.tensor_copy(out=D_t[:, 0:n], in_=banks[5][:, 0:n]), (bank_b[5],), (D_b,))

                def fin_a(n=n):
                    act(A_t[:, 0:n], A_t[:, 0:n], AF.Exp, (A_b,), (A_b,), scale=-1.0)
                    act(B_t[:, 0:n], B_t[:, 0:n], AF.Exp, (B_b,), (B_b,), scale=-1.0)
                    tt(C_t[:, 0:n], C_t[:, 0:n], A_t[:, 0:n], ALU.mult, (C_b, A_b), (C_b,))
                    tt(D_t[:, 0:n], D_t[:, 0:n], B_t[:, 0:n], ALU.mult, (D_b, B_b), (D_b,))
                    stt(C_t[:, 0:n], D_t[:, 0:n], neglam, C_t[:, 0:n], ALU.mult, ALU.add, (C_b, D_b, lam_b), (C_b,))

                def fin_b(n=n):
                    nb_ = fin_bank[0]
                    act(sq_t[0][:, 0, 0:n], C_t[:, 0:n], AF.Square, (C_b,), (sq_b[0],))
                    mm(banks[nb_][:, 0:n], ones_bf[:], sq_t[0][:, 0, 0:n], True, True, (sq_b[0], const_b), (bank_b[nb_],), True)
                    act(A_t[:, 0:n], banks[nb_][:, 0:n], AF.Ln, (bank_b[nb_], const_b), (A_b,), bias=eps_t[:], scale=1.0 / P)

                def fin_c(hd=hd, t0=t0, n=n):
                    act(A_t[:, 0:n], A_t[:, 0:n], AF.Exp, (A_b,), (A_b,), scale=-0.5)
                    stt(o_t[:, hd, t0:t0 + n], C_t[:, 0:n], gsub, A_t[:, 0:n], ALU.mult, ALU.mult,
                        (C_b, lam_b, A_b), (o_b[hd],))
                pend_fin.extend([fin_a, fin_b, fin_c])
        while pend_fin:
            fin_bank[0] = it % 4
            pend_fin.pop(0)()
        if ATT_LEVEL >= 5:
            out_proj(o_t, o_b)

    def pooling(j, ph):
        l = cur["l"]
        sbp = lambda name, shape, dtype=F32: ph.enter_context(nc.sbuf_tensor(_un(name), list(shape), dtype))
        hp_t = sbp("hp", [P, 2, NSEG, HPW]); hp_b = Buf()
        L_t = [sbp("L%d" % i, [P, 2, NSEG, HPW]) for i in range(2)]; L_b = [Buf(), Buf()]
        ic_t = [sbp("ic%d" % i, [P, TOK]) for i in range(2)]; ic_b = [Buf(), Buf()]
        d_t = sbp("dd", [P, 2, TOK], BF16); d_b = Buf()
        df_t = sbp("df", [P, 2, NSEG, SEG]); df_b = Buf()
        gsc_t = sbp("gsc", [P, DC, 2]); gsc_b = Buf()
        rsa_t = sbp("rsa", [P, TOK]); rsa_b = Buf()
        sf = pc_t[:, PC_SF:PC_SF + 1]
        for cd in range(2):
            tt(gsc_t[:, :, cd], HG[:, 1, :, cd], small_t[:, _SM["c_scale"]:_SM["c_scale"] + 8], ALU.mult,
               (gs_b, small_b), (gsc_b,))
        for ti, (t0, n) in enumerate(TILES):
            rs, rsb = rms_stats(ti)
            emit(DVE, lambda: nc.vector.tensor_copy(out=rsa_t[:, t0:t0 + n], in_=rs[:, 0:n]), (rsb,), (rsa_b,))
        w, wb = ws_next("pool")
        wv = w[:, 0:2048].rearrange("p (g ki mo m) -> p g ki mo m", g=4, ki=2, mo=2)
        emit(DVE, lambda: nc.vector.memset(hp_t[:], 0.0), (), (hp_b,))
        it = 0
        for g in range(4):
            dma(ic_t[g % 2][:], icnt_d[:, g, :], (), (ic_b[g % 2],))
            for ci in range(2):
                c = 2 * g + ci
                for ti, (t0, n) in enumerate(TILES):
                    cd = 0 if ti < 2 else 1
                    ns = n // SEG
                    jt = tmp_i[0] % 3
                    tmp_i[0] += 1
                    tt(tmp_t[jt][:, 0:n], x_t[:, c, t0:t0 + n], rsa_t[:, t0:t0 + n], ALU.mult, (xb[c][ti], rsa_b), (tmp_b[jt],))
                    act(hp_t[:, ci, 2 * ti:2 * ti + ns, 8:8 + SEG], tmp_t[jt][:, 0:n].rearrange("p (s t) -> p s t", t=SEG),
                        AF.Identity, (tmp_b[jt], gs_b, mod_b[l]), (hp_b,),
                        bias=MOD[:, l, 3 * 8 + c:3 * 8 + c + 1, cd], scale=GS[:, 1, c:c + 1, cd])
            for ci in range(2):
                ts(hp_t[:, ci, 1:4, 0:8], hp_t[:, ci, 0:3, SEG:SEG + 8], sf, None, ALU.mult, ALU.bypass, (hp_b, pc_b), (hp_b,))
                ts(hp_t[:, ci, 0:3, SEG + 8:SEG + 16], hp_t[:, ci, 1:4, 8:16], sf, None, ALU.mult, ALU.bypass,
                   (hp_b, pc_b), (hp_b,))
            src, src_b = hp_t, hp_b
            lo, hi = 0, HPW
            for lev in range(g + 1):
                dst, dst_b = L_t[lev % 2], L_b[lev % 2]
                sh = 1 if lev == 0 else 2 ** (lev - 1)
                if lev == 0:
                    nlo, nhi = lo + 1, hi
                    for ci in range(2):
                        tt(dst[:, ci, :, nlo:nhi], src[:, ci, :, nlo - 1:nhi - 1], src[:, ci, :, nlo:nhi], ALU.add,
                           (src_b,), (dst_b,))
                else:
                    nlo, nhi = lo + sh, hi - sh
                    for ci in range(2):
                        tt(dst[:, ci, :, nlo:nhi], src[:, ci, :, nlo - sh:nhi - sh], src[:, ci, :, nlo + sh:nhi + sh],
                           ALU.add, (src_b,), (dst_b,))
                lo, hi = nlo, nhi
                src, src_b = dst, dst_b
            assert lo <= 8 and hi >= 8 + SEG
            ic3 = ic_t[g % 2][:, :].rearrange("p (s t) -> p s t", t=SEG)
            for ci in range(2):
                tt(df_t[:, ci], src[:, ci, :, 8:8 + SEG], ic3, ALU.mult, (src_b, ic_b[g % 2]), (df_b,))
                tt(d_t[:, ci, :].rearrange("p (s t) -> p s t", t=SEG), df_t[:, ci], hp_t[:, ci, :, 8:8 + SEG], ALU.subtract,
                   (df_b, hp_b), (d_b,))
            for mo in range(2):
                c = 2 * g + mo
                for ti, (t0, n) in enumerate(TILES):
                    bk = 4 + it % 2
                    it += 1
                    for ki in range(2):
                        mm(banks[bk][:, 0:n], wv[:, g, ki, mo, :], d_t[:, ki, t0:t0 + n], ki == 0, ki == 1,
                           (wb, d_b), (bank_b[bk],), ki == 1)
                    resid_add(c, ti, banks[bk][:, 0:n], bank_b[bk], 1, extra_reads=(gsc_b,),
                              gate_ap=lambda c_, cd_: gsc_t[:, c_, cd_:cd_ + 1])

    for n_ in range(6):
        mod_piece(0, n_)
    nst = 0
    for l in range(DEPTH):
        cur["l"] = l
        if nst >= STAGES:
            break
        prep_mods(l, 0)
        with ExitStack() as ph:
            ffn(0, ph)
            barrier()
        dbg_dump()
        nst += 1
        if nst >= STAGES:
            break
        kind, j = l % 3, l // 3
        prep_mods(l, 1)
        with ExitStack() as ph:
            if kind == 0:
                rglru(j, ph)
            elif kind == 1:
                attention(j, ph)
            else:
                pooling(j, ph)
            barrier()
        dbg_dump()
        nst += 1
        if nst >= STAGES:
            break
        prep_mods(l, 2)
        with ExitStack() as ph:
            ffn(2, ph)
            barrier()
        dbg_dump()
        nst += 1
    assert STAGES < 12 or wst["next"] == n_pieces, (wst["next"], n_pieces)

    with ExitStack() as ph:
        yo_t = [ph.enter_context(nc.sbuf_tensor(_un("yo"), [P, 512], F32)) for i in range(3)]
        yo_b = [Buf() for _ in range(3)]
        oi = 0
        for ti, (t0, n) in enumerate(TILES):
            rs, rsb = rms_stats(ti)
            for c in range(DC):
                k3 = oi % 3
                oi += 1
                stt(yo_t[k3][:, 0:n], x_t[:, c, t0:t0 + n], small_t[:, _SM["fin_g"] + c:_SM["fin_g"] + c + 1], rs[:, 0:n],
                    ALU.mult, ALU.mult, (xb[c][ti], small_b, rsb), (yo_b[k3],))
                dma(yT_d[:, c, t0:t0 + n], yo_t[k3][:, 0:n], (yo_b[k3],), ())
        dma(st_d, fin_t[:].rearrange("p j s d c -> p (j s d c)"), (fin_b,), ())
        for e in chan:
            if e.count > SP.seen.get(e, 0):
                nc.sync.wait_ge(e.sem, e.count)
                SP.seen[e] = e.count
        barrier()
    es.close()
    return nc, wst["issued"]


_CACHE = {}


def kernel(**inputs):
    tags = _piece_tags()
    key = (STAGES, DEBUG)
    if key not in _CACHE:
        n_used = len(tags)
        if STAGES < 12:
            _, n_used = _build(len(tags), sum(e for _, e in tags))
        wtotal = sum(e for _, e in tags[:n_used])
        _CACHE[key] = (_build(n_used, wtotal)[0], n_used)
    nc, n_used = _CACHE[key]
    import time as _t
    _t0 = _t.time()
    in_maps, _, _ = _host_prep(inputs, n_used)
    _t1 = _t.time()
    res = run_bass_kernel_spmd(nc, in_maps, core_ids=list(range(NCORES)))
    if DEBUG:
        print('prep %.1fs run %.1fs' % (_t1 - _t0, _t.time() - _t1))
    outs = res.results
    B, S = 32, 256
    y_prompt = np.zeros((B, S, D), np.float32)
    y_sample = np.zeros((2, 1024, D), np.float32)
    new_state = np.zeros((B, 2, 2, D), np.float32)
    new_k = np.zeros((B, 1, S, 8, 128), np.float32)
    new_v = np.zeros((B, 1, S, 8, 128), np.float32)
    for c in range(NCORES):
        r = outs[c]
        yT = np.asarray(r["yT"]).transpose(1, 0, 2).reshape(D, TOK)
        kT = np.asarray(r["kT"]).transpose(1, 0, 2).reshape(D, TOK)
        vv = np.asarray(r["vout"])
        st = np.asarray(r["stout"]).reshape(P, 2, NSEG, 2, DC)
        y = yT.T
        kk = kT.T
        if c < 6:
            segs = [(s, 5 * c + s) for s in range(5)]
        else:
            segs = [(4, 30 + (c - 6))]
            y_sample[c - 6] = y[0:1024]
        for s, b in segs:
            y_prompt[b] = y[s * S:(s + 1) * S]
            new_k[b, 0] = kk[s * S:(s + 1) * S].reshape(S, 8, 128)
            new_v[b, 0] = vv[s * S:(s + 1) * S].reshape(S, 8, 128)
            new_state[b] = st[:, :, s, :, :].transpose(1, 2, 3, 0).reshape(2, 2, D)
    if DEBUG:
        kernel.dbg = [np.asarray(outs[c]["dbg"]) for c in range(NCORES)]
    return (y_prompt, y_sample, new_state, new_k, new_v)
```
